# Optimizing a Trainium2 kernel written in Bass

```python
import jax, jax.numpy as jnp
from jax import lax
import numpy as np

D_MODEL = 2048
BATCH = 2
SEQ = 4096
DEPTH = 2
DEC_BATCH = 8
DEC_SEQ = 32
PAST_LEN = 4096

CHUNK = 64
Q_BLOCK = 128
D_CONV = D_MODEL // 2
CONV_K = 31
D_POOL = D_MODEL // 2
POOL_WINDOWS = (2, 4, 8, 16)
POOL_MAX = 16
N_POOL_GROUPS = 4
POOL_GROUP = D_POOL // N_POOL_GROUPS
POOL_OUT = D_MODEL // N_POOL_GROUPS
N_HEADS = D_MODEL // 128
NOPE_DIM = 128
ROPE_DIM = 64
V_DIM = 128
Q_RANK = D_MODEL // 4
KV_RANK = D_MODEL // 4
ROPE_THETA = 10000.0
ATTN_SCALE = (NOPE_DIM + ROPE_DIM) ** -0.5
D_FF = 256 * ((8 * D_MODEL // 3 + 255) // 256)
FFN_K = 3
N_BRANCH = 3
EPS = 1e-6
NEG = -1e30
OFF_A = 0
OFF_B = OFF_A + 2 * D_CONV
OFF_Q = OFF_B + D_POOL
OFF_KV = OFF_Q + Q_RANK
OFF_R = OFF_KV + KV_RANK
OFF_G = OFF_R + ROPE_DIM
N_IN = OFF_G + N_BRANCH * D_MODEL

kernel_name = 'hybrid_streaming_encoder_step'


def rms_norm(x, g):
    x32 = x.astype(jnp.float32)
    y = x32 * lax.rsqrt(jnp.mean(x32 * x32, axis=-1, keepdims=True) + EPS)
    return (y * g.astype(jnp.float32)).astype(x.dtype)


def layer_norm(x, g, b):
    x32 = x.astype(jnp.float32)
    mu = jnp.mean(x32, axis=-1, keepdims=True)
    var = jnp.mean(jnp.square(x32 - mu), axis=-1, keepdims=True)
    y = (x32 - mu) * lax.rsqrt(var + EPS) * g.astype(jnp.float32) + b.astype(jnp.float32)
    return y.astype(x.dtype)


def rope(x, pos):
    half = ROPE_DIM // 2
    inv = ROPE_THETA ** (-jnp.arange(half, dtype=jnp.float32) / half)
    ang = pos[:, None] * inv[None, :]
    ang = ang.reshape((ang.shape[0],) + (1,) * (x.ndim - 3) + (half,))
    cos, sin = jnp.cos(ang), jnp.sin(ang)
    x32 = x.astype(jnp.float32)
    x1, x2 = x32[..., :half], x32[..., half:]
    return jnp.concatenate([x1 * cos - x2 * sin, x2 * cos + x1 * sin], axis=-1).astype(x.dtype)


def causal_dwconv(u_ext, w, b):
    c = u_ext.shape[-1]
    out = lax.conv_general_dilated(u_ext, w[:, None, :].astype(u_ext.dtype), (1,), 'VALID',
                                   dimension_numbers=('NWC', 'WIO', 'NWC'), feature_group_count=c)
    return out + b


def multiscale_pool(u_ext, pos, w_pool):
    t = pos.shape[0]
    h0 = POOL_MAX - 1
    u32 = u_ext.astype(jnp.float32)
    cs = jnp.concatenate([jnp.zeros_like(u32[:, :1]), jnp.cumsum(u32, axis=1)], axis=1)
    u_new = u32[:, h0:]
    groups = []
    for g, w in enumerate(POOL_WINDOWS):
        sl = slice(g * POOL_GROUP, (g + 1) * POOL_GROUP)
        win_sum = cs[:, h0 + 1:, sl] - cs[:, h0 + 1 - w:h0 + 1 - w + t, sl]
        cnt = jnp.minimum(w, pos + 1).astype(jnp.float32)[None, :, None]
        groups.append(win_sum / cnt - u_new[..., sl])
    m = jnp.stack(groups, axis=2).astype(u_ext.dtype)
    out = jnp.einsum('btgc,gcd->btgd', m, w_pool)
    return out.reshape(out.shape[0], t, D_MODEL)


def mla_attend(q_abs, q_rope, ckv_all, krope_all, q_pos, w_uv):
    s_len = ckv_all.shape[1]
    k_chunk = jnp.arange(s_len, dtype=jnp.int32) // CHUNK

    def block(args):
        qa, qr, qp = args
        s = (jnp.einsum('bqhr,bsr->bhqs', qa, ckv_all)
             + jnp.einsum('bqhe,bse->bhqs', qr, krope_all)).astype(jnp.float32) * ATTN_SCALE
        allowed = k_chunk[None, :] <= (qp // CHUNK)[:, None]
        s = jnp.where(allowed[None, None], s, NEG)
        pr = jax.nn.softmax(s, axis=-1).astype(ckv_all.dtype)
        o_lat = jnp.einsum('bhqs,bsr->bqhr', pr, ckv_all)
        return jnp.einsum('bqhr,rhv->bqhv', o_lat, w_uv)

    b, t = q_abs.shape[0], q_abs.shape[1]
    if t <= Q_BLOCK:
        return block((q_abs, q_rope, q_pos))
    nb = t // Q_BLOCK

    def split(a):
        return jnp.moveaxis(a.reshape((b, nb, Q_BLOCK) + a.shape[2:]), 1, 0)

    out = lax.map(block, (split(q_abs), split(q_rope), q_pos.reshape(nb, Q_BLOCK)))
    return jnp.moveaxis(out, 0, 1).reshape(b, t, N_HEADS, V_DIM)


def trunk_layer(x, c, past_ckv, past_krope, hist_conv, hist_pool, hist_ffn, pos0, p):
    b, t, _ = x.shape
    pos_i = pos0 + jnp.arange(t, dtype=jnp.int32)
    pos_f = pos_i.astype(jnp.float32)
    mod = (jax.nn.silu(c) @ p['w_mod'] + p['b_mod']).reshape(b, 6, 1, D_MODEL)
    shift1, scale1, gate1 = mod[:, 0], mod[:, 1], mod[:, 2]
    shift2, scale2, gate2 = mod[:, 3], mod[:, 4], mod[:, 5]

    h = rms_norm(x, p['g_pre_mix']) * (1 + scale1) + shift1
    z = h @ p['w_in']
    za, zb = z[..., OFF_A:OFF_B], z[..., OFF_B:OFF_Q]
    zq, zkv, zr = z[..., OFF_Q:OFF_KV], z[..., OFF_KV:OFF_R], z[..., OFF_R:OFF_G]
    gates = jax.nn.sigmoid(z[..., OFF_G:].reshape(b, t, N_BRANCH, D_MODEL))

    u_a = za[..., :D_CONV] * jax.nn.sigmoid(za[..., D_CONV:])
    ua_ext = jnp.concatenate([hist_conv, u_a], axis=1)
    a = causal_dwconv(ua_ext, p['w_dwa'], p['b_dwa'])
    out_a = jax.nn.silu(layer_norm(a, p['ln_a_g'], p['ln_a_b'])) @ p['w_pa']

    ub_ext = jnp.concatenate([hist_pool, zb], axis=1)
    out_b = multiscale_pool(ub_ext, pos_i, p['w_pool']) * p['pool_scale']

    q = (rms_norm(zq, p['g_q_lat']) @ p['w_uq']).reshape(b, t, N_HEADS, NOPE_DIM + ROPE_DIM)
    q_abs = jnp.einsum('bthn,rhn->bthr', q[..., :NOPE_DIM], p['w_uk'])
    q_rope = rope(q[..., NOPE_DIM:], pos_f)
    ckv = rms_norm(zkv, p['g_kv_lat'])
    krope = rope(zr, pos_f)
    ckv_all = jnp.concatenate([past_ckv, ckv], axis=1)
    krope_all = jnp.concatenate([past_krope, krope], axis=1)
    o = mla_attend(q_abs, q_rope, ckv_all, krope_all, pos_i, p['w_uv'])
    out_c = o.reshape(b, t, N_HEADS * V_DIM) @ p['w_oc']

    merged = gates[:, :, 0] * out_a + gates[:, :, 1] * out_b + gates[:, :, 2] * out_c
    y = merged @ p['w_out']
    x = x + gate1 * rms_norm(y, p['g_post_mix'])

    h2 = rms_norm(x, p['g_pre_ffn']) * (1 + scale2) + shift2
    up = h2 @ p['w_up']
    af_ext = jnp.concatenate([hist_ffn, up[..., :D_FF]], axis=1)
    act = jax.nn.silu(causal_dwconv(af_ext, p['w_dwf'], p['b_dwf'])) * up[..., D_FF:]
    x = x + gate2 * rms_norm(act @ p['w_down'], p['g_post_ffn'])

    new_state = (ckv, krope, ua_ext[:, -(CONV_K - 1):], ub_ext[:, -(POOL_MAX - 1):], af_ext[:, -(FFN_K - 1):])
    return x, new_state


def setup_inputs(seed: int = 0) -> dict:
    key = jax.random.key(seed)
    ks = jax.random.split(key, 40)
    cnt = [0]

    def nrm(shape, scale=1.0):
        k = ks[cnt[0]]
        cnt[0] += 1
        return jax.random.normal(k, shape, jnp.float32) * scale

    def gain(shape):
        return 1.0 + 0.05 * nrm(shape)

    L = DEPTH
    return {
        'x_prompt': nrm((BATCH, SEQ, D_MODEL)),
        'x_sample': nrm((DEC_BATCH, DEC_SEQ, D_MODEL)),
        'cache_ckv': nrm((L, DEC_BATCH, PAST_LEN, KV_RANK)),
        'cache_krope': nrm((L, DEC_BATCH, PAST_LEN, ROPE_DIM)),
        'state_conv': nrm((L, DEC_BATCH, CONV_K - 1, D_CONV), 0.5),
        'state_pool': nrm((L, DEC_BATCH, POOL_MAX - 1, D_POOL)),
        'state_ffn': nrm((L, DEC_BATCH, FFN_K - 1, D_FF)),
        'c_prompt': nrm((BATCH, D_MODEL)),
        'c_sample': nrm((DEC_BATCH, D_MODEL)),
        'w_mod': nrm((L, D_MODEL, 6 * D_MODEL), 0.5 * D_MODEL ** -0.5),
        'b_mod': nrm((L, 6 * D_MODEL), 0.02),
        'g_pre_mix': gain((L, D_MODEL)),
        'g_post_mix': gain((L, D_MODEL)),
        'w_in': nrm((L, D_MODEL, N_IN), D_MODEL ** -0.5),
        'w_dwa': nrm((L, CONV_K, D_CONV), CONV_K ** -0.5),
        'b_dwa': nrm((L, D_CONV), 0.02),
        'ln_a_g': gain((L, D_CONV)),
        'ln_a_b': nrm((L, D_CONV), 0.02),
        'w_pa': nrm((L, D_CONV, D_MODEL), D_CONV ** -0.5),
        'w_pool': nrm((L, N_POOL_GROUPS, POOL_GROUP, POOL_OUT), POOL_GROUP ** -0.5),
        'pool_scale': gain((L, D_MODEL)),
        'g_q_lat': gain((L, Q_RANK)),
        'g_kv_lat': gain((L, KV_RANK)),
        'w_uq': nrm((L, Q_RANK, N_HEADS * (NOPE_DIM + ROPE_DIM)), Q_RANK ** -0.5),
        'w_uk': nrm((L, KV_RANK, N_HEADS, NOPE_DIM), KV_RANK ** -0.5),
        'w_uv': nrm((L, KV_RANK, N_HEADS, V_DIM), KV_RANK ** -0.5),
        'w_oc': nrm((L, N_HEADS * V_DIM, D_MODEL), (N_HEADS * V_DIM) ** -0.5),
        'w_out': nrm((L, D_MODEL, D_MODEL), D_MODEL ** -0.5),
        'g_pre_ffn': gain((L, D_MODEL)),
        'g_post_ffn': gain((L, D_MODEL)),
        'w_up': nrm((L, D_MODEL, 2 * D_FF), D_MODEL ** -0.5),
        'w_dwf': nrm((L, FFN_K, D_FF), FFN_K ** -0.5),
        'b_dwf': nrm((L, D_FF), 0.02),
        'w_down': nrm((L, D_FF, D_MODEL), D_FF ** -0.5),
    }


def reference(x_prompt, x_sample, cache_ckv, cache_krope, state_conv, state_pool, state_ffn,
              c_prompt, c_sample, w_mod, b_mod, g_pre_mix, g_post_mix, w_in, w_dwa, b_dwa,
              ln_a_g, ln_a_b, w_pa, w_pool, pool_scale, g_q_lat, g_kv_lat, w_uq, w_uk, w_uv,
              w_oc, w_out, g_pre_ffn, g_post_ffn, w_up, w_dwf, b_dwf, w_down):
    xp, xs = x_prompt, x_sample
    bp = x_prompt.shape[0]
    st_p = [[], [], [], [], []]
    st_s = [[], [], [], [], []]
    for l in range(DEPTH):
        p = dict(w_mod=w_mod[l], b_mod=b_mod[l], g_pre_mix=g_pre_mix[l], g_post_mix=g_post_mix[l],
                 w_in=w_in[l], w_dwa=w_dwa[l], b_dwa=b_dwa[l], ln_a_g=ln_a_g[l], ln_a_b=ln_a_b[l],
                 w_pa=w_pa[l], w_pool=w_pool[l], pool_scale=pool_scale[l], g_q_lat=g_q_lat[l],
                 g_kv_lat=g_kv_lat[l], w_uq=w_uq[l], w_uk=w_uk[l], w_uv=w_uv[l], w_oc=w_oc[l],
                 w_out=w_out[l], g_pre_ffn=g_pre_ffn[l], g_post_ffn=g_post_ffn[l], w_up=w_up[l],
                 w_dwf=w_dwf[l], b_dwf=b_dwf[l], w_down=w_down[l])
        xp, sp = trunk_layer(
            xp, c_prompt,
            jnp.zeros((bp, 0, KV_RANK), xp.dtype), jnp.zeros((bp, 0, ROPE_DIM), xp.dtype),
            jnp.zeros((bp, CONV_K - 1, D_CONV), xp.dtype), jnp.zeros((bp, POOL_MAX - 1, D_POOL), xp.dtype),
            jnp.zeros((bp, FFN_K - 1, D_FF), xp.dtype), 0, p)
        xs, ss = trunk_layer(xs, c_sample, cache_ckv[l], cache_krope[l], state_conv[l], state_pool[l],
                             state_ffn[l], PAST_LEN, p)
        for i in range(5):
            st_p[i].append(sp[i])
            st_s[i].append(ss[i])
    return (xp, xs,
            jnp.stack(st_p[0]), jnp.stack(st_p[1]), jnp.stack(st_p[2]), jnp.stack(st_p[3]), jnp.stack(st_p[4]),
            jnp.stack(st_s[0]), jnp.stack(st_s[1]), jnp.stack(st_s[2]), jnp.stack(st_s[3]), jnp.stack(st_s[4]))
```

```python
import numpy as np
from contextlib import ExitStack
import concourse.bass as bass
import concourse.mybir as mybir
from concourse.bass_utils import run_bass_kernel_spmd

F32 = mybir.dt.float32
BF16 = mybir.dt.bfloat16
AF = mybir.ActivationFunctionType
ALU = mybir.AluOpType
AX = mybir.AxisListType

L = 2
D = 2048
T = 1056
TP = 1024
TS = 32
NT = 352
DFF = 5632
NFF = 44
EPS = 1e-6
ATTN_SCALE = 192.0 ** -0.5
OFF_A, OFF_B, OFF_Q, OFF_KV, OFF_R, OFF_G = 0, 2048, 3072, 3584, 4096, 4160
PFL = 628
PF = dict(gpm=0, gpo=16, gpf=32, gpof=48, psc=64, bdwa=80, lng=88, lnb=96, gq=104, bdwf=108, bmod=152,
          wdwa=248, wdwf=496)
WCAP = 4096
NB = 3
ARENA = 163840


def dsize(dt):
    s = str(dt)
    if "64" in s:
        return 8
    if "32" in s:
        return 4
    if "16" in s:
        return 2
    return 1


class Op:
    __slots__ = ("eng", "seq", "fn", "waits", "kind", "gid", "qidx")


class KB:
    COMPUTE = ("pe", "act", "dve", "pool")
    EPOCH = 30000
    ND = {"sp": 16, "pool": 8}

    def __init__(self, nc):
        self.nc = nc
        self.ops = {e: [] for e in ("pe", "act", "dve", "pool", "sp")}
        self.state = {}
        self.ext_in = set()
        self.ext_out = set()
        self.known = {e: {} for e in self.ops}
        self.known_d = {e: set() for e in self.ops}
        self.needed = set()
        self.dmaq = {"sp": [], "pool": []}
        self.ccs = []
        self.gcount = 0
        self.allops = []

    def _span(self, ap):
        name = ap.tensor.name
        dsz = dsize(ap.dtype)
        dims = ap.ap
        off = ap.offset
        if name.startswith("sb") or name.startswith("ps"):
            row = dims[0][0]
            col = off % row if row > 0 else off
            ext = 1
            for st, cnt in dims[1:]:
                ext += (cnt - 1) * abs(st)
            page = 256 if name.startswith("sb") else 2048
            lo = col * dsz
            hi = (col + ext) * dsz - 1
        else:
            ext = 1
            for st, cnt in dims:
                ext += (cnt - 1) * abs(st)
            page = 65536
            lo = off * dsz
            hi = (off + ext) * dsz - 1
        return name, lo // page, hi // page

    def rec(self, eng, fn, ins, outs, kind="c"):
        op = Op()
        op.eng = eng
        op.fn = fn
        op.kind = kind
        op.seq = len(self.ops[eng])
        op.gid = self.gcount
        self.gcount += 1
        deps = set()
        for ap in ins:
            name, p0, p1 = self._span(ap)
            if name in self.ext_in:
                continue
            st = self.state.setdefault(name, {})
            isps = name.startswith("ps")
            for p in range(p0, p1 + 1):
                s = st.get(p)
                if s is not None and s[0] is not None:
                    deps.add(s[0])
                if isps and s is not None:
                    for e2, rop in s[1].items():
                        if e2 != eng:
                            deps.add(rop)
        for ap in outs:
            name, p0, p1 = self._span(ap)
            if name in self.ext_out:
                continue
            st = self.state.setdefault(name, {})
            for p in range(p0, p1 + 1):
                s = st.get(p)
                if s is not None:
                    if s[0] is not None:
                        deps.add(s[0])
                    deps.update(s[1].values())
                    deps.update(s[2])
        need_c = {}
        need_d = []
        for d in deps:
            if d.kind == "c":
                if eng == "pe" and d.eng == "pe":
                    continue
                if need_c.get(d.eng, -1) < d.seq:
                    need_c[d.eng] = d.seq
            else:
                need_d.append(d)
        waits = []
        kn = self.known[eng]
        for pe, sq in need_c.items():
            if kn.get(pe, -1) >= sq:
                continue
            kn[pe] = sq
            waits.append(("c", pe, sq))
            self.needed.add((pe, sq))
        kd = self.known_d[eng]
        for d in need_d:
            if d.gid in kd:
                continue
            kd.add(d.gid)
            waits.append(("d", d))
        op.waits = waits
        for ap in ins:
            name, p0, p1 = self._span(ap)
            if name in self.ext_in:
                continue
            st = self.state[name]
            for p in range(p0, p1 + 1):
                s = st.get(p)
                if s is None:
                    s = [None, {}, []]
                    st[p] = s
                if kind == "c":
                    s[1][eng] = op
                else:
                    s[2].append(op)
        for ap in outs:
            name, p0, p1 = self._span(ap)
            if name in self.ext_out:
                continue
            st = self.state[name]
            for p in range(p0, p1 + 1):
                st[p] = [op, {}, []]
        self.ops[eng].append(op)
        if kind == "d":
            op.qidx = len(self.dmaq[eng])
            self.dmaq[eng].append(op)
        elif kind == "cc":
            op.qidx = len(self.ccs)
            self.ccs.append(op)
        return op

    def emit(self):
        nc = self.nc
        with ExitStack() as es:
            signo = {}
            cnt = {}
            for e in self.COMPUTE:
                n = 0
                for op in self.ops[e]:
                    if op.kind == "c" and (e, op.seq) in self.needed:
                        signo[(e, op.seq)] = n
                        n += 1
                cnt[e] = n
            csem = {}
            for e in self.COMPUTE:
                ne = cnt[e] // self.EPOCH + 1
                csem[e] = [es.enter_context(nc.semaphore(f"c_{e}_{i}")) for i in range(ne)]
            dsem = {q: [es.enter_context(nc.semaphore(f"d_{q}_{i}")) for i in range(self.ND[q])] for q in self.dmaq}
            ccsem = [es.enter_context(nc.semaphore(f"cc_{i}")) for i in range(len(self.ccs))]

            def ev(w):
                if w[0] == "c":
                    n = signo[(w[1], w[2])]
                    return csem[w[1]][n // self.EPOCH], n % self.EPOCH + 1
                d = w[1]
                if d.kind == "cc":
                    return ccsem[d.qidx], 1
                nd = self.ND[d.eng]
                return dsem[d.eng][d.qidx % nd], 16 * (d.qidx // nd + 1)

            block = es.enter_context(nc.Block())

            def run(ename, e):
                for op in self.ops[ename]:
                    for w in op.waits:
                        s, v = ev(w)
                        e.wait_ge(s, v)
                    if op.kind == "d":
                        nd = self.ND[ename]
                        if op.qidx >= nd:
                            e.wait_ge(dsem[ename][op.qidx % nd], 16 * (op.qidx // nd))
                        op.fn(e).then_inc(dsem[ename][op.qidx % nd], 16)
                    elif op.kind == "cc":
                        op.fn(e).then_inc(ccsem[op.qidx])
                    else:
                        ins = op.fn(e)
                        key = (ename, op.seq)
                        if key in signo:
                            n = signo[key]
                            ins.then_inc(csem[ename][n // self.EPOCH], 1)
                if ename in self.dmaq:
                    nd = self.ND[ename]
                    q = self.dmaq[ename]
                    for i in range(min(nd, len(q))):
                        last = ((len(q) - 1 - i) // nd) * nd + i
                        e.wait_ge(dsem[ename][i], 16 * (last // nd + 1))
                if ename == "pool":
                    for i in range(len(self.ccs)):
                        e.wait_ge(ccsem[i], 1)

            @block.tensor
            def _(e):
                run("pe", e)

            @block.scalar
            def _(e):
                run("act", e)

            @block.vector
            def _(e):
                run("dve", e)

            @block.gpsimd
            def _(e):
                run("pool", e)

            @block.sync
            def _(e):
                run("sp", e)


class _Stop(Exception):
    pass


STAGE = None


def build_program():
    nc = bass.Bass("TRN2", target_bir_lowering=False)
    K = KB(nc)

    def ck(n):
        if STAGE is not None and n > STAGE:
            raise _Stop()

    import os
    DUMP = os.environ.get("DBG_DUMP", "") != ""

    def dump(name, ap2d, dt, ncols):
        if not DUMP:
            return
        K.ext_out.add("dbg_" + name)
        t_ = nc.dram_tensor("dbg_" + name, [128, ncols], dt, kind="ExternalOutput")
        K.rec("sp", lambda e: e.dma_start(out=t_[:, :], in_=ap2d), [ap2d], [t_[:, :]], kind="d")

    def din(name, shape, dt=F32):
        K.ext_in.add(name)
        return nc.dram_tensor(name, list(shape), dt, kind="ExternalInput")

    def dout(name, shape):
        K.ext_out.add(name)
        return nc.dram_tensor(name, list(shape), F32, kind="ExternalOutput")

    def dscr(name, shape, dt=F32):
        return nc.dram_tensor(name, list(shape), dt)

    d_xin = din("xin", [T, D])
    d_cT = din("cT", [128, 16, 2])
    d_pf = din("pf", [128, L * PFL])
    d_gkv = din("gkv", [L, 128, 512])
    d_ropeF = din("ropeF", [64, 2, T])
    d_ropeT = din("ropeT", [128, 9, 2, 64])
    d_qB = din("qB", [64, TP])
    d_khot = din("khot", [64, TP])
    d_hsel = din("hsel", [128, 4])
    d_icnt = din("icnt", [128, 4, 16])
    d_sconv = din("sconv", [L, 128, 8, 30])
    d_spool = din("spool", [L, 128, 8, 15])
    d_sffn = din("sffn", [L, 128, 2, NFF])
    if STAGE is None or STAGE >= 4:
        d_cacheT = din("cacheT", [L, 576, 4096])
        d_cacheV = din("cacheV", [L, 4096, 512])
    WSPEC = [("wmod", 48, 4096), ("wA", 8, 4096), ("wB", 4, 4096), ("wQ", 2, 4096), ("wKVR", 1, 9216),
             ("wHD", 16, 2048), ("wUV", 4, 2048), ("wM4", 16, 9472), ("wOUT", 8, 4096), ("wUP", NFF, 4096),
             ("wDN", 32, 2816)]
    FIRST = {"wmod": 0, "wKVR": 1, "wA": 2, "wB": 2, "wQ": 2, "wHD": 4, "wUV": 4, "wM4": 5, "wOUT": 6, "wUP": 7, "wDN": 8}
    w_full = {}
    for name, nblk, E in WSPEC:
        w_full[name] = []
        for l in range(L):
            need = STAGE is None or (l == 0 and STAGE >= FIRST[name])
            if need:
                w_full[name].append(din(f"{name}{l}", [nblk * 128, E]))
            else:
                w_full[name].append(nc.dram_tensor(f"d_wf_{name}{l}", [nblk * 128, E], F32))

    def gw(name, l, n):
        return w_full[name][l][n * 128:(n + 1) * 128, :]
    o_y = dout("o_y", [T, D])
    o_ckv = dout("o_ckv", [L, T, 512])
    o_kr = dout("o_kr", [L, T, 64])
    o_conv = dout("o_conv", [L, 2, 30, 1024])
    o_pool = dout("o_pool", [L, 2, 15, 1024])
    o_ffn = dout("o_ffn", [L, 2, 2, DFF])
    xs = [dscr(f"d_xs{f}", [128, T]) for f in range(16)]
    ys = [dscr(f"d_ys{f}", [128, T]) for f in range(16)]
    d_hsp = dscr("d_hsp", [128, 16 * T], BF16)
    d_sasp = dscr("d_sasp", [128, 8 * T], BF16)
    d_msp = dscr("d_msp", [128, 8 * T], BF16)
    d_bkv = [dscr(f"d_bkv{i}", [384, 1024], BF16) for i in range(3)]
    d_gkvb = [dscr(f"d_gkvb{i}", [1536, 1024], BF16) for i in range(3)]
    d_bh = dscr("d_bh", [128, 360])
    d_gh = dscr("d_gh", [512, 360])
    d_bh3 = dscr("d_bh3", [128, 88])
    d_gh3 = dscr("d_gh3", [512, 88])

    es = ExitStack()
    sbt = lambda name, shape, dt: es.enter_context(nc.sbuf_tensor(name, list(shape), dt))
    identf = sbt("sb_identf", [128, 128], F32)
    identb = sbt("sb_identb", [128, 128], BF16)
    ones = sbt("sb_ones", [128, 128], BF16)
    pf = sbt("sb_pf", [128, L * PFL], F32)
    modT = sbt("sb_mod", [128, L, 96, 2], F32)
    mv = sbt("sb_mv", [128, L, 6, 16, 2], F32)
    ropeF = sbt("sb_ropeF", [64, 2, T], F32)
    hsel = sbt("sb_hsel", [128, 4], F32)
    icnt = sbt("sb_icnt", [128, 4, 16], F32)
    cTf = sbt("sb_cTf", [128, 16, 2], F32)
    cTb = sbt("sb_cTb", [128, 16, 2], BF16)
    small = sbt("sb_small", [128, 64], F32)
    wbuf = sbt("sb_wbuf", [128, NB, WCAP], BF16)
    arena = sbt("sb_arena", [128, ARENA // 4], F32)
    ps = es.enter_context(nc.psum_tensor("ps_all", [128, 4096], F32))

    def view(off, dt, shape):
        dsz = dsize(dt)
        n = int(np.prod(shape[1:]))
        nbytes = n * dsz
        assert off % 4 == 0 and nbytes % 4 == 0 and off + nbytes <= ARENA, (off, shape)
        a = arena[:, off // 4:(off + nbytes) // 4]
        if dt == BF16:
            a = a.bitcast(BF16)
        if len(shape) == 3:
            a = a.rearrange("p (a b) -> p a b", a=shape[1])
        elif len(shape) == 4:
            a = a.rearrange("p (a b c) -> p a b c", a=shape[1], b=shape[2])
        if shape[0] != 128:
            a = a[0:shape[0]]
        return a

    def bank(b, n=512, lo=0):
        return ps[:, 512 * b + lo:512 * b + lo + n]

    def bank_bf(b, lo_bytes, shape):
        n = int(np.prod(shape[1:]))
        a = ps[:, 512 * b + lo_bytes // 4:512 * b + lo_bytes // 4 + n // 2].bitcast(BF16)
        if len(shape) == 3:
            a = a.rearrange("p (a b) -> p a b", a=shape[1])
        return a

    def mm(out, lhsT, rhs, start=True, stop=True):
        K.rec("pe", lambda e: e.matmul(out, lhsT=lhsT, rhs=rhs, start=start, stop=stop), [lhsT, rhs], [out])

    def tr(out, in_, ident):
        K.rec("pe", lambda e: e.transpose(out, in_, ident), [in_, ident], [out])

    def act(out, in_, func, scale=None, bias=None):
        ins = [in_]
        kw = {}
        if scale is not None:
            kw["scale"] = scale
            if not isinstance(scale, float):
                ins.append(scale)
        if bias is not None:
            kw["bias"] = bias
            if not isinstance(bias, float):
                ins.append(bias)
        K.rec("act", lambda e: e.activation(out=out, in_=in_, func=func, **kw), ins, [out])

    def tt(eng, out, in0, in1, op):
        K.rec(eng, lambda e: e.tensor_tensor(out=out, in0=in0, in1=in1, op=op), [in0, in1], [out])

    def ts(eng, out, in0, s1, s2, op0, op1=None):
        ins = [in0] + [s for s in (s1, s2) if s is not None and not isinstance(s, float)]
        if op1 is None:
            K.rec(eng, lambda e: e.tensor_scalar(out=out, in0=in0, scalar1=s1, scalar2=None, op0=op0), ins, [out])
        else:
            K.rec(eng, lambda e: e.tensor_scalar(out=out, in0=in0, scalar1=s1, scalar2=s2, op0=op0, op1=op1), ins, [out])

    def stt(eng, out, in0, scalar, in1, op0, op1):
        ins = [in0, in1] + ([] if isinstance(scalar, float) else [scalar])
        K.rec(eng, lambda e: e.scalar_tensor_tensor(out=out, in0=in0, scalar=scalar, in1=in1, op0=op0, op1=op1), ins, [out])

    def cp(eng, out, in_):
        if eng == "act":
            K.rec("act", lambda e: e.copy(out=out, in_=in_), [in_], [out])
        else:
            K.rec(eng, lambda e: e.tensor_copy(out=out, in_=in_), [in_], [out])

    def recip(out, in_):
        K.rec("dve", lambda e: e.reciprocal(out=out, in_=in_), [in_], [out])

    def dma(q, out, in_):
        return K.rec(q, lambda e: e.dma_start(out=out, in_=in_), [in_], [out], kind="d")

    def allgather(src, dst, groups=((0, 1, 2, 3), (4, 5, 6, 7))):
        assert src.ap().nbytes() if False else True
        K.rec("pool", lambda e: e.collective_compute("AllGather", ALU.bypass, replica_groups=[list(g) for g in groups],
                                                     ins=[src.ap().opt()], outs=[dst.ap().opt()]),
              [src.ap()], [dst.ap()], kind="cc")

    cpi = [0]

    def cpa(out, in_):
        cpi[0] += 1
        cp("act" if cpi[0] % 2 else "dve", out, in_)

    def rsqrt_to(dst, src, scale):
        ts("dve", dst, src, scale, EPS, ALU.mult, ALU.add)
        act(dst, dst, AF.Sqrt)
        recip(dst, dst)

    plan = []
    for n in range(48):
        plan.append(("mod0", gw("wmod", 0, n), 4096))
    for l in range(L):
        for j in range(8):
            plan.append((f"A{l}", gw("wA", l, j), 4096))
        for j in range(4):
            plan.append((f"B{l}", gw("wB", l, j), 4096))
        for j in range(2):
            plan.append((f"Q{l}", gw("wQ", l, j), 4096))
        if l == 0 and STAGE is None:
            for n in range(48):
                plan.append(("mod1", gw("wmod", 1, n), 4096))
        for h in range(16):
            plan.append((f"HD{l}", gw("wHD", l, h), 2048))
        for j in range(4):
            plan.append((f"UV{l}", gw("wUV", l, j), 2048))
        for f in range(16):
            plan.append((f"M4{l}", gw("wM4", l, f)[:, 0:3072], 3072))
            plan.append((f"M4{l}", gw("wM4", l, f)[:, 3072:5376], 2304))
            plan.append((f"M4{l}", gw("wM4", l, f)[:, 5376:9472], 4096))
        for j in range(8):
            plan.append((f"OUT{l}", gw("wOUT", l, j), 4096))
        for j in range(NFF):
            plan.append((f"UP{l}", gw("wUP", l, j), 4096))
        for j in range(32):
            plan.append((f"DN{l}", gw("wDN", l, j), 2816))
    wstate = {"issued": 0, "next": 0}

    def wget(tag):
        n = wstate["next"]
        assert plan[n][0] == tag, (plan[n][0], tag, n)
        while wstate["issued"] < min(len(plan), n + wstate.get("depth", NB)):
            m = wstate["issued"]
            _, src, E = plan[m]
            dma("pool", wbuf[:, m % NB, 0:E], src)
            wstate["issued"] += 1
        wstate["next"] += 1
        return wbuf[:, n % NB, 0:plan[n][2]]

    def w3(wb, k, m):
        return wb.rearrange("p (k m) -> p k m", k=k)

    pbs = {"list": list(range(8)), "i": 0}

    def pb_set(lst):
        pbs["list"] = list(lst)
        pbs["i"] = 0

    def pb():
        b = pbs["list"][pbs["i"] % len(pbs["list"])]
        pbs["i"] += 1
        return b

    tiles3 = [(0, 352), (352, 704), (704, 1056)]

    def segs(lo, hi, H):
        out = []
        if lo < TP:
            e = min(hi, TP)
            out.append((0, e - lo, H + lo))
        if hi > TP:
            s = max(lo, TP)
            out.append((s - lo, hi - lo, s + 2 * H))
        return out

    def pfv(l, name, n):
        o = l * PFL + PF[name]
        return pf[:, o:o + n]

    A0 = 0
    hT = view(A0, BF16, [128, 16, T])
    XNEW = view(33792, F32, [128, 16, T])
    QLAT = view(150528, BF16, [128, 4, T])
    KTSN = view(158976, BF16, [128, 5, 32])
    VSN = view(159296, BF16, [128, 512])
    OT = view(107520, BF16, [128, 16, T])
    BC0 = view(140864, F32, [128, 1088])
    BC1 = view(145216, F32, [128, 1088])
    SM2 = view(160320, F32, [128, 880])

    K.rec("pool", lambda e: e.memset(identf[:], 0.0), [], [identf[:]])
    K.rec("pool", lambda e: e.affine_select(out=identf[:], in_=identf[:], pattern=[[-1, 128]], compare_op=ALU.not_equal,
                                            fill=1.0, base=0, channel_multiplier=1), [identf[:]], [identf[:]])
    cp("dve", identb[:], identf[:])
    K.rec("dve", lambda e: e.memset(ones[:], 1.0), [], [ones[:]])
    dma("sp", pf[:], d_pf[:, :])
    dma("sp", cTf[:], d_cT[:, :, :])
    dma("sp", ropeF[:], d_ropeF[:, :, :])
    dma("sp", hsel[:], d_hsel[:, :])
    dma("sp", icnt[:], d_icnt[:, :, :])
    act(cTb[:], cTf[:], AF.Silu)

    def mod_compute(l):
        bk = 7
        for n in range(48):
            wb = w3(wget(f"mod{l}"), 16, 256)
            for m in range(2):
                ch = 2 * n + m
                for k in range(16):
                    mm(bank(bk, 2, 2 * ch), wb[:, k, 128 * m:128 * m + 128], cTb[:, k, :], k == 0, k == 15)
        psm = bank(bk, 192).rearrange("p (c s) -> p c s", s=2)
        for s in range(2):
            tt("dve", modT[:, l, :, s], psm[:, :, s], pfv(l, "bmod", 96), ALU.add)
        for s in range(2):
            tmp = small[:, 0:16]
            ts("dve", tmp, modT[:, l, 16:32, s], 1.0, None, ALU.add)
            tt("dve", mv[:, l, 0, :, s], tmp, pfv(l, "gpm", 16), ALU.mult)
            cp("dve", mv[:, l, 1, :, s], modT[:, l, 0:16, s])
            tt("dve", mv[:, l, 2, :, s], modT[:, l, 32:48, s], pfv(l, "gpo", 16), ALU.mult)
            ts("dve", tmp, modT[:, l, 64:80, s], 1.0, None, ALU.add)
            tt("dve", mv[:, l, 3, :, s], tmp, pfv(l, "gpf", 16), ALU.mult)
            cp("dve", mv[:, l, 4, :, s], modT[:, l, 48:64, s])
            tt("dve", mv[:, l, 5, :, s], modT[:, l, 80:96, s], pfv(l, "gpof", 16), ALU.mult)

    mod_compute(0)

    TOK = [view(101376, F32, [128, D]), view(109568, F32, [128, D])]
    pb_set([0, 1, 2, 3])
    for t9 in range(9):
        rows = 128 if t9 < 8 else 32
        tc0 = t9 * 128
        tok = TOK[t9 % 2]
        dma("sp", tok[0:rows, :], d_xin[tc0:tc0 + rows, :])
        for g in range(4):
            b = pb()
            for j in range(4):
                f = 4 * g + j
                tr(bank(b, rows, 128 * j), tok[0:rows, 128 * f:128 * f + 128], identf[0:rows, 0:rows])
            src = bank(b).rearrange("p (j t) -> p j t", j=4)[:, :, 0:rows]
            cpa(XNEW[:, 4 * g:4 * g + 4, tc0:tc0 + rows], src)

    def prenorm(l, sub, dst):
        SQ = [view(101376, BF16, [128, T]), view(103488, BF16, [128, T])]
        TMPF = [view(105600, F32, [128, T]), view(109824, F32, [128, T])]
        ssb = [5, 6, 7]
        for f in range(16):
            sq = SQ[f % 2]
            act(sq, XNEW[:, f, :], AF.Square)
            for i, (lo, hi) in enumerate(tiles3):
                mm(bank(ssb[i], NT), ones[:], sq[:, lo:hi], f == 0, f == 15)
            dma("sp", xs[f][:, :], XNEW[:, f, :])
        for i, (lo, hi) in enumerate(tiles3):
            rsqrt_to(BC0[:, lo:hi], bank(ssb[i], NT), 1.0 / D)
        ia, ib = (0, 1) if sub == 0 else (3, 4)
        for f in range(16):
            tmp = TMPF[f % 2]
            tt("dve", tmp, XNEW[:, f, :], BC0[:, 0:T], ALU.mult)
            ts("dve", dst[:, f, 0:TP], tmp[:, 0:TP], mv[:, l, ia, f, 0:1], mv[:, l, ib, f, 0:1], ALU.mult, ALU.add)
            ts("dve", dst[:, f, TP:T], tmp[:, TP:T], mv[:, l, ia, f, 1:2], mv[:, l, ib, f, 1:2], ALU.mult, ALU.add)

    def tail_out(src_fn, nch, w, dst):
        st = view(101376, F32, [128, 1024])
        for j in range(nch):
            tr(bank(2 + j // 4, 128, 128 * (j % 4))[0:w, :], src_fn(j), identf[:])
        for half in range(nch // 4):
            cpa(st[0:w, 512 * half:512 * half + 512], bank(2 + half)[0:w, :])
        dma("sp", dst, st[0:w, 0:nch * 128])

    def residual_pass(l, sub):
        YB = [view(101376, F32, [128, T]), view(105600, F32, [128, T])]
        TMPF = [view(109824, F32, [128, T]), view(114048, F32, [128, T])]
        for i, (lo, hi) in enumerate(tiles3):
            rsqrt_to(BC0[:, lo:hi], bank(5 + i, NT), 1.0 / D)
        ig = 2 if sub == 0 else 5
        for f in range(16):
            yb = YB[f % 2]
            tmp = TMPF[f % 2]
            dma("sp", yb, ys[f][:, :])
            dma("sp", XNEW[:, f, :], xs[f][:, :])
            tt("dve", tmp, yb, BC0[:, 0:T], ALU.mult)
            stt("dve", XNEW[:, f, 0:TP], tmp[:, 0:TP], mv[:, l, ig, f, 0:1], XNEW[:, f, 0:TP], ALU.mult, ALU.add)
            stt("dve", XNEW[:, f, TP:T], tmp[:, TP:T], mv[:, l, ig, f, 1:2], XNEW[:, f, TP:T], ALU.mult, ALU.add)

    def yproj_evac(b, f, lo, hi, i, YF, first, last):
        SQt = [view(141312, BF16, [128, NT]), view(142016, BF16, [128, NT])]
        sq = SQt[(f * 3 + i) % 2]
        cp("act", YF[:, lo:hi], bank(b, NT))
        act(sq, bank(b, NT), AF.Square)
        mm(bank(5 + i, NT), ones[:], sq, first, last)

    def layer(l):
        UAT = view(33792, BF16, [128, 8, 1116])
        ZBT = view(51648, BF16, [128, 8, 1086])
        KTST = view(85920, BF16, [128, 5, T])
        VST = view(96480, BF16, [128, 9, 512])
        WKVR = view(105696, BF16, [128, 16, 576])
        ZQST = view(105696, F32, [128, 4, T])
        UATAIL = view(124128, F32, [128, 8, 60])
        ZBTAIL = view(126048, F32, [128, 8, 30])
        GHB = view(127008, F32, [128, 4, 360])
        ROPET = view(132768, F32, [128, 9, 2, 64])
        GKV = view(137376, F32, [128, 512])
        SCONV = view(139424, F32, [128, 8, 30])
        SPOOL = view(140384, F32, [128, 8, 15])
        JUNK = [view(69024, F32, [128, 512]), view(71072, F32, [128, 512])]
        CKVF = [view(73120, F32, [128, 512]), view(75168, F32, [128, 512])]
        KRU = [view(77216, F32, [128, 64]), view(77472, F32, [128, 64])]
        KRV = [view(77728, F32, [128, 64]), view(77984, F32, [128, 64])]
        KRF = [view(78240, F32, [128, 64]), view(78496, F32, [128, 64])]
        KRB = [view(78752, BF16, [128, 64]), view(78880, BF16, [128, 64])]
        SGT = [view(79008, F32, [128, NT]), view(80416, F32, [128, NT])]
        SQT = [view(81824, BF16, [128, NT]), view(82528, BF16, [128, NT])]

        dma("pool", WKVR.rearrange("p k m -> p (k m)"), gw("wKVR", l, 0))
        dma("sp", ROPET, d_ropeT[:, :, :, :])
        dma("sp", GKV, d_gkv[l])
        dma("pool", KTST[64:128, 4, 0:TP], d_khot[:, :])
        dma("sp", SCONV, d_sconv[l])
        dma("sp", SPOOL, d_spool[l])
        for t9 in range(9):
            rows = 128 if t9 < 8 else 32
            tc0 = t9 * 128
            i2 = t9 % 2
            bx, by = (0, 1) if i2 == 0 else (2, 3)
            psx = bank(bx)[0:rows, :]
            psy = bank(by, 64)[0:rows, :]
            for k in range(16):
                mm(psx, hT[:, k, tc0:tc0 + rows], WKVR[:, k, 0:512], k == 0, k == 15)
            for k in range(16):
                mm(psy, hT[:, k, tc0:tc0 + rows], WKVR[:, k, 512:576], k == 0, k == 15)
            junk = JUNK[i2][0:rows]
            ssk = small[0:rows, 16 + t9:17 + t9]
            act(junk, psx, AF.Square)
            K.rec("dve", lambda e, ssk=ssk, junk=junk: e.reduce_sum(out=ssk, in_=junk, axis=AX.X), [junk], [ssk])
            rsqrt_to(ssk, ssk, 1.0 / 512)
            ckvf = CKVF[i2][0:rows]
            stt("dve", ckvf, psx, ssk, GKV[0:rows], ALU.mult, ALU.mult)
            dma("sp", o_ckv[l, tc0:tc0 + rows, :], ckvf)
            cp("act", VST[0:rows, t9, :], ckvf)
            pst = bank_bf(4 + i2, 0, [128, 4, 128])
            for c in range(4):
                tr(pst[:, c, 0:rows], VST[0:rows, t9, 128 * c:128 * c + 128], identb[0:rows, 0:rows])
            cpa(KTST[:, 0:4, tc0:tc0 + rows], pst[:, :, 0:rows])
            kru, krv, krf, krb = KRU[i2][0:rows], KRV[i2][0:rows], KRF[i2][0:rows], KRB[i2][0:rows]
            tt("dve", kru, psy, ROPET[0:rows, t9, 0, :], ALU.mult)
            tt("dve", krv[:, 0:32], psy[:, 32:64], ROPET[0:rows, t9, 1, 0:32], ALU.mult)
            tt("dve", krv[:, 32:64], psy[:, 0:32], ROPET[0:rows, t9, 1, 32:64], ALU.mult)
            tt("dve", krf, kru, krv, ALU.add)
            dma("sp", o_kr[l, tc0:tc0 + rows, :], krf)
            cp("act", krb, krf)
            pst2 = bank_bf(6 + i2, 0, [128, 128])
            tr(pst2[0:64, 0:rows], krb, identb[0:rows, 0:rows])
            cpa(KTST[0:64, 4, tc0:tc0 + rows], pst2[0:64, 0:rows])
        ck(1.5)
        dma("sp", d_bkv[0][0:384, :].rearrange("(c p) k -> p c k", p=128), KTST[:, 0:3, 0:TP])
        dma("sp", d_bkv[1][0:256, :].rearrange("(c p) k -> p c k", p=128), KTST[:, 3:5, 0:TP])
        dma("sp", d_bkv[1][256:384, :].rearrange("r (x d) -> (r x) d", d=512).rearrange("(t p) d -> p t d", p=128),
            VST[:, 0:2, :])
        dma("sp", d_bkv[2][0:384, :].rearrange("r (x d) -> (r x) d", d=512).rearrange("(t p) d -> p t d", p=128),
            VST[:, 2:8, :])
        cp("dve", KTSN, KTST[:, :, TP:T])
        cp("dve", VSN[0:32, :], VST[0:32, 8, :])
        ck(1.7)
        for i3 in range(3):
            allgather(d_bkv[i3], d_gkvb[i3])

        ck(2)
        pb_set([0, 1, 2, 3, 4, 5, 6, 7])
        for j in range(8):
            wb = w3(wget(f"A{l}"), 16, 256)
            for i, (lo, hi) in enumerate(tiles3):
                bu, bg = pb(), pb()
                for k in range(16):
                    mm(bank(bu, NT), wb[:, k, 0:128], hT[:, k, lo:hi], k == 0, k == 15)
                for k in range(16):
                    mm(bank(bg, NT), wb[:, k, 128:256], hT[:, k, lo:hi], k == 0, k == 15)
                sg = SGT[i % 2]
                act(sg, bank(bg, NT), AF.Sigmoid)
                for (a, b_, dlo) in segs(lo, hi, 30):
                    tt("dve", UAT[:, j, dlo:dlo + (b_ - a)], bank(bu, NT)[:, a:b_], sg[:, a:b_], ALU.mult)
                if i == 2:
                    tt("dve", UATAIL[:, j, 0:30], bank(bu, NT)[:, 290:320], sg[:, 290:320], ALU.mult)
                    tt("dve", UATAIL[:, j, 30:60], bank(bu, NT)[:, 322:352], sg[:, 322:352], ALU.mult)
        for n in range(4):
            wb = w3(wget(f"B{l}"), 16, 256)
            for m in range(2):
                ch = 2 * n + m
                for i, (lo, hi) in enumerate(tiles3):
                    b = pb()
                    for k in range(16):
                        mm(bank(b, NT), wb[:, k, 128 * m:128 * m + 128], hT[:, k, lo:hi], k == 0, k == 15)
                    for (a, b_, dlo) in segs(lo, hi, 15):
                        cpa(ZBT[:, ch, dlo:dlo + (b_ - a)], bank(b, NT)[:, a:b_])
                    if i == 2:
                        cp("dve", ZBTAIL[:, ch, 0:15], bank(b, NT)[:, 305:320])
                        cp("dve", ZBTAIL[:, ch, 15:30], bank(b, NT)[:, 337:352])
        dma("sp", d_bh[:, 0:240].rearrange("p (j t) -> p j t", j=8), UATAIL[:, :, 0:30])
        dma("sp", d_bh[:, 240:360].rearrange("p (j t) -> p j t", j=8), ZBTAIL[:, :, 0:15])
        allgather(d_bh, d_gh)
        tail_out(lambda j: UATAIL[:, j, 0:30], 8, 30, o_conv[l, 0])
        tail_out(lambda j: UATAIL[:, j, 30:60], 8, 30, o_conv[l, 1])
        tail_out(lambda j: ZBTAIL[:, j, 0:15], 8, 15, o_pool[l, 0])
        tail_out(lambda j: ZBTAIL[:, j, 15:30], 8, 15, o_pool[l, 1])

        pb_set([0, 1, 2, 3, 4])
        for n in range(2):
            wb = w3(wget(f"Q{l}"), 16, 256)
            for m in range(2):
                ch = 2 * n + m
                for i, (lo, hi) in enumerate(tiles3):
                    b = pb()
                    for k in range(16):
                        mm(bank(b, NT), wb[:, k, 128 * m:128 * m + 128], hT[:, k, lo:hi], k == 0, k == 15)
                    cp("act", ZQST[:, ch, lo:hi], bank(b, NT))
                    sq = SQT[i % 2]
                    act(sq, bank(b, NT), AF.Square)
                    mm(bank(5 + i, NT), ones[:], sq, ch == 0, ch == 3)
        for i, (lo, hi) in enumerate(tiles3):
            rsqrt_to(BC0[:, lo:hi], bank(5 + i, NT), 1.0 / 512)
        for ch in range(4):
            stt("dve", QLAT[:, ch, :], ZQST[:, ch, :], pfv(l, "gq", 4)[:, ch:ch + 1], BC0[:, 0:T], ALU.mult, ALU.mult)
        dma("sp", d_hsp[:, :], hT.rearrange("p a b -> p (a b)"))
        if l == 0:
            dump("hT", hT.rearrange("p a b -> p (a b)"), BF16, 16 * T)
            dump("qlat", QLAT.rearrange("p a b -> p (a b)"), BF16, 4 * T)
        if l == 0 and STAGE is None:
            mod_compute(1)

        ck(3)
        HT = view(69024, F32, [128, 360])
        dma("sp", GHB, d_gh.ap().rearrange("(r p) f -> p r f", p=128))
        ts("dve", HT, GHB[:, 0, :], hsel[:, 0:1], None, ALU.mult)
        for r in range(1, 4):
            stt("dve", HT, GHB[:, r, :], hsel[:, r:r + 1], HT, ALU.mult, ALU.add)
        cp("dve", UAT[:, :, 0:30], HT[:, 0:240].rearrange("p (j t) -> p j t", j=8))
        cp("dve", ZBT[:, :, 0:15], HT[:, 240:360].rearrange("p (j t) -> p j t", j=8))
        cp("dve", UAT[:, :, 1054:1084], SCONV)
        cp("dve", ZBT[:, :, 1039:1054], SPOOL)
        MT = view(120672, BF16, [128, 8, T])
        SAT = view(103776, BF16, [128, 8, T])
        AT = view(69024, F32, [128, 8, 1086])
        P = [view(15872, F32, [128, 2, 1086]), view(24560, F32, [128, 2, 1086])]
        T16 = view(33248, F32, [128, 16])
        for g in range(4):
            src = ZBT[:, 2 * g:2 * g + 2, :]
            w = 2 ** (g + 1)
            cur = src
            for i in range(g + 1):
                st = 2 ** i
                dst = P[i % 2]
                tt("dve", dst[:, :, st:1086], cur[:, :, st:1086], cur[:, :, 0:1086 - st], ALU.add)
                cur = dst
            stt("dve", MT[:, 2 * g:2 * g + 2, 0:TP], cur[:, :, 15:15 + TP], 1.0 / w, src[:, :, 15:15 + TP], ALU.mult, ALU.subtract)
            stt("dve", MT[:, 2 * g:2 * g + 2, TP:T], cur[:, :, 1054:1086], 1.0 / w, src[:, :, 1054:1086], ALU.mult, ALU.subtract)
            for c2 in range(2):
                tt("dve", T16, cur[:, c2, 15:31], icnt[:, g, :], ALU.mult)
                tt("dve", MT[:, 2 * g + c2, 0:16], T16, src[:, c2, 15:31], ALU.subtract)
        DG = [view(0, BF16, [128, 31, 128]), view(7936, BF16, [128, 31, 128])]
        SQc = [view(160320, BF16, [128, 362]), view(161044, BF16, [128, 362])]
        ABc = [view(161768, BF16, [128, 362]), view(162492, BF16, [128, 362])]
        ctiles = [(0, 362), (362, 724), (724, 1086)]
        wd = pfv(l, "wdwa", 248).rearrange("p (j k) -> p j k", j=8)
        for j in range(8):
            dg = DG[j % 2]
            for k in range(31):
                if k % 2 == 0:
                    ts("dve", dg[:, k, :], identb[:], wd[:, j, k:k + 1], None, ALU.mult)
                else:
                    act(dg[:, k, :], identb[:], AF.Copy, scale=wd[:, j, k:k + 1])
            for i, (lo, hi) in enumerate(ctiles):
                b = i if j % 2 == 0 else 3 + i
                b = [0, 1][(3 * j + i) % 2]
                for k in range(31):
                    mm(bank(b, 362), dg[:, k, :], UAT[:, j, lo + k:lo + k + 362], k == 0, k == 30)
                ts("dve", AT[:, j, lo:hi], bank(b, 362), pfv(l, "bdwa", 8)[:, j:j + 1], None, ALU.add)
                sq, ab = SQc[i % 2], ABc[i % 2]
                act(sq, AT[:, j, lo:hi], AF.Square)
                cp("act", ab, AT[:, j, lo:hi])
                mm(bank(2 + i, 362), ones[:], sq, j == 0, j == 7)
                mm(bank(5 + i, 362), ones[:], ab, j == 0, j == 7)
        TMPL = view(0, F32, [128, 1088])
        for i, (lo, hi) in enumerate(ctiles):
            ts("dve", BC1[:, lo:hi], bank(5 + i, 362), 1.0 / 1024, None, ALU.mult)
            tt("dve", TMPL[:, lo:hi], BC1[:, lo:hi], BC1[:, lo:hi], ALU.mult)
            stt("dve", BC0[:, lo:hi], bank(2 + i, 362), 1.0 / 1024, TMPL[:, lo:hi], ALU.mult, ALU.subtract)
            ts("dve", BC0[:, lo:hi], BC0[:, lo:hi], EPS, None, ALU.add)
            act(BC0[:, lo:hi], BC0[:, lo:hi], AF.Sqrt)
            recip(BC0[:, lo:hi], BC0[:, lo:hi])
        for j in range(8):
            tt("dve", AT[:, j, :], AT[:, j, :], BC1[:, 0:1086], ALU.subtract)
            tt("dve", AT[:, j, :], AT[:, j, :], BC0[:, 0:1086], ALU.mult)
            ts("dve", AT[:, j, :], AT[:, j, :], pfv(l, "lng", 8)[:, j:j + 1], pfv(l, "lnb", 8)[:, j:j + 1], ALU.mult, ALU.add)
            act(SAT[:, j, 0:TP], AT[:, j, 0:TP], AF.Silu)
            act(SAT[:, j, TP:T], AT[:, j, 1054:1086], AF.Silu)
        if l == 0:
            dump("sat", SAT.rearrange("p a b -> p (a b)"), BF16, 8 * T)
            dump("mt", MT.rearrange("p a b -> p (a b)"), BF16, 8 * T)
        dma("sp", d_sasp[:, :], SAT.rearrange("p a b -> p (a b)"))
        dma("sp", d_msp[:, :], MT.rearrange("p a b -> p (a b)"))

        ck(4)
        QF = [view(0, BF16, [128, 5, TP]), view(10240, BF16, [128, 5, TP])]
        QH1 = view(20480, BF16, [128, T])
        QH = [QH1, QH1]
        ACCS = view(22592, F32, [128, 512])
        PT = [view(24704 + 1024 * i, BF16, [128, 512]) for i in range(4)]
        ONORM = view(28800, BF16, [128, 4, 512])
        RS = view(32896, F32, [128, 8])
        KT = view(33792, BF16, [128, 5, 4096])
        VV = view(74752, BF16, [128, 32, 512])
        ONT = view(141312, BF16, [128, 4, 512])
        QS = view(145408, BF16, [128, 5, 512])
        RT1 = view(160320, F32, [128, NT])
        RT2 = view(161728, F32, [128, NT])
        dma("pool", QF[0][64:128, 4, :], d_qB[:, :])
        dma("pool", QF[1][64:128, 4, :], d_qB[:, :])
        for g in range(4):
            r0 = 384 * g
            dma("sp", KT[:, 0:3, 1024 * g:1024 * g + 1024], d_gkvb[0][r0:r0 + 384, :].rearrange("(c p) k -> p c k", p=128))
            dma("sp", KT[:, 3:5, 1024 * g:1024 * g + 1024], d_gkvb[1][r0:r0 + 256, :].rearrange("(c p) k -> p c k", p=128))
            dma("sp", VV[:, 8 * g:8 * g + 2, :],
                d_gkvb[1][r0 + 256:r0 + 384, :].rearrange("r (x d) -> (r x) d", d=512).rearrange("(t p) d -> p t d", p=128))
            dma("sp", VV[:, 8 * g + 2:8 * g + 8, :],
                d_gkvb[2][r0:r0 + 384, :].rearrange("r (x d) -> (r x) d", d=512).rearrange("(t p) d -> p t d", p=128))
        UB = 7
        SUMS = bank(6, 8, 0)
        onesf = small[:, 40:41]
        K.rec("dve", lambda e: e.memset(onesf, 1.0), [], [onesf])

        def units(h):
            wb = wget(f"HD{l}")
            hp = h % 2
            wqN = wb[:, 0:512].rearrange("p (k m) -> p k m", k=4)
            wqR = wb[:, 512:768].rearrange("p (k m) -> p k m", k=4)
            wqS = wb[:, 768:1024].rearrange("p (k m) -> p k m", k=4)
            wuk = wb[:, 1024:1536]
            wuv = wb[:, 1536:2048].rearrange("p (k m) -> p k m", k=4)
            us = []
            for i, (lo, hi) in enumerate(tiles3):
                def u1(lo=lo, hi=hi):
                    b = bank(UB, NT)
                    for k in range(4):
                        mm(b, wqN[:, k, :], QLAT[:, k, lo:hi], k == 0, k == 3)
                    cp("act", QH[hp][:, lo:hi], b)
                us.append(u1)
                for rc in range(4):
                    def u2(lo=lo, hi=hi, rc=rc, i=i):
                        b = bank(UB, NT)
                        mm(b, wuk[:, 128 * rc:128 * rc + 128], QH[hp][:, lo:hi], True, True)
                        pe_ = min(hi, TP)
                        cp("act", QF[hp][:, rc, lo:pe_], b[:, 0:pe_ - lo])
                        if i == 2:
                            cp(os.environ.get("DBG_QSE", "dve"), QS[:, rc, 32 * h:32 * h + 32], b[:, 320:352])
                    us.append(u2)

                def u3a(lo=lo, hi=hi):
                    b = bank(UB, NT)[0:64, :]
                    for k in range(4):
                        mm(b, wqR[:, k, :], QLAT[:, k, lo:hi], k == 0, k == 3)
                    tt("dve", RT1[0:64, :], b, ropeF[:, 0, lo:hi], ALU.mult)
                us.append(u3a)

                def u3b(lo=lo, hi=hi, i=i):
                    b = bank(UB, NT)[0:64, :]
                    for k in range(4):
                        mm(b, wqS[:, k, :], QLAT[:, k, lo:hi], k == 0, k == 3)
                    tt("dve", RT2[0:64, :], b, ropeF[:, 1, lo:hi], ALU.mult)
                    pe_ = min(hi, TP)
                    tt("dve", QF[hp][0:64, 4, lo:pe_], RT1[0:64, 0:pe_ - lo], RT2[0:64, 0:pe_ - lo], ALU.add)
                    if i == 2:
                        tt("dve", QS[0:64, 4, 32 * h:32 * h + 32], RT1[0:64, 320:352], RT2[0:64, 320:352], ALU.add)
                us.append(u3b)
            return us, wuv

        def attention(qchunk, ktiles, par, hook):
            n = len(ktiles)

            def pv(kt):
                _, vap, nk = ktiles[kt]
                p_ = PT[kt % 4]
                for s in range(4):
                    mm(bank(2 + s), p_[0:nk, 128 * s:128 * s + 128], vap, kt == 0, kt == n - 1)
                if kt == 0:
                    cp("dve", ACCS[0:nk, :], p_[0:nk, :])
                else:
                    tt("dve", ACCS[0:nk, :], ACCS[0:nk, :], p_[0:nk, :], ALU.add)
            for kt in range(n):
                kfn, _, nk = ktiles[kt]
                sb_ = bank(kt % 2)[0:nk, :]
                for c in range(5):
                    mm(sb_, kfn(c), qchunk(c), c == 0, c == 4)
                if kt >= 1:
                    pv(kt - 1)
                act(PT[kt % 4][0:nk, :], sb_, AF.Exp, scale=ATTN_SCALE)
                hook(kt)
            pv(n - 1)
            for s in range(4):
                mm(SUMS[:, 4 * par + s:4 * par + s + 1], ACCS[:, 128 * s:128 * s + 128], onesf, True, True)

        def tail_evac(par):
            recip(RS[:, 4 * par:4 * par + 4], SUMS[:, 4 * par:4 * par + 4])
            for s in range(4):
                if s % 2 == 0:
                    ts("dve", ONORM[:, s, :], bank(2 + s), RS[:, 4 * par + s:4 * par + s + 1], None, ALU.mult)
                else:
                    act(ONORM[:, s, :], bank(2 + s), AF.Copy, scale=RS[:, 4 * par + s:4 * par + s + 1])

        def tail_tr():
            tb = bank_bf(7, 0, [128, 4, 128])
            for s in range(4):
                for rc in range(4):
                    tr(tb[:, rc, :], ONORM[:, s, 128 * rc:128 * rc + 128], identb[:])
                cpa(ONT[:, :, 128 * s:128 * s + 128], tb)

        def tail_pe(h, qb, wuv):
            tail_tr()
            for half in range(2):
                wbk = bank(7, 256, 256)
                for rc in range(4):
                    mm(wbk, wuv[:, rc, :], ONT[:, rc, 256 * half:256 * half + 256], rc == 0, rc == 3)
                cpa(OT[:, h, 512 * qb + 256 * half:512 * qb + 256 * half + 256], wbk)

        import os
        NH = int(os.environ.get("DBG_NH", "16"))
        NOS = os.environ.get("DBG_NOS", "") != ""
        ck(4.1)
        wstate["depth"] = 2
        us0, wuv_cur = units(0)
        NU = int(os.environ.get("DBG_NU", "99"))
        for u in us0[:NU]:
            u()
        ck(4.2)
        pend = []
        it = 0
        nxt = {}
        for h in range(NH):
            usn, wuv_next = [], None
            hp = h % 2
            for qb in range(2):
                par = it % 2
                it += 1
                q0 = 512 * qb
                ktl = [((lambda c, kt=kt: KT[:, c, 128 * kt:128 * kt + 128]), VV[:, kt, :], 128) for kt in range(32)]
                upos = [0]

                def hook(kt, qb=qb, h=h):
                    if kt == 4 and pend:
                        pend.pop(0)()
                    if kt == 5 and qb == 0 and h < NH - 1:
                        u_, w_ = units(h + 1)
                        usn.extend(u_)
                        nxt["wuv"] = w_
                    tot = qb * 32 + kt
                    want = (tot * len(usn)) // 60 if usn else 0
                    while usn and upos_g[0] < min(want, len(usn)):
                        usn[upos_g[0]]()
                        upos_g[0] += 1
                if qb == 0:
                    upos_g = [0]
                attention(lambda c: QF[hp][:, c, q0:q0 + 512], ktl, par, hook)
                tail_evac(par)
                pend.append(lambda h=h, qb=qb, wuv=wuv_cur: tail_pe(h, qb, wuv))
            while usn and upos_g[0] < len(usn):
                usn[upos_g[0]]()
                upos_g[0] += 1
            wuv_cur = nxt.get("wuv")
        while pend:
            pend.pop(0)()
        if NOS:
            raise _Stop()
        for g in range(4):
            dma("pool", KT[:, 0:4, 1024 * g:1024 * g + 1024],
                d_cacheT[l, 0:512, 1024 * g:1024 * g + 1024].rearrange("(c p) k -> p c k", p=128))
            dma("pool", KT[0:64, 4, 1024 * g:1024 * g + 1024], d_cacheT[l, 512:576, 1024 * g:1024 * g + 1024])
            dma("pool", VV[:, 8 * g:8 * g + 8, :],
                d_cacheV[l, 1024 * g:1024 * g + 1024, :].rearrange("(t p) d -> p t d", p=128))

        def kf_cache(kt):
            return lambda c: (KT[:, c, 128 * kt:128 * kt + 128] if c < 4 else KT[0:64, 4, 128 * kt:128 * kt + 128])
        ktl = [(kf_cache(kt), VV[:, kt, :], 128) for kt in range(32)]
        ktl.append(((lambda c: (KTSN[:, c, :] if c < 4 else KTSN[0:64, 4, :])), VSN[0:32, :], 32))
        par = it % 2
        attention(lambda c: (QS[:, c, :] if c < 4 else QS[0:64, 4, :]), ktl, par, lambda kt: None)
        tail_evac(par)
        tail_tr()
        for j in range(4):
            wb = wget(f"UV{l}").rearrange("p (h k m) -> p h k m", h=4, k=4)
            for hl in range(4):
                h = 4 * j + hl
                wbk = bank(7, 32, 256)
                for rc in range(4):
                    mm(wbk, wb[:, hl, rc, :], ONT[:, rc, 128 * j + 32 * hl:128 * j + 32 * hl + 32], rc == 0, rc == 3)
                cpa(OT[:, h, TP:T], wbk)

        if l == 0:
            dump("ot", OT.rearrange("p a b -> p (a b)"), BF16, 16 * T)
        ck(5)
        wstate["depth"] = NB
        SAT2 = view(33792, BF16, [128, 8, T])
        MT2 = view(50688, BF16, [128, 8, T])
        MERGED = view(67584, BF16, [128, 16, T])
        dma("sp", hT.rearrange("p a b -> p (a b)"), d_hsp[:, :])
        dma("sp", SAT2.rearrange("p a b -> p (a b)"), d_sasp[:, :])
        dma("sp", MT2.rearrange("p a b -> p (a b)"), d_msp[:, :])
        SG = [view(101376, F32, [128, NT]), view(102784, F32, [128, NT])]
        ACC = [view(141312 + 1408 * i, F32, [128, NT]) for i in range(3)]
        T2 = [view(145536, F32, [128, NT]), view(146944, F32, [128, NT])]
        pb_set([0, 1, 2, 3, 4, 5, 6, 7])
        psc = pfv(l, "psc", 16)
        for f in range(16):
            g4 = f // 4
            for pair in range(3):
                cw = wget(f"M4{l}").rearrange("p (k m) -> p k m", m=128)
                c1 = c2 = c3 = cw
                for i, (lo, hi) in enumerate(tiles3):
                    bo, bg = pb(), pb()
                    if pair == 0:
                        for k in range(8):
                            mm(bank(bo, NT), c1[:, k, :], SAT2[:, k, lo:hi], k == 0, k == 7)
                        gwt, gof = c1, 8
                    elif pair == 1:
                        for k in range(2):
                            mm(bank(bo, NT), c2[:, k, :], MT2[:, 2 * g4 + k, lo:hi], k == 0, k == 1)
                        gwt, gof = c2, 2
                    else:
                        for k in range(16):
                            mm(bank(bo, NT), c3[:, k, :], OT[:, k, lo:hi], k == 0, k == 15)
                        gwt, gof = c3, 16
                    for k in range(16):
                        mm(bank(bg, NT), gwt[:, gof + k, :], hT[:, k, lo:hi], k == 0, k == 15)
                    sg = SG[(pair * 3 + i) % 2]
                    act(sg, bank(bg, NT), AF.Sigmoid)
                    if pair == 0:
                        tt("dve", ACC[i], bank(bo, NT), sg, ALU.mult)
                    elif pair == 1:
                        t2 = T2[i % 2]
                        stt("dve", t2, bank(bo, NT), psc[:, f:f + 1], sg, ALU.mult, ALU.mult)
                        tt("dve", ACC[i], ACC[i], t2, ALU.add)
                    else:
                        t2 = T2[i % 2]
                        tt("dve", t2, bank(bo, NT), sg, ALU.mult)
                        tt("dve", MERGED[:, f, lo:hi], ACC[i], t2, ALU.add)

        if l == 0:
            dump("merged", MERGED.rearrange("p a b -> p (a b)"), BF16, 16 * T)
        ck(6)
        YF = [view(101376, F32, [128, T]), view(105600, F32, [128, T])]
        pb_set([0, 1, 2, 3, 4])
        for n in range(8):
            wb = w3(wget(f"OUT{l}"), 16, 256)
            for m in range(2):
                f = 2 * n + m
                yf = YF[f % 2]
                for i, (lo, hi) in enumerate(tiles3):
                    b = pb()
                    for k in range(16):
                        mm(bank(b, NT), wb[:, k, 128 * m:128 * m + 128], MERGED[:, k, lo:hi], k == 0, k == 15)
                    yproj_evac(b, f, lo, hi, i, yf, f == 0, f == 15)
                dma("sp", ys[f][:, :], yf)
        residual_pass(l, 0)
        if l == 0:
            dump("xnew", XNEW.rearrange("p a b -> p (a b)"), F32, 16 * T)
        prenorm(l, 1, hT)

        ck(7)
        ACTT = view(33792, BF16, [128, NFF, T])
        UPG = [view(126720, F32, [128, 1060]), view(130960, F32, [128, 1060])]
        UPV = [view(135200, F32, [128, T]), view(139424, F32, [128, T])]
        CV = view(143648, F32, [128, 1060])
        SFFN = view(160320, F32, [128, 2, NFF])
        PG01 = view(160672, F32, [128, 2, NFF])
        PV01 = view(161024, F32, [128, 2, NFF])
        BH3 = view(161376, F32, [128, 2, NFF])
        STL = view(161728, F32, [128, 2, NFF])
        H3 = view(162080, F32, [128, 2, NFF])
        GH3B = view(147888, F32, [128, 4, 88])
        C01 = view(149296, F32, [128, 2, NFF])
        TQ = view(149648, F32, [128, NFF])
        dma("sp", SFFN, d_sffn[l])
        for u in UPG:
            K.rec("dve", lambda e, u=u: e.memset(u[:, 0:2], 0.0), [], [u[:, 0:2]])
        wf = pfv(l, "wdwf", 132).rearrange("p (j k) -> p j k", j=NFF)
        bf_ = pfv(l, "bdwf", NFF)
        pb_set([0, 1, 2, 3, 4, 5, 6, 7])
        for j in range(NFF):
            wb = w3(wget(f"UP{l}"), 16, 256)
            upg, upv = UPG[j % 2], UPV[j % 2]
            cp("dve", upg[:, 1026:1028], SFFN[:, :, j])
            for i, (lo, hi) in enumerate(tiles3):
                bg, bv = pb(), pb()
                for k in range(16):
                    mm(bank(bg, NT), wb[:, k, 0:128], hT[:, k, lo:hi], k == 0, k == 15)
                for k in range(16):
                    mm(bank(bv, NT), wb[:, k, 128:256], hT[:, k, lo:hi], k == 0, k == 15)
                for (a, b_, dlo) in segs(lo, hi, 2):
                    cp("act", upg[:, dlo:dlo + (b_ - a)], bank(bg, NT)[:, a:b_])
                cp("act", upv[:, lo:hi], bank(bv, NT))
            cp("dve", PG01[:, :, j], upg[:, 2:4])
            cp("dve", PV01[:, :, j], upv[:, 0:2])
            cp("dve", BH3[:, :, j], upg[:, 1024:1026])
            cp("dve", STL[:, :, j], upg[:, 1058:1060])
            ts("dve", CV[:, 0:1058], upg[:, 0:1058], wf[:, j, 0:1], bf_[:, j:j + 1], ALU.mult, ALU.add)
            stt("dve", CV[:, 0:1058], upg[:, 1:1059], wf[:, j, 1:2], CV[:, 0:1058], ALU.mult, ALU.add)
            stt("dve", CV[:, 0:1058], upg[:, 2:1060], wf[:, j, 2:3], CV[:, 0:1058], ALU.mult, ALU.add)
            act(CV[:, 0:1058], CV[:, 0:1058], AF.Silu)
            tt("dve", ACTT[:, j, 0:TP], CV[:, 0:TP], upv[:, 0:TP], ALU.mult)
            tt("dve", ACTT[:, j, TP:T], CV[:, 1026:1058], upv[:, TP:T], ALU.mult)
        dma("sp", d_bh3[:, :].rearrange("p (t j) -> p t j", t=2), BH3)
        allgather(d_bh3, d_gh3)
        stf = view(126720, F32, [128, 128])
        for which, src in ((0, BH3), (1, STL)):
            tr(bank(0, 128)[0:88, :], src.rearrange("p t j -> p (t j)"), identf[:])
            cp("dve", stf[0:88, :], bank(0, 128)[0:88, :])
            for t2_ in range(2):
                dma("sp", o_ffn[l, which, t2_, :].rearrange("(j p) -> j p", p=128), stf[44 * t2_:44 * t2_ + 44, :])

        def patch():
            dma("sp", GH3B, d_gh3.ap().rearrange("(r p) f -> p r f", p=128))
            h3f = H3.rearrange("p t j -> p (t j)")
            ts("dve", h3f, GH3B[:, 0, :], hsel[:, 0:1], None, ALU.mult)
            for r in range(1, 4):
                stt("dve", h3f, GH3B[:, r, :], hsel[:, r:r + 1], h3f, ALU.mult, ALU.add)
            w0, w1, w2 = wf[:, :, 0], wf[:, :, 1], wf[:, :, 2]
            h0, h1 = H3[:, 0, :], H3[:, 1, :]
            g0, g1 = PG01[:, 0, :], PG01[:, 1, :]
            c0, c1_ = C01[:, 0, :], C01[:, 1, :]
            tt("dve", c0, w0, h0, ALU.mult)
            tt("dve", TQ, w1, h1, ALU.mult)
            tt("dve", c0, c0, TQ, ALU.add)
            tt("dve", TQ, w2, g0, ALU.mult)
            tt("dve", c0, c0, TQ, ALU.add)
            tt("dve", c0, c0, bf_, ALU.add)
            tt("dve", c1_, w0, h1, ALU.mult)
            tt("dve", TQ, w1, g0, ALU.mult)
            tt("dve", c1_, c1_, TQ, ALU.add)
            tt("dve", TQ, w2, g1, ALU.mult)
            tt("dve", c1_, c1_, TQ, ALU.add)
            tt("dve", c1_, c1_, bf_, ALU.add)
            act(C01, C01, AF.Silu)
            tt("dve", C01, C01, PV01, ALU.mult)
            cp("dve", ACTT[:, :, 0:2].rearrange("p j t -> p t j"), C01)

        ck(8)
        YF = [view(126720, F32, [128, T]), view(130944, F32, [128, T])]
        pb_set([0, 1, 2, 3, 4])
        wstate["depth"] = 2
        for f in range(16):
            hb0 = wget(f"DN{l}").rearrange("p (k m) -> p k m", m=128)
            hb1 = wget(f"DN{l}").rearrange("p (k m) -> p k m", m=128)
            yf = YF[f % 2]
            for i in (1, 2, 0):
                lo, hi = tiles3[i]
                if f == 0 and i == 0:
                    patch()
                b = pb()
                for k in range(NFF):
                    hb = hb0 if k < 22 else hb1
                    mm(bank(b, NT), hb[:, k % 22, :], ACTT[:, k, lo:hi], k == 0, k == NFF - 1)
                yproj_evac(b, f, lo, hi, i, yf, f == 0, f == 15)
            dma("sp", ys[f][:, :], yf)
        wstate["depth"] = NB
        residual_pass(l, 1)
        ck(9)

    def final_out():
        pb_set([0, 1, 2, 3])
        for t9 in range(9):
            rows = 128 if t9 < 8 else 32
            tc0 = t9 * 128
            tok = TOK[t9 % 2]
            for g in range(4):
                b = pb()
                for j in range(4):
                    f = 4 * g + j
                    tr(bank(b, 128, 128 * j)[0:rows, :], XNEW[:, f, tc0:tc0 + rows], identf[:])
                cpa(tok[0:rows, 512 * g:512 * g + 512], bank(b)[0:rows, :])
            dma("sp", o_y[tc0:tc0 + rows, :], tok[0:rows, :])

    try:
        prenorm(0, 0, hT)
        ck(1)
        for l in range(L):
            layer(l)
            if l + 1 < L:
                prenorm(l + 1, 0, hT)
        final_out()
    except _Stop:
        if STAGE == 0:
            final_out()

    K.emit()
    es.close()
    nc._ext_in = set(K.ext_in)
    return nc


def _blk(w, cols):
    K_ = w.shape[0]
    kc = K_ // 128
    out = []
    for c in cols:
        sub = w[:, c]
        out.append(sub.reshape(kc, 128, len(c)).transpose(1, 0, 2).reshape(128, kc * len(c)))
    return np.ascontiguousarray(np.stack(out))


def prep_shared(inp):
    f = np.float32
    sh = {}
    w_in = inp["w_in"]
    ar = np.arange
    sh["wmod"] = np.stack([_blk(inp["w_mod"][l], [ar(256 * n, 256 * n + 256) for n in range(48)]) for l in range(L)])
    sh["wA"] = np.stack([_blk(w_in[l], [np.concatenate([ar(128 * j, 128 * j + 128), ar(1024 + 128 * j, 1024 + 128 * j + 128)])
                                       for j in range(8)]) for l in range(L)])
    sh["wB"] = np.stack([_blk(w_in[l], [ar(OFF_B + 256 * n, OFF_B + 256 * n + 256) for n in range(4)]) for l in range(L)])
    sh["wQ"] = np.stack([_blk(w_in[l], [ar(OFF_Q + 256 * n, OFF_Q + 256 * n + 256) for n in range(2)]) for l in range(L)])
    sh["wKVR"] = np.stack([_blk(w_in[l], [ar(OFF_KV, OFF_G)])[0] for l in range(L)])
    whd = np.zeros((L, 16, 128, 2048), f)
    wuvp = np.zeros((L, 4, 128, 4, 512), f)
    for l in range(L):
        wuq = inp["w_uq"][l].reshape(4, 128, 16, 192)
        wuk = inp["w_uk"][l]
        wuv = inp["w_uv"][l].reshape(4, 128, 16, 128)
        for h in range(16):
            qn = wuq[:, :, h, 0:128].transpose(1, 0, 2).reshape(128, 512)
            qr = wuq[:, :, h, 128:192]
            qs = np.concatenate([qr[..., 32:64], qr[..., 0:32]], axis=-1)
            whd[l, h, :, 0:512] = qn
            whd[l, h, :, 512:768] = qr.transpose(1, 0, 2).reshape(128, 256)
            whd[l, h, :, 768:1024] = qs.transpose(1, 0, 2).reshape(128, 256)
            whd[l, h, :, 1024:1536] = wuk[:, h, :].T
            uv = wuv[:, :, h, :].transpose(1, 0, 2).reshape(128, 512)
            whd[l, h, :, 1536:2048] = uv
            wuvp[l, h // 4, :, h % 4, :] = uv
    sh["wHD"] = whd
    sh["wUV"] = wuvp.reshape(L, 4, 128, 2048)
    wm4 = np.zeros((L, 16, 128, 9472), f)
    for l in range(L):
        pa = inp["w_pa"][l].reshape(8, 128, 16, 128)
        oc = inp["w_oc"][l].reshape(16, 128, 16, 128)
        pool = inp["w_pool"][l].reshape(4, 2, 128, 4, 128)
        wg = w_in[l][:, OFF_G:].reshape(16, 128, 3, 16, 128)
        for fch in range(16):
            o = 0
            blkA = pa[:, :, fch, :].transpose(1, 0, 2).reshape(128, 1024)
            wm4[l, fch, :, 0:1024] = blkA
            wm4[l, fch, :, 1024:3072] = wg[:, :, 0, fch, :].transpose(1, 0, 2).reshape(128, 2048)
            wm4[l, fch, :, 3072:3328] = pool[fch // 4, :, :, fch % 4, :].transpose(1, 0, 2).reshape(128, 256)
            wm4[l, fch, :, 3328:5376] = wg[:, :, 1, fch, :].transpose(1, 0, 2).reshape(128, 2048)
            wm4[l, fch, :, 5376:7424] = oc[:, :, fch, :].transpose(1, 0, 2).reshape(128, 2048)
            wm4[l, fch, :, 7424:9472] = wg[:, :, 2, fch, :].transpose(1, 0, 2).reshape(128, 2048)
    sh["wM4"] = wm4
    sh["wOUT"] = np.stack([_blk(inp["w_out"][l], [ar(256 * n, 256 * n + 256) for n in range(8)]) for l in range(L)])
    sh["wUP"] = np.stack([_blk(inp["w_up"][l], [np.concatenate([ar(128 * j, 128 * j + 128), ar(DFF + 128 * j, DFF + 128 * j + 128)])
                                              for j in range(NFF)]) for l in range(L)])
    wdn = np.zeros((L, 32, 128, 2816), f)
    for l in range(L):
        wd = inp["w_down"][l].reshape(2, 22, 128, 16, 128)
        for fch in range(16):
            for half in range(2):
                wdn[l, 2 * fch + half] = wd[half, :, :, fch, :].transpose(1, 0, 2).reshape(128, 2816)
    sh["wDN"] = wdn
    pfa = np.zeros((128, L * PFL), f)
    for l in range(L):
        def put(name, arr):
            pfa[:, l * PFL + PF[name]:l * PFL + PF[name] + arr.shape[1]] = arr
        fm = lambda v: v.reshape(-1, 128).T
        put("gpm", fm(inp["g_pre_mix"][l])); put("gpo", fm(inp["g_post_mix"][l]))
        put("gpf", fm(inp["g_pre_ffn"][l])); put("gpof", fm(inp["g_post_ffn"][l]))
        put("psc", fm(inp["pool_scale"][l])); put("bdwa", fm(inp["b_dwa"][l]))
        put("lng", fm(inp["ln_a_g"][l])); put("lnb", fm(inp["ln_a_b"][l]))
        put("gq", fm(inp["g_q_lat"][l])); put("bdwf", fm(inp["b_dwf"][l]))
        put("bmod", fm(inp["b_mod"][l]))
        put("wdwa", inp["w_dwa"][l].reshape(31, 8, 128).transpose(2, 1, 0).reshape(128, 248))
        put("wdwf", inp["w_dwf"][l].reshape(3, NFF, 128).transpose(2, 1, 0).reshape(128, 132))
    sh["pf"] = pfa
    sh["gkv"] = np.ascontiguousarray(np.broadcast_to(inp["g_kv_lat"][:, None, :], (L, 128, 512))).astype(f)
    return sh


def prep_core(inp, c):
    f = np.float32
    b, r = c // 4, c % 4
    m = {}
    m["xin"] = np.ascontiguousarray(np.concatenate([inp["x_prompt"][b, 1024 * r:1024 * r + 1024], inp["x_sample"][c]], axis=0))
    cv = np.stack([inp["c_prompt"][b], inp["c_sample"][c]], axis=0)
    m["cT"] = np.ascontiguousarray(cv.reshape(2, 16, 128).transpose(2, 1, 0))
    pos = np.concatenate([1024 * r + np.arange(1024), 4096 + np.arange(32)]).astype(f)
    inv = (np.float32(10000.0) ** (-np.arange(32, dtype=f) / np.float32(32))).astype(f)
    ang = (pos[:, None] * inv[None, :]).astype(f)
    cos, sin = np.cos(ang).astype(f), np.sin(ang).astype(f)
    ropeF = np.zeros((64, 2, T), f)
    ropeF[0:32, 0] = cos.T; ropeF[32:64, 0] = cos.T
    ropeF[0:32, 1] = -sin.T; ropeF[32:64, 1] = sin.T
    m["ropeF"] = ropeF
    ropeT = np.zeros((128, 9, 2, 64), f)
    cc2 = np.concatenate([cos, cos], axis=1)
    ss2 = np.concatenate([-sin, sin], axis=1)
    pad = np.zeros((9 * 128, 64), f)
    pad[:T] = cc2
    ropeT[:, :, 0, :] = pad.reshape(9, 128, 64).transpose(1, 0, 2)
    pad = np.zeros((9 * 128, 64), f)
    pad[:T] = ss2
    ropeT[:, :, 1, :] = pad.reshape(9, 128, 64).transpose(1, 0, 2)
    m["ropeT"] = ropeT
    qch = (1024 * r + np.arange(1024)) // 64
    cidx = np.arange(64)[:, None]
    m["qB"] = np.where(cidx > qch[None, :], -30000.0, 0.0).astype(f)
    m["khot"] = (cidx == qch[None, :]).astype(f)
    hs = np.zeros((128, 4), f)
    if r > 0:
        hs[:, r - 1] = 1.0
    m["hsel"] = hs
    ic = np.zeros((128, 4, 16), f)
    for g, w in enumerate((2, 4, 8, 16)):
        ic[:, g, :] = 1.0 / np.minimum(w, 1024 * r + np.arange(16) + 1).astype(f)
    m["icnt"] = ic
    m["sconv"] = np.ascontiguousarray(inp["state_conv"][:, c].reshape(L, 30, 8, 128).transpose(0, 3, 2, 1))
    m["spool"] = np.ascontiguousarray(inp["state_pool"][:, c].reshape(L, 15, 8, 128).transpose(0, 3, 2, 1))
    m["sffn"] = np.ascontiguousarray(inp["state_ffn"][:, c].reshape(L, 2, NFF, 128).transpose(0, 3, 1, 2))
    m["cacheT"] = np.ascontiguousarray(np.concatenate([inp["cache_ckv"][:, c].transpose(0, 2, 1),
                                                       inp["cache_krope"][:, c].transpose(0, 2, 1)], axis=1))
    m["cacheV"] = np.ascontiguousarray(inp["cache_ckv"][:, c])
    return m


WNAMES = ("wmod", "wA", "wB", "wQ", "wKVR", "wHD", "wUV", "wM4", "wOUT", "wUP", "wDN")


def shard_shared(sh, c):
    m = {"pf": sh["pf"], "gkv": sh["gkv"]}
    for name in WNAMES:
        a = sh[name]
        E = a.shape[-1]
        a2 = a.reshape(L, -1, E)
        for l in range(L):
            m[f"{name}{l}"] = a2[l]
    return m


_NC = None


def kernel(**inputs):
    global _NC
    inp = {k: np.asarray(v, dtype=np.float32) for k, v in inputs.items()}
    if _NC is None:
        _NC = build_program()
    nc = _NC
    sh = prep_shared(inp)
    in_maps = []
    for c in range(8):
        m = prep_core(inp, c)
        m.update(shard_shared(sh, c))
        m = {k: v for k, v in m.items() if k in nc._ext_in}
        in_maps.append(m)
    res = run_bass_kernel_spmd(nc, in_maps, core_ids=list(range(8)))
    R = res.results
    f = np.float32
    y_p = np.zeros((2, 4096, D), f); y_s = np.zeros((8, 32, D), f)
    p_ckv = np.zeros((L, 2, 4096, 512), f); p_kr = np.zeros((L, 2, 4096, 64), f)
    p_conv = np.zeros((L, 2, 30, 1024), f); p_pool = np.zeros((L, 2, 15, 1024), f); p_ffn = np.zeros((L, 2, 2, DFF), f)
    s_ckv = np.zeros((L, 8, 32, 512), f); s_kr = np.zeros((L, 8, 32, 64), f)
    s_conv = np.zeros((L, 8, 30, 1024), f); s_pool = np.zeros((L, 8, 15, 1024), f); s_ffn = np.zeros((L, 8, 2, DFF), f)
    for c in range(8):
        b, r = c // 4, c % 4
        o = R[c]
        y_p[b, 1024 * r:1024 * r + 1024] = o["o_y"][:1024]
        y_s[c] = o["o_y"][1024:]
        p_ckv[:, b, 1024 * r:1024 * r + 1024] = o["o_ckv"][:, :1024]
        s_ckv[:, c] = o["o_ckv"][:, 1024:]
        p_kr[:, b, 1024 * r:1024 * r + 1024] = o["o_kr"][:, :1024]
        s_kr[:, c] = o["o_kr"][:, 1024:]
        if r == 3:
            p_conv[:, b] = o["o_conv"][:, 0]
            p_pool[:, b] = o["o_pool"][:, 0]
            p_ffn[:, b] = o["o_ffn"][:, 0]
        s_conv[:, c] = o["o_conv"][:, 1]
        s_pool[:, c] = o["o_pool"][:, 1]
        s_ffn[:, c] = o["o_ffn"][:, 1]
    return (y_p, y_s, p_ckv, p_kr, p_conv, p_pool, p_ffn, s_ckv, s_kr, s_conv, s_pool, s_ffn)
```

```python
import numpy as np
from contextlib import ExitStack
import concourse.bass as bass
import concourse.mybir as mybir
from concourse.bass_utils import run_bass_kernel_spmd

F32 = mybir.dt.float32
BF16 = mybir.dt.bfloat16
AF = mybir.ActivationFunctionType
ALU = mybir.AluOpType
AX = mybir.AxisListType

L = 2
D = 2048
T = 1056
TP = 1024
TS = 32
NT = 352
DFF = 5632
NFF = 44
EPS = 1e-6
ATTN_SCALE = 192.0 ** -0.5
OFF_A, OFF_B, OFF_Q, OFF_KV, OFF_R, OFF_G = 0, 2048, 3072, 3584, 4096, 4160
PFL = 628
PF = dict(gpm=0, gpo=16, gpf=32, gpof=48, psc=64, bdwa=80, lng=88, lnb=96, gq=104, bdwf=108, bmod=152,
          wdwa=248, wdwf=496)
WCAP = 4096
NB = 3
ARENA = 163840


def dsize(dt):
    s = str(dt)
    if "64" in s:
        return 8
    if "32" in s:
        return 4
    if "16" in s:
        return 2
    return 1


class Op:
    __slots__ = ("eng", "seq", "fn", "waits", "kind", "gid", "qidx")


class KB:
    COMPUTE = ("pe", "act", "dve", "pool")
    EPOCH = 30000
    ND = {"sp": 16, "pool": 8}

    def __init__(self, nc):
        self.nc = nc
        self.ops = {e: [] for e in ("pe", "act", "dve", "pool", "sp")}
        self.state = {}
        self.ext_in = set()
        self.ext_out = set()
        self.known = {e: {} for e in self.ops}
        self.known_d = {e: set() for e in self.ops}
        self.needed = set()
        self.dmaq = {"sp": [], "pool": []}
        self.ccs = []
        self.gcount = 0
        self.allops = []

    def _span(self, ap):
        name = ap.tensor.name
        dsz = dsize(ap.dtype)
        dims = ap.ap
        off = ap.offset
        if name.startswith("sb") or name.startswith("ps"):
            row = dims[0][0]
            col = off % row if row > 0 else off
            ext = 1
            for st, cnt in dims[1:]:
                ext += (cnt - 1) * abs(st)
            page = 256 if name.startswith("sb") else 2048
            lo = col * dsz
            hi = (col + ext) * dsz - 1
        else:
            ext = 1
            for st, cnt in dims:
                ext += (cnt - 1) * abs(st)
            page = 65536
            lo = off * dsz
            hi = (off + ext) * dsz - 1
        return name, lo // page, hi // page

    def rec(self, eng, fn, ins, outs, kind="c"):
        op = Op()
        op.eng = eng
        op.fn = fn
        op.kind = kind
        op.seq = len(self.ops[eng])
        op.gid = self.gcount
        self.gcount += 1
        deps = set()
        for ap in ins:
            name, p0, p1 = self._span(ap)
            if name in self.ext_in:
                continue
            st = self.state.setdefault(name, {})
            isps = name.startswith("ps")
            for p in range(p0, p1 + 1):
                s = st.get(p)
                if s is not None and s[0] is not None:
                    deps.add(s[0])
                if isps and s is not None:
                    for e2, rop in s[1].items():
                        if e2 != eng:
                            deps.add(rop)
        for ap in outs:
            name, p0, p1 = self._span(ap)
            if name in self.ext_out:
                continue
            st = self.state.setdefault(name, {})
            for p in range(p0, p1 + 1):
                s = st.get(p)
                if s is not None:
                    if s[0] is not None:
                        deps.add(s[0])
                    deps.update(s[1].values())
                    deps.update(s[2])
        need_c = {}
        need_d = []
        for d in deps:
            if d.kind == "c":
                if eng == "pe" and d.eng == "pe":
                    continue
                if need_c.get(d.eng, -1) < d.seq:
                    need_c[d.eng] = d.seq
            else:
                need_d.append(d)
        waits = []
        kn = self.known[eng]
        for pe, sq in need_c.items():
            if kn.get(pe, -1) >= sq:
                continue
            kn[pe] = sq
            waits.append(("c", pe, sq))
            self.needed.add((pe, sq))
        kd = self.known_d[eng]
        for d in need_d:
            if d.gid in kd:
                continue
            kd.add(d.gid)
            waits.append(("d", d))
        op.waits = waits
        for ap in ins:
            name, p0, p1 = self._span(ap)
            if name in self.ext_in:
                continue
            st = self.state[name]
            for p in range(p0, p1 + 1):
                s = st.get(p)
                if s is None:
                    s = [None, {}, []]
                    st[p] = s
                if kind == "c":
                    s[1][eng] = op
                else:
                    s[2].append(op)
        for ap in outs:
            name, p0, p1 = self._span(ap)
            if name in self.ext_out:
                continue
            st = self.state[name]
            for p in range(p0, p1 + 1):
                st[p] = [op, {}, []]
        self.ops[eng].append(op)
        if kind == "d":
            op.qidx = len(self.dmaq[eng])
            self.dmaq[eng].append(op)
        elif kind == "cc":
            op.qidx = len(self.ccs)
            self.ccs.append(op)
        return op

    def emit(self):
        nc = self.nc
        with ExitStack() as es:
            signo = {}
            cnt = {}
            for e in self.COMPUTE:
                n = 0
                for op in self.ops[e]:
                    if op.kind == "c" and (e, op.seq) in self.needed:
                        signo[(e, op.seq)] = n
                        n += 1
                cnt[e] = n
            csem = {}
            for e in self.COMPUTE:
                ne = cnt[e] // self.EPOCH + 1
                csem[e] = [es.enter_context(nc.semaphore(f"c_{e}_{i}")) for i in range(ne)]
            dsem = {q: [es.enter_context(nc.semaphore(f"d_{q}_{i}")) for i in range(self.ND[q])] for q in self.dmaq}
            ccsem = [es.enter_context(nc.semaphore(f"cc_{i}")) for i in range(len(self.ccs))]

            def ev(w):
                if w[0] == "c":
                    n = signo[(w[1], w[2])]
                    return csem[w[1]][n // self.EPOCH], n % self.EPOCH + 1
                d = w[1]
                if d.kind == "cc":
                    return ccsem[d.qidx], 1
                nd = self.ND[d.eng]
                return dsem[d.eng][d.qidx % nd], 16 * (d.qidx // nd + 1)

            block = es.enter_context(nc.Block())

            def run(ename, e):
                for op in self.ops[ename]:
                    for w in op.waits:
                        s, v = ev(w)
                        e.wait_ge(s, v)
                    if op.kind == "d":
                        nd = self.ND[ename]
                        if op.qidx >= nd:
                            e.wait_ge(dsem[ename][op.qidx % nd], 16 * (op.qidx // nd))
                        op.fn(e).then_inc(dsem[ename][op.qidx % nd], 16)
                    elif op.kind == "cc":
                        op.fn(e).then_inc(ccsem[op.qidx])
                    else:
                        ins = op.fn(e)
                        key = (ename, op.seq)
                        if key in signo:
                            n = signo[key]
                            ins.then_inc(csem[ename][n // self.EPOCH], 1)
                if ename in self.dmaq:
                    nd = self.ND[ename]
                    q = self.dmaq[ename]
                    for i in range(min(nd, len(q))):
                        last = ((len(q) - 1 - i) // nd) * nd + i
                        e.wait_ge(dsem[ename][i], 16 * (last // nd + 1))
                if ename == "pool":
                    for i in range(len(self.ccs)):
                        e.wait_ge(ccsem[i], 1)

            @block.tensor
            def _(e):
                run("pe", e)

            @block.scalar
            def _(e):
                run("act", e)

            @block.vector
            def _(e):
                run("dve", e)

            @block.gpsimd
            def _(e):
                run("pool", e)

            @block.sync
            def _(e):
                run("sp", e)


class _Stop(Exception):
    pass


STAGE = None


def build_program():
    nc = bass.Bass("TRN2", target_bir_lowering=False)
    K = KB(nc)

    def ck(n):
        if STAGE is not None and n > STAGE:
            raise _Stop()

    import os
    DUMP = os.environ.get("DBG_DUMP", "") != ""

    def dump(name, ap2d, dt, ncols):
        if not DUMP:
            return
        K.ext_out.add("dbg_" + name)
        t_ = nc.dram_tensor("dbg_" + name, [128, ncols], dt, kind="ExternalOutput")
        K.rec("sp", lambda e: e.dma_start(out=t_[:, :], in_=ap2d), [ap2d], [t_[:, :]], kind="d")

    def din(name, shape, dt=F32):
        K.ext_in.add(name)
        return nc.dram_tensor(name, list(shape), dt, kind="ExternalInput")

    def dout(name, shape):
        K.ext_out.add(name)
        return nc.dram_tensor(name, list(shape), F32, kind="ExternalOutput")

    def dscr(name, shape, dt=F32):
        return nc.dram_tensor(name, list(shape), dt)

    d_xin = din("xin", [T, D])
    d_cT = din("cT", [128, 16, 2])
    d_pf = din("pf", [128, L * PFL])
    d_gkv = din("gkv", [L, 128, 512])
    d_ropeF = din("ropeF", [64, 2, T])
    d_ropeT = din("ropeT", [128, 9, 2, 64])
    d_qB = din("qB", [64, TP])
    d_khot = din("khot", [64, TP])
    d_hsel = din("hsel", [128, 4])
    d_icnt = din("icnt", [128, 4, 16])
    d_sconv = din("sconv", [L, 128, 8, 30])
    d_spool = din("spool", [L, 128, 8, 15])
    d_sffn = din("sffn", [L, 128, 2, NFF])
    if STAGE is None or STAGE >= 4:
        d_cacheT = din("cacheT", [L, 576, 4096])
        d_cacheV = din("cacheV", [L, 4096, 512])
    WSPEC = [("wmod", 48, 4096), ("wA", 8, 4096), ("wB", 4, 4096), ("wQ", 2, 4096), ("wKVR", 1, 9216),
             ("wHD", 16, 2048), ("wUV", 4, 2048), ("wM4", 16, 9472), ("wOUT", 8, 4096), ("wUP", NFF, 4096),
             ("wDN", 32, 2816)]
    FIRST = {"wmod": 0, "wKVR": 1, "wA": 2, "wB": 2, "wQ": 2, "wHD": 4, "wUV": 4, "wM4": 5, "wOUT": 6, "wUP": 7, "wDN": 8}
    w_full = {}
    for name, nblk, E in WSPEC:
        w_full[name] = []
        for l in range(L):
            need = STAGE is None or (l == 0 and STAGE >= FIRST[name])
            if need:
                w_full[name].append(din(f"{name}{l}", [nblk * 128, E]))
            else:
                w_full[name].append(nc.dram_tensor(f"d_wf_{name}{l}", [nblk * 128, E], F32))

    def gw(name, l, n):
        return w_full[name][l][n * 128:(n + 1) * 128, :]
    o_y = dout("o_y", [T, D])
    o_ckv = dout("o_ckv", [L, T, 512])
    o_kr = dout("o_kr", [L, T, 64])
    o_conv = dout("o_conv", [L, 2, 30, 1024])
    o_pool = dout("o_pool", [L, 2, 15, 1024])
    o_ffn = dout("o_ffn", [L, 2, 2, DFF])
    xs = [dscr(f"d_xs{f}", [128, T]) for f in range(16)]
    ys = [dscr(f"d_ys{f}", [128, T]) for f in range(16)]
    d_hsp = dscr("d_hsp", [128, 16 * T], BF16)
    d_sasp = dscr("d_sasp", [128, 8 * T], BF16)
    d_msp = dscr("d_msp", [128, 8 * T], BF16)
    d_bkv = [dscr(f"d_bkv{i}", [384, 1024], BF16) for i in range(3)]
    d_gkvb = [dscr(f"d_gkvb{i}", [1536, 1024], BF16) for i in range(3)]
    d_bh = dscr("d_bh", [128, 360])
    d_gh = dscr("d_gh", [512, 360])
    d_bh3 = dscr("d_bh3", [128, 88])
    d_gh3 = dscr("d_gh3", [512, 88])

    es = ExitStack()
    sbt = lambda name, shape, dt: es.enter_context(nc.sbuf_tensor(name, list(shape), dt))
    identf = sbt("sb_identf", [128, 128], F32)
    identb = sbt("sb_identb", [128, 128], BF16)
    ones = sbt("sb_ones", [128, 128], BF16)
    pf = sbt("sb_pf", [128, L * PFL], F32)
    modT = sbt("sb_mod", [128, L, 96, 2], F32)
    mv = sbt("sb_mv", [128, L, 6, 16, 2], F32)
    ropeF = sbt("sb_ropeF", [64, 2, T], F32)
    hsel = sbt("sb_hsel", [128, 4], F32)
    icnt = sbt("sb_icnt", [128, 4, 16], F32)
    cTf = sbt("sb_cTf", [128, 16, 2], F32)
    cTb = sbt("sb_cTb", [128, 16, 2], BF16)
    small = sbt("sb_small", [128, 64], F32)
    wbuf = sbt("sb_wbuf", [128, NB, WCAP], BF16)
    arena = sbt("sb_arena", [128, ARENA // 4], F32)
    ps = es.enter_context(nc.psum_tensor("ps_all", [128, 4096], F32))

    def view(off, dt, shape):
        dsz = dsize(dt)
        n = int(np.prod(shape[1:]))
        nbytes = n * dsz
        assert off % 4 == 0 and nbytes % 4 == 0 and off + nbytes <= ARENA, (off, shape)
        a = arena[:, off // 4:(off + nbytes) // 4]
        if dt == BF16:
            a = a.bitcast(BF16)
        if len(shape) == 3:
            a = a.rearrange("p (a b) -> p a b", a=shape[1])
        elif len(shape) == 4:
            a = a.rearrange("p (a b c) -> p a b c", a=shape[1], b=shape[2])
        if shape[0] != 128:
            a = a[0:shape[0]]
        return a

    def bank(b, n=512, lo=0):
        return ps[:, 512 * b + lo:512 * b + lo + n]

    def bank_bf(b, lo_bytes, shape):
        n = int(np.prod(shape[1:]))
        a = ps[:, 512 * b + lo_bytes // 4:512 * b + lo_bytes // 4 + n // 2].bitcast(BF16)
        if len(shape) == 3:
            a = a.rearrange("p (a b) -> p a b", a=shape[1])
        return a

    def mm(out, lhsT, rhs, start=True, stop=True):
        K.rec("pe", lambda e: e.matmul(out, lhsT=lhsT, rhs=rhs, start=start, stop=stop), [lhsT, rhs], [out])

    def tr(out, in_, ident):
        K.rec("pe", lambda e: e.transpose(out, in_, ident), [in_, ident], [out])

    def act(out, in_, func, scale=None, bias=None):
        ins = [in_]
        kw = {}
        if scale is not None:
            kw["scale"] = scale
            if not isinstance(scale, float):
                ins.append(scale)
        if bias is not None:
            kw["bias"] = bias
            if not isinstance(bias, float):
                ins.append(bias)
        K.rec("act", lambda e: e.activation(out=out, in_=in_, func=func, **kw), ins, [out])

    def tt(eng, out, in0, in1, op):
        K.rec(eng, lambda e: e.tensor_tensor(out=out, in0=in0, in1=in1, op=op), [in0, in1], [out])

    def ts(eng, out, in0, s1, s2, op0, op1=None):
        ins = [in0] + [s for s in (s1, s2) if s is not None and not isinstance(s, float)]
        if op1 is None:
            K.rec(eng, lambda e: e.tensor_scalar(out=out, in0=in0, scalar1=s1, scalar2=None, op0=op0), ins, [out])
        else:
            K.rec(eng, lambda e: e.tensor_scalar(out=out, in0=in0, scalar1=s1, scalar2=s2, op0=op0, op1=op1), ins, [out])

    def stt(eng, out, in0, scalar, in1, op0, op1):
        ins = [in0, in1] + ([] if isinstance(scalar, float) else [scalar])
        K.rec(eng, lambda e: e.scalar_tensor_tensor(out=out, in0=in0, scalar=scalar, in1=in1, op0=op0, op1=op1), ins, [out])

    def cp(eng, out, in_):
        if eng == "act":
            K.rec("act", lambda e: e.copy(out=out, in_=in_), [in_], [out])
        else:
            K.rec(eng, lambda e: e.tensor_copy(out=out, in_=in_), [in_], [out])

    def recip(out, in_):
        K.rec("dve", lambda e: e.reciprocal(out=out, in_=in_), [in_], [out])

    def dma(q, out, in_):
        return K.rec(q, lambda e: e.dma_start(out=out, in_=in_), [in_], [out], kind="d")

    def allgather(src, dst, groups=((0, 1, 2, 3), (4, 5, 6, 7))):
        assert src.ap().nbytes() if False else True
        K.rec("pool", lambda e: e.collective_compute("AllGather", ALU.bypass, replica_groups=[list(g) for g in groups],
                                                     ins=[src.ap().opt()], outs=[dst.ap().opt()]),
              [src.ap()], [dst.ap()], kind="cc")

    cpi = [0]

    def cpa(out, in_):
        cpi[0] += 1
        cp("act" if cpi[0] % 2 else "dve", out, in_)

    def rsqrt_to(dst, src, scale):
        ts("dve", dst, src, scale, EPS, ALU.mult, ALU.add)
        act(dst, dst, AF.Sqrt)
        recip(dst, dst)

    plan = []
    for n in range(16):
        plan.append(("mod0a", gw("wmod", 0, n), 4096))
    for l in range(L):
        for j in range(8):
            plan.append((f"A{l}", gw("wA", l, j), 4096))
        for j in range(4):
            plan.append((f"B{l}", gw("wB", l, j), 4096))
        for j in range(2):
            plan.append((f"Q{l}", gw("wQ", l, j), 4096))
        if l == 0:
            for n in range(16, 48):
                plan.append(("mod0b", gw("wmod", 0, n), 4096))
        if l == 0 and STAGE is None:
            for n in range(48):
                plan.append(("mod1", gw("wmod", 1, n), 4096))
        for h in range(16):
            plan.append((f"HD{l}", gw("wHD", l, h), 2048))
        for j in range(4):
            plan.append((f"UV{l}", gw("wUV", l, j), 2048))
        for f in range(16):
            plan.append((f"M4{l}", gw("wM4", l, f)[:, 0:3072], 3072))
            plan.append((f"M4{l}", gw("wM4", l, f)[:, 3072:5376], 2304))
            plan.append((f"M4{l}", gw("wM4", l, f)[:, 5376:9472], 4096))
        for j in range(8):
            plan.append((f"OUT{l}", gw("wOUT", l, j), 4096))
        for j in range(NFF):
            plan.append((f"UP{l}", gw("wUP", l, j), 4096))
        for j in range(32):
            plan.append((f"DN{l}", gw("wDN", l, j), 2816))
    wstate = {"issued": 0, "next": 0}

    def wget(tag):
        n = wstate["next"]
        assert plan[n][0] == tag, (plan[n][0], tag, n)
        while wstate["issued"] < min(len(plan), n + wstate.get("depth", NB)):
            m = wstate["issued"]
            _, src, E = plan[m]
            dma("pool", wbuf[:, m % NB, 0:E], src)
            wstate["issued"] += 1
        wstate["next"] += 1
        return wbuf[:, n % NB, 0:plan[n][2]]

    def w3(wb, k, m):
        return wb.rearrange("p (k m) -> p k m", k=k)

    pbs = {"list": list(range(8)), "i": 0}

    def pb_set(lst):
        pbs["list"] = list(lst)
        pbs["i"] = 0

    def pb():
        b = pbs["list"][pbs["i"] % len(pbs["list"])]
        pbs["i"] += 1
        return b

    tiles3 = [(0, 352), (352, 704), (704, 1056)]

    def segs(lo, hi, H):
        out = []
        if lo < TP:
            e = min(hi, TP)
            out.append((0, e - lo, H + lo))
        if hi > TP:
            s = max(lo, TP)
            out.append((s - lo, hi - lo, s + 2 * H))
        return out

    def pfv(l, name, n):
        o = l * PFL + PF[name]
        return pf[:, o:o + n]

    A0 = 0
    hT = view(A0, BF16, [128, 16, T])
    XNEW = view(33792, F32, [128, 16, T])
    QLAT = view(150528, BF16, [128, 4, T])
    KTSN = view(158976, BF16, [128, 5, 32])
    VSN = view(159296, BF16, [128, 512])
    OT = view(107520, BF16, [128, 16, T])
    BC0 = view(140864, F32, [128, 1088])
    BC1 = view(145216, F32, [128, 1088])
    SM2 = view(160320, F32, [128, 880])

    K.rec("pool", lambda e: e.memset(identf[:], 0.0), [], [identf[:]])
    K.rec("pool", lambda e: e.affine_select(out=identf[:], in_=identf[:], pattern=[[-1, 128]], compare_op=ALU.not_equal,
                                            fill=1.0, base=0, channel_multiplier=1), [identf[:]], [identf[:]])
    cp("dve", identb[:], identf[:])
    K.rec("dve", lambda e: e.memset(ones[:], 1.0), [], [ones[:]])
    dma("sp", pf[:], d_pf[:, :])
    dma("sp", cTf[:], d_cT[:, :, :])
    dma("sp", ropeF[:], d_ropeF[:, :, :])
    dma("sp", hsel[:], d_hsel[:, :])
    dma("sp", icnt[:], d_icnt[:, :, :])
    act(cTb[:], cTf[:], AF.Silu)

    def mod_compute(l, n0, n1, tag):
        bk = 7
        for n in range(n0, n1):
            wb = w3(wget(tag), 16, 256)
            for m in range(2):
                ch = 2 * n + m
                for k in range(16):
                    mm(bank(bk, 2, 2 * ch), wb[:, k, 128 * m:128 * m + 128], cTb[:, k, :], k == 0, k == 15)
        psm = bank(bk, 192).rearrange("p (c s) -> p c s", s=2)
        c0, c1 = 2 * n0, 2 * n1
        for s in range(2):
            tt("dve", modT[:, l, c0:c1, s], psm[:, c0:c1, s], pfv(l, "bmod", 96)[:, c0:c1], ALU.add)
        for s in range(2):
            tmp = small[:, 0:16]
            if c0 == 0:
                ts("dve", tmp, modT[:, l, 16:32, s], 1.0, None, ALU.add)
                tt("dve", mv[:, l, 0, :, s], tmp, pfv(l, "gpm", 16), ALU.mult)
                cp("dve", mv[:, l, 1, :, s], modT[:, l, 0:16, s])
            if c1 == 96:
                tt("dve", mv[:, l, 2, :, s], modT[:, l, 32:48, s], pfv(l, "gpo", 16), ALU.mult)
                ts("dve", tmp, modT[:, l, 64:80, s], 1.0, None, ALU.add)
                tt("dve", mv[:, l, 3, :, s], tmp, pfv(l, "gpf", 16), ALU.mult)
                cp("dve", mv[:, l, 4, :, s], modT[:, l, 48:64, s])
                tt("dve", mv[:, l, 5, :, s], modT[:, l, 80:96, s], pfv(l, "gpof", 16), ALU.mult)


    TOK = [view(101376, F32, [128, D]), view(109568, F32, [128, D])]
    pb_set([0, 1, 2, 3])
    for t9 in range(9):
        rows = 128 if t9 < 8 else 32
        tc0 = t9 * 128
        tok = TOK[t9 % 2]
        dma("sp", tok[0:rows, :], d_xin[tc0:tc0 + rows, :])
        for g in range(4):
            b = pb()
            for j in range(4):
                f = 4 * g + j
                tr(bank(b, rows, 128 * j), tok[0:rows, 128 * f:128 * f + 128], identf[0:rows, 0:rows])
            src = bank(b).rearrange("p (j t) -> p j t", j=4)[:, :, 0:rows]
            cpa(XNEW[:, 4 * g:4 * g + 4, tc0:tc0 + rows], src)

    mod_compute(0, 0, 16, "mod0a")

    def prenorm(l, sub, dst):
        SQ = [view(101376, BF16, [128, T]), view(103488, BF16, [128, T])]
        TMPF = [view(105600, F32, [128, T]), view(109824, F32, [128, T])]
        ssb = [5, 6, 7]
        for f in range(16):
            sq = SQ[f % 2]
            act(sq, XNEW[:, f, :], AF.Square)
            for i, (lo, hi) in enumerate(tiles3):
                mm(bank(ssb[i], NT), ones[:], sq[:, lo:hi], f == 0, f == 15)
            dma("sp", xs[f][:, :], XNEW[:, f, :])
        for i, (lo, hi) in enumerate(tiles3):
            rsqrt_to(BC0[:, lo:hi], bank(ssb[i], NT), 1.0 / D)
        ia, ib = (0, 1) if sub == 0 else (3, 4)
        for f in range(16):
            tmp = TMPF[f % 2]
            tt("dve", tmp, XNEW[:, f, :], BC0[:, 0:T], ALU.mult)
            ts("dve", dst[:, f, 0:TP], tmp[:, 0:TP], mv[:, l, ia, f, 0:1], mv[:, l, ib, f, 0:1], ALU.mult, ALU.add)
            ts("dve", dst[:, f, TP:T], tmp[:, TP:T], mv[:, l, ia, f, 1:2], mv[:, l, ib, f, 1:2], ALU.mult, ALU.add)

    def tail_out(src_fn, nch, w, dst):
        st = view(101376, F32, [128, 1024])
        for j in range(nch):
            tr(bank(2 + j // 4, 128, 128 * (j % 4))[0:w, :], src_fn(j), identf[:])
        for half in range(nch // 4):
            cpa(st[0:w, 512 * half:512 * half + 512], bank(2 + half)[0:w, :])
        dma("sp", dst, st[0:w, 0:nch * 128])

    def residual_pass(l, sub):
        YB = [view(101376, F32, [128, T]), view(105600, F32, [128, T])]
        TMPF = [view(109824, F32, [128, T]), view(114048, F32, [128, T])]
        for i, (lo, hi) in enumerate(tiles3):
            rsqrt_to(BC0[:, lo:hi], bank(5 + i, NT), 1.0 / D)
        ig = 2 if sub == 0 else 5
        for f in range(16):
            yb = YB[f % 2]
            tmp = TMPF[f % 2]
            dma("sp", yb, ys[f][:, :])
            dma("sp", XNEW[:, f, :], xs[f][:, :])
            tt("dve", tmp, yb, BC0[:, 0:T], ALU.mult)
            stt("dve", XNEW[:, f, 0:TP], tmp[:, 0:TP], mv[:, l, ig, f, 0:1], XNEW[:, f, 0:TP], ALU.mult, ALU.add)
            stt("dve", XNEW[:, f, TP:T], tmp[:, TP:T], mv[:, l, ig, f, 1:2], XNEW[:, f, TP:T], ALU.mult, ALU.add)

    def yproj_evac(b, f, lo, hi, i, YF, first, last):
        SQt = [view(141312, BF16, [128, NT]), view(142016, BF16, [128, NT])]
        sq = SQt[(f * 3 + i) % 2]
        cp("act", YF[:, lo:hi], bank(b, NT))
        act(sq, bank(b, NT), AF.Square)
        mm(bank(5 + i, NT), ones[:], sq, first, last)

    def layer(l):
        UAT = view(33792, BF16, [128, 8, 1116])
        ZBT = view(51648, BF16, [128, 8, 1086])
        KTST = view(85920, BF16, [128, 5, T])
        VST = view(96480, BF16, [128, 9, 512])
        WKVR = view(105696, BF16, [128, 16, 576])
        ZQST = view(105696, F32, [128, 4, T])
        UATAIL = view(124128, F32, [128, 8, 60])
        ZBTAIL = view(126048, F32, [128, 8, 30])
        GHB = view(127008, F32, [128, 4, 360])
        ROPET = view(132768, F32, [128, 9, 2, 64])
        GKV = view(137376, F32, [128, 512])
        SCONV = view(139424, F32, [128, 8, 30])
        SPOOL = view(140384, F32, [128, 8, 15])
        JUNK = [view(69024, F32, [128, 512]), view(71072, F32, [128, 512])]
        CKVF = [view(73120, F32, [128, 512]), view(75168, F32, [128, 512])]
        KRU = [view(77216, F32, [128, 64]), view(77472, F32, [128, 64])]
        KRV = [view(77728, F32, [128, 64]), view(77984, F32, [128, 64])]
        KRF = [view(78240, F32, [128, 64]), view(78496, F32, [128, 64])]
        KRB = [view(78752, BF16, [128, 64]), view(78880, BF16, [128, 64])]
        SGT = [view(79008, F32, [128, NT]), view(80416, F32, [128, NT])]
        SQT = [view(81824, BF16, [128, NT]), view(82528, BF16, [128, NT])]

        dma("pool", WKVR.rearrange("p k m -> p (k m)"), gw("wKVR", l, 0))
        dma("sp", ROPET, d_ropeT[:, :, :, :])
        dma("sp", GKV, d_gkv[l])
        dma("pool", KTST[64:128, 4, 0:TP], d_khot[:, :])
        dma("sp", SCONV, d_sconv[l])
        dma("sp", SPOOL, d_spool[l])
        for t9 in range(9):
            rows = 128 if t9 < 8 else 32
            tc0 = t9 * 128
            i2 = t9 % 2
            bx, by = (0, 1) if i2 == 0 else (2, 3)
            psx = bank(bx)[0:rows, :]
            psy = bank(by, 64)[0:rows, :]
            for k in range(16):
                mm(psx, hT[:, k, tc0:tc0 + rows], WKVR[:, k, 0:512], k == 0, k == 15)
            for k in range(16):
                mm(psy, hT[:, k, tc0:tc0 + rows], WKVR[:, k, 512:576], k == 0, k == 15)
            junk = JUNK[i2][0:rows]
            ssk = small[0:rows, 16 + t9:17 + t9]
            act(junk, psx, AF.Square)
            K.rec("dve", lambda e, ssk=ssk, junk=junk: e.reduce_sum(out=ssk, in_=junk, axis=AX.X), [junk], [ssk])
            rsqrt_to(ssk, ssk, 1.0 / 512)
            ckvf = CKVF[i2][0:rows]
            stt("dve", ckvf, psx, ssk, GKV[0:rows], ALU.mult, ALU.mult)
            dma("sp", o_ckv[l, tc0:tc0 + rows, :], ckvf)
            cp("act", VST[0:rows, t9, :], ckvf)
            pst = bank_bf(4 + i2, 0, [128, 4, 128])
            for c in range(4):
                tr(pst[:, c, 0:rows], VST[0:rows, t9, 128 * c:128 * c + 128], identb[0:rows, 0:rows])
            cpa(KTST[:, 0:4, tc0:tc0 + rows], pst[:, :, 0:rows])
            kru, krv, krf, krb = KRU[i2][0:rows], KRV[i2][0:rows], KRF[i2][0:rows], KRB[i2][0:rows]
            tt("dve", kru, psy, ROPET[0:rows, t9, 0, :], ALU.mult)
            tt("dve", krv[:, 0:32], psy[:, 32:64], ROPET[0:rows, t9, 1, 0:32], ALU.mult)
            tt("dve", krv[:, 32:64], psy[:, 0:32], ROPET[0:rows, t9, 1, 32:64], ALU.mult)
            tt("dve", krf, kru, krv, ALU.add)
            dma("sp", o_kr[l, tc0:tc0 + rows, :], krf)
            cp("act", krb, krf)
            pst2 = bank_bf(6 + i2, 0, [128, 128])
            tr(pst2[0:64, 0:rows], krb, identb[0:rows, 0:rows])
            cpa(KTST[0:64, 4, tc0:tc0 + rows], pst2[0:64, 0:rows])
        ck(1.5)
        dma("sp", d_bkv[0][0:384, :].rearrange("(c p) k -> p c k", p=128), KTST[:, 0:3, 0:TP])
        dma("sp", d_bkv[1][0:256, :].rearrange("(c p) k -> p c k", p=128), KTST[:, 3:5, 0:TP])
        dma("sp", d_bkv[1][256:384, :].rearrange("r (x d) -> (r x) d", d=512).rearrange("(t p) d -> p t d", p=128),
            VST[:, 0:2, :])
        dma("sp", d_bkv[2][0:384, :].rearrange("r (x d) -> (r x) d", d=512).rearrange("(t p) d -> p t d", p=128),
            VST[:, 2:8, :])
        cp("dve", KTSN, KTST[:, :, TP:T])
        cp("dve", VSN[0:32, :], VST[0:32, 8, :])
        ck(1.7)
        for i3 in range(3):
            allgather(d_bkv[i3], d_gkvb[i3])

        ck(2)
        pb_set([0, 1, 2, 3, 4, 5, 6, 7])
        for j in range(8):
            wb = w3(wget(f"A{l}"), 16, 256)
            for i, (lo, hi) in enumerate(tiles3):
                bu, bg = pb(), pb()
                for k in range(16):
                    mm(bank(bu, NT), wb[:, k, 0:128], hT[:, k, lo:hi], k == 0, k == 15)
                for k in range(16):
                    mm(bank(bg, NT), wb[:, k, 128:256], hT[:, k, lo:hi], k == 0, k == 15)
                sg = SGT[i % 2]
                act(sg, bank(bg, NT), AF.Sigmoid)
                for (a, b_, dlo) in segs(lo, hi, 30):
                    tt("dve", UAT[:, j, dlo:dlo + (b_ - a)], bank(bu, NT)[:, a:b_], sg[:, a:b_], ALU.mult)
                if i == 2:
                    tt("dve", UATAIL[:, j, 0:30], bank(bu, NT)[:, 290:320], sg[:, 290:320], ALU.mult)
                    tt("dve", UATAIL[:, j, 30:60], bank(bu, NT)[:, 322:352], sg[:, 322:352], ALU.mult)
        for n in range(4):
            wb = w3(wget(f"B{l}"), 16, 256)
            for m in range(2):
                ch = 2 * n + m
                for i, (lo, hi) in enumerate(tiles3):
                    b = pb()
                    for k in range(16):
                        mm(bank(b, NT), wb[:, k, 128 * m:128 * m + 128], hT[:, k, lo:hi], k == 0, k == 15)
                    for (a, b_, dlo) in segs(lo, hi, 15):
                        cpa(ZBT[:, ch, dlo:dlo + (b_ - a)], bank(b, NT)[:, a:b_])
                    if i == 2:
                        cp("dve", ZBTAIL[:, ch, 0:15], bank(b, NT)[:, 305:320])
                        cp("dve", ZBTAIL[:, ch, 15:30], bank(b, NT)[:, 337:352])
        dma("sp", d_bh[:, 0:240].rearrange("p (j t) -> p j t", j=8), UATAIL[:, :, 0:30])
        dma("sp", d_bh[:, 240:360].rearrange("p (j t) -> p j t", j=8), ZBTAIL[:, :, 0:15])
        allgather(d_bh, d_gh)
        tail_out(lambda j: UATAIL[:, j, 0:30], 8, 30, o_conv[l, 0])
        tail_out(lambda j: UATAIL[:, j, 30:60], 8, 30, o_conv[l, 1])
        tail_out(lambda j: ZBTAIL[:, j, 0:15], 8, 15, o_pool[l, 0])
        tail_out(lambda j: ZBTAIL[:, j, 15:30], 8, 15, o_pool[l, 1])

        pb_set([0, 1, 2, 3, 4])
        for n in range(2):
            wb = w3(wget(f"Q{l}"), 16, 256)
            for m in range(2):
                ch = 2 * n + m
                for i, (lo, hi) in enumerate(tiles3):
                    b = pb()
                    for k in range(16):
                        mm(bank(b, NT), wb[:, k, 128 * m:128 * m + 128], hT[:, k, lo:hi], k == 0, k == 15)
                    cp("act", ZQST[:, ch, lo:hi], bank(b, NT))
                    sq = SQT[i % 2]
                    act(sq, bank(b, NT), AF.Square)
                    mm(bank(5 + i, NT), ones[:], sq, ch == 0, ch == 3)
        for i, (lo, hi) in enumerate(tiles3):
            rsqrt_to(BC0[:, lo:hi], bank(5 + i, NT), 1.0 / 512)
        for ch in range(4):
            stt("dve", QLAT[:, ch, :], ZQST[:, ch, :], pfv(l, "gq", 4)[:, ch:ch + 1], BC0[:, 0:T], ALU.mult, ALU.mult)
        dma("sp", d_hsp[:, :], hT.rearrange("p a b -> p (a b)"))
        if l == 0:
            dump("hT", hT.rearrange("p a b -> p (a b)"), BF16, 16 * T)
            dump("qlat", QLAT.rearrange("p a b -> p (a b)"), BF16, 4 * T)
        if l == 0:
            mod_compute(0, 16, 48, "mod0b")
        if l == 0 and STAGE is None:
            mod_compute(1, 0, 48, "mod1")

        ck(3)
        HT = view(69024, F32, [128, 360])
        dma("sp", GHB, d_gh.ap().rearrange("(r p) f -> p r f", p=128))
        ts("dve", HT, GHB[:, 0, :], hsel[:, 0:1], None, ALU.mult)
        for r in range(1, 4):
            stt("dve", HT, GHB[:, r, :], hsel[:, r:r + 1], HT, ALU.mult, ALU.add)
        cp("dve", UAT[:, :, 0:30], HT[:, 0:240].rearrange("p (j t) -> p j t", j=8))
        cp("dve", ZBT[:, :, 0:15], HT[:, 240:360].rearrange("p (j t) -> p j t", j=8))
        cp("dve", UAT[:, :, 1054:1084], SCONV)
        cp("dve", ZBT[:, :, 1039:1054], SPOOL)
        MT = view(120672, BF16, [128, 8, T])
        SAT = view(103776, BF16, [128, 8, T])
        AT = view(69024, F32, [128, 8, 1086])
        P = [view(15872, F32, [128, 2, 1086]), view(24560, F32, [128, 2, 1086])]
        T16 = view(33248, F32, [128, 16])
        for g in range(4):
            src = ZBT[:, 2 * g:2 * g + 2, :]
            w = 2 ** (g + 1)
            cur = src
            for i in range(g + 1):
                st = 2 ** i
                dst = P[i % 2]
                tt("dve", dst[:, :, st:1086], cur[:, :, st:1086], cur[:, :, 0:1086 - st], ALU.add)
                cur = dst
            stt("dve", MT[:, 2 * g:2 * g + 2, 0:TP], cur[:, :, 15:15 + TP], 1.0 / w, src[:, :, 15:15 + TP], ALU.mult, ALU.subtract)
            stt("dve", MT[:, 2 * g:2 * g + 2, TP:T], cur[:, :, 1054:1086], 1.0 / w, src[:, :, 1054:1086], ALU.mult, ALU.subtract)
            for c2 in range(2):
                tt("dve", T16, cur[:, c2, 15:31], icnt[:, g, :], ALU.mult)
                tt("dve", MT[:, 2 * g + c2, 0:16], T16, src[:, c2, 15:31], ALU.subtract)
        DG = [view(0, BF16, [128, 31, 128]), view(7936, BF16, [128, 31, 128])]
        SQc = [view(160320, BF16, [128, 362]), view(161044, BF16, [128, 362])]
        ABc = [view(161768, BF16, [128, 362]), view(162492, BF16, [128, 362])]
        ctiles = [(0, 362), (362, 724), (724, 1086)]
        wd = pfv(l, "wdwa", 248).rearrange("p (j k) -> p j k", j=8)
        for j in range(8):
            dg = DG[j % 2]
            for k in range(31):
                if k % 2 == 0:
                    ts("dve", dg[:, k, :], identb[:], wd[:, j, k:k + 1], None, ALU.mult)
                else:
                    act(dg[:, k, :], identb[:], AF.Copy, scale=wd[:, j, k:k + 1])
            for i, (lo, hi) in enumerate(ctiles):
                b = i if j % 2 == 0 else 3 + i
                b = [0, 1][(3 * j + i) % 2]
                for k in range(31):
                    mm(bank(b, 362), dg[:, k, :], UAT[:, j, lo + k:lo + k + 362], k == 0, k == 30)
                ts("dve", AT[:, j, lo:hi], bank(b, 362), pfv(l, "bdwa", 8)[:, j:j + 1], None, ALU.add)
                sq, ab = SQc[i % 2], ABc[i % 2]
                act(sq, AT[:, j, lo:hi], AF.Square)
                cp("act", ab, AT[:, j, lo:hi])
                mm(bank(2 + i, 362), ones[:], sq, j == 0, j == 7)
                mm(bank(5 + i, 362), ones[:], ab, j == 0, j == 7)
        TMPL = view(0, F32, [128, 1088])
        for i, (lo, hi) in enumerate(ctiles):
            ts("dve", BC1[:, lo:hi], bank(5 + i, 362), 1.0 / 1024, None, ALU.mult)
            tt("dve", TMPL[:, lo:hi], BC1[:, lo:hi], BC1[:, lo:hi], ALU.mult)
            stt("dve", BC0[:, lo:hi], bank(2 + i, 362), 1.0 / 1024, TMPL[:, lo:hi], ALU.mult, ALU.subtract)
            ts("dve", BC0[:, lo:hi], BC0[:, lo:hi], EPS, None, ALU.add)
            act(BC0[:, lo:hi], BC0[:, lo:hi], AF.Sqrt)
            recip(BC0[:, lo:hi], BC0[:, lo:hi])
        for j in range(8):
            tt("dve", AT[:, j, :], AT[:, j, :], BC1[:, 0:1086], ALU.subtract)
            tt("dve", AT[:, j, :], AT[:, j, :], BC0[:, 0:1086], ALU.mult)
            ts("dve", AT[:, j, :], AT[:, j, :], pfv(l, "lng", 8)[:, j:j + 1], pfv(l, "lnb", 8)[:, j:j + 1], ALU.mult, ALU.add)
            act(SAT[:, j, 0:TP], AT[:, j, 0:TP], AF.Silu)
            act(SAT[:, j, TP:T], AT[:, j, 1054:1086], AF.Silu)
        if l == 0:
            dump("sat", SAT.rearrange("p a b -> p (a b)"), BF16, 8 * T)
            dump("mt", MT.rearrange("p a b -> p (a b)"), BF16, 8 * T)
        dma("sp", d_sasp[:, :], SAT.rearrange("p a b -> p (a b)"))
        dma("sp", d_msp[:, :], MT.rearrange("p a b -> p (a b)"))

        ck(4)
        QF = [view(0, BF16, [128, 5, TP]), view(10240, BF16, [128, 5, TP])]
        QH1 = view(20480, BF16, [128, T])
        QH = [QH1, QH1]
        ACCS = view(22592, F32, [128, 512])
        PT = [view(24704 + 1024 * i, BF16, [128, 512]) for i in range(4)]
        ONORM = view(28800, BF16, [128, 4, 512])
        RS = view(32896, F32, [128, 8])
        KT = view(33792, BF16, [128, 5, 4096])
        VV = view(74752, BF16, [128, 32, 512])
        ONT = view(141312, BF16, [128, 4, 512])
        QS = view(145408, BF16, [128, 5, 512])
        RT1 = view(160320, F32, [128, NT])
        RT2 = view(161728, F32, [128, NT])
        dma("pool", QF[0][64:128, 4, :], d_qB[:, :])
        dma("pool", QF[1][64:128, 4, :], d_qB[:, :])
        for g in range(4):
            r0 = 384 * g
            dma("sp", KT[:, 0:3, 1024 * g:1024 * g + 1024], d_gkvb[0][r0:r0 + 384, :].rearrange("(c p) k -> p c k", p=128))
            dma("sp", KT[:, 3:5, 1024 * g:1024 * g + 1024], d_gkvb[1][r0:r0 + 256, :].rearrange("(c p) k -> p c k", p=128))
            dma("sp", VV[:, 8 * g:8 * g + 2, :],
                d_gkvb[1][r0 + 256:r0 + 384, :].rearrange("r (x d) -> (r x) d", d=512).rearrange("(t p) d -> p t d", p=128))
            dma("sp", VV[:, 8 * g + 2:8 * g + 8, :],
                d_gkvb[2][r0:r0 + 384, :].rearrange("r (x d) -> (r x) d", d=512).rearrange("(t p) d -> p t d", p=128))
        UB = 7
        SUMS = bank(6, 8, 0)
        onesf = small[:, 40:41]
        K.rec("dve", lambda e: e.memset(onesf, 1.0), [], [onesf])

        def units(h):
            wb = wget(f"HD{l}")
            hp = h % 2
            wqN = wb[:, 0:512].rearrange("p (k m) -> p k m", k=4)
            wqR = wb[:, 512:768].rearrange("p (k m) -> p k m", k=4)
            wqS = wb[:, 768:1024].rearrange("p (k m) -> p k m", k=4)
            wuk = wb[:, 1024:1536]
            wuv = wb[:, 1536:2048].rearrange("p (k m) -> p k m", k=4)
            us = []
            for i, (lo, hi) in enumerate(tiles3):
                def u1(lo=lo, hi=hi):
                    b = bank(UB, NT)
                    for k in range(4):
                        mm(b, wqN[:, k, :], QLAT[:, k, lo:hi], k == 0, k == 3)
                    cp("act", QH[hp][:, lo:hi], b)
                us.append(u1)
                for rc in range(4):
                    def u2(lo=lo, hi=hi, rc=rc, i=i):
                        b = bank(UB, NT)
                        mm(b, wuk[:, 128 * rc:128 * rc + 128], QH[hp][:, lo:hi], True, True)
                        pe_ = min(hi, TP)
                        cp("act", QF[hp][:, rc, lo:pe_], b[:, 0:pe_ - lo])
                        if i == 2:
                            cp(os.environ.get("DBG_QSE", "dve"), QS[:, rc, 32 * h:32 * h + 32], b[:, 320:352])
                    us.append(u2)

                def u3a(lo=lo, hi=hi):
                    b = bank(UB, NT)[0:64, :]
                    for k in range(4):
                        mm(b, wqR[:, k, :], QLAT[:, k, lo:hi], k == 0, k == 3)
                    tt("dve", RT1[0:64, :], b, ropeF[:, 0, lo:hi], ALU.mult)
                us.append(u3a)

                def u3b(lo=lo, hi=hi, i=i):
                    b = bank(UB, NT)[0:64, :]
                    for k in range(4):
                        mm(b, wqS[:, k, :], QLAT[:, k, lo:hi], k == 0, k == 3)
                    tt("dve", RT2[0:64, :], b, ropeF[:, 1, lo:hi], ALU.mult)
                    pe_ = min(hi, TP)
                    tt("dve", QF[hp][0:64, 4, lo:pe_], RT1[0:64, 0:pe_ - lo], RT2[0:64, 0:pe_ - lo], ALU.add)
                    if i == 2:
                        tt("dve", QS[0:64, 4, 32 * h:32 * h + 32], RT1[0:64, 320:352], RT2[0:64, 320:352], ALU.add)
                us.append(u3b)
            return us, wuv

        def attention(qchunk, ktiles, par, hook):
            n = len(ktiles)

            def pv(kt):
                _, vap, nk = ktiles[kt]
                p_ = PT[kt % 4]
                for s in range(4):
                    mm(bank(2 + s), p_[0:nk, 128 * s:128 * s + 128], vap, kt == 0, kt == n - 1)
                if kt == 0:
                    cp("dve", ACCS[0:nk, :], p_[0:nk, :])
                else:
                    tt("dve", ACCS[0:nk, :], ACCS[0:nk, :], p_[0:nk, :], ALU.add)
            for kt in range(n):
                kfn, _, nk = ktiles[kt]
                sb_ = bank(kt % 2)[0:nk, :]
                for c in range(5):
                    mm(sb_, kfn(c), qchunk(c), c == 0, c == 4)
                if kt >= 1:
                    pv(kt - 1)
                act(PT[kt % 4][0:nk, :], sb_, AF.Exp, scale=ATTN_SCALE)
                hook(kt)
            pv(n - 1)
            for s in range(4):
                mm(SUMS[:, 4 * par + s:4 * par + s + 1], ACCS[:, 128 * s:128 * s + 128], onesf, True, True)

        def tail_evac(par):
            recip(RS[:, 4 * par:4 * par + 4], SUMS[:, 4 * par:4 * par + 4])
            for s in range(4):
                if s % 2 == 0:
                    ts("dve", ONORM[:, s, :], bank(2 + s), RS[:, 4 * par + s:4 * par + s + 1], None, ALU.mult)
                else:
                    act(ONORM[:, s, :], bank(2 + s), AF.Copy, scale=RS[:, 4 * par + s:4 * par + s + 1])

        def tail_tr():
            tb = bank_bf(7, 0, [128, 4, 128])
            for s in range(4):
                for rc in range(4):
                    tr(tb[:, rc, :], ONORM[:, s, 128 * rc:128 * rc + 128], identb[:])
                cpa(ONT[:, :, 128 * s:128 * s + 128], tb)

        def tail_pe(h, qb, wuv):
            tail_tr()
            for half in range(2):
                wbk = bank(7, 256, 256)
                for rc in range(4):
                    mm(wbk, wuv[:, rc, :], ONT[:, rc, 256 * half:256 * half + 256], rc == 0, rc == 3)
                cpa(OT[:, h, 512 * qb + 256 * half:512 * qb + 256 * half + 256], wbk)

        import os
        NH = int(os.environ.get("DBG_NH", "16"))
        NOS = os.environ.get("DBG_NOS", "") != ""
        ck(4.1)
        wstate["depth"] = 2
        us0, wuv_cur = units(0)
        NU = int(os.environ.get("DBG_NU", "99"))
        for u in us0[:NU]:
            u()
        ck(4.2)
        pend = []
        it = 0
        nxt = {}
        for h in range(NH):
            usn, wuv_next = [], None
            hp = h % 2
            for qb in range(2):
                par = it % 2
                it += 1
                q0 = 512 * qb
                ktl = [((lambda c, kt=kt: KT[:, c, 128 * kt:128 * kt + 128]), VV[:, kt, :], 128) for kt in range(32)]
                upos = [0]

                def hook(kt, qb=qb, h=h):
                    if kt == 4 and pend:
                        pend.pop(0)()
                    if kt == 5 and qb == 0 and h < NH - 1:
                        u_, w_ = units(h + 1)
                        usn.extend(u_)
                        nxt["wuv"] = w_
                    tot = qb * 32 + kt
                    want = (tot * len(usn)) // 60 if usn else 0
                    while usn and upos_g[0] < min(want, len(usn)):
                        usn[upos_g[0]]()
                        upos_g[0] += 1
                if qb == 0:
                    upos_g = [0]
                attention(lambda c: QF[hp][:, c, q0:q0 + 512], ktl, par, hook)
                tail_evac(par)
                pend.append(lambda h=h, qb=qb, wuv=wuv_cur: tail_pe(h, qb, wuv))
            while usn and upos_g[0] < len(usn):
                usn[upos_g[0]]()
                upos_g[0] += 1
            wuv_cur = nxt.get("wuv")
        while pend:
            pend.pop(0)()
        if NOS:
            raise _Stop()
        for g in range(4):
            dma("pool", KT[:, 0:4, 1024 * g:1024 * g + 1024],
                d_cacheT[l, 0:512, 1024 * g:1024 * g + 1024].rearrange("(c p) k -> p c k", p=128))
            dma("pool", KT[0:64, 4, 1024 * g:1024 * g + 1024], d_cacheT[l, 512:576, 1024 * g:1024 * g + 1024])
            dma("pool", VV[:, 8 * g:8 * g + 8, :],
                d_cacheV[l, 1024 * g:1024 * g + 1024, :].rearrange("(t p) d -> p t d", p=128))

        def kf_cache(kt):
            return lambda c: (KT[:, c, 128 * kt:128 * kt + 128] if c < 4 else KT[0:64, 4, 128 * kt:128 * kt + 128])
        ktl = [(kf_cache(kt), VV[:, kt, :], 128) for kt in range(32)]
        ktl.append(((lambda c: (KTSN[:, c, :] if c < 4 else KTSN[0:64, 4, :])), VSN[0:32, :], 32))
        par = it % 2
        attention(lambda c: (QS[:, c, :] if c < 4 else QS[0:64, 4, :]), ktl, par, lambda kt: None)
        tail_evac(par)
        tail_tr()
        for j in range(4):
            wb = wget(f"UV{l}").rearrange("p (h k m) -> p h k m", h=4, k=4)
            for hl in range(4):
                h = 4 * j + hl
                wbk = bank(7, 32, 256)
                for rc in range(4):
                    mm(wbk, wb[:, hl, rc, :], ONT[:, rc, 128 * j + 32 * hl:128 * j + 32 * hl + 32], rc == 0, rc == 3)
                cpa(OT[:, h, TP:T], wbk)

        if l == 0:
            dump("ot", OT.rearrange("p a b -> p (a b)"), BF16, 16 * T)
        ck(5)
        wstate["depth"] = NB
        SAT2 = view(33792, BF16, [128, 8, T])
        MT2 = view(50688, BF16, [128, 8, T])
        MERGED = view(67584, BF16, [128, 16, T])
        dma("sp", hT.rearrange("p a b -> p (a b)"), d_hsp[:, :])
        dma("sp", SAT2.rearrange("p a b -> p (a b)"), d_sasp[:, :])
        dma("sp", MT2.rearrange("p a b -> p (a b)"), d_msp[:, :])
        SG = [view(101376, F32, [128, NT]), view(102784, F32, [128, NT])]
        ACC = [view(141312 + 1408 * i, F32, [128, NT]) for i in range(3)]
        T2 = [view(145536, F32, [128, NT]), view(146944, F32, [128, NT])]
        pb_set([0, 1, 2, 3, 4, 5, 6, 7])
        psc = pfv(l, "psc", 16)
        for f in range(16):
            g4 = f // 4
            for pair in range(3):
                cw = wget(f"M4{l}").rearrange("p (k m) -> p k m", m=128)
                c1 = c2 = c3 = cw
                for i, (lo, hi) in enumerate(tiles3):
                    bo, bg = pb(), pb()
                    if pair == 0:
                        for k in range(8):
                            mm(bank(bo, NT), c1[:, k, :], SAT2[:, k, lo:hi], k == 0, k == 7)
                        gwt, gof = c1, 8
                    elif pair == 1:
                        for k in range(2):
                            mm(bank(bo, NT), c2[:, k, :], MT2[:, 2 * g4 + k, lo:hi], k == 0, k == 1)
                        gwt, gof = c2, 2
                    else:
                        for k in range(16):
                            mm(bank(bo, NT), c3[:, k, :], OT[:, k, lo:hi], k == 0, k == 15)
                        gwt, gof = c3, 16
                    for k in range(16):
                        mm(bank(bg, NT), gwt[:, gof + k, :], hT[:, k, lo:hi], k == 0, k == 15)
                    sg = SG[(pair * 3 + i) % 2]
                    act(sg, bank(bg, NT), AF.Sigmoid)
                    if pair == 0:
                        tt("dve", ACC[i], bank(bo, NT), sg, ALU.mult)
                    elif pair == 1:
                        t2 = T2[i % 2]
                        stt("dve", t2, bank(bo, NT), psc[:, f:f + 1], sg, ALU.mult, ALU.mult)
                        tt("dve", ACC[i], ACC[i], t2, ALU.add)
                    else:
                        t2 = T2[i % 2]
                        tt("dve", t2, bank(bo, NT), sg, ALU.mult)
                        tt("dve", MERGED[:, f, lo:hi], ACC[i], t2, ALU.add)

        if l == 0:
            dump("merged", MERGED.rearrange("p a b -> p (a b)"), BF16, 16 * T)
        ck(6)
        YF = [view(101376, F32, [128, T]), view(105600, F32, [128, T])]
        pb_set([0, 1, 2, 3, 4])
        for n in range(8):
            wb = w3(wget(f"OUT{l}"), 16, 256)
            for m in range(2):
                f = 2 * n + m
                yf = YF[f % 2]
                for i, (lo, hi) in enumerate(tiles3):
                    b = pb()
                    for k in range(16):
                        mm(bank(b, NT), wb[:, k, 128 * m:128 * m + 128], MERGED[:, k, lo:hi], k == 0, k == 15)
                    yproj_evac(b, f, lo, hi, i, yf, f == 0, f == 15)
                dma("sp", ys[f][:, :], yf)
        residual_pass(l, 0)
        if l == 0:
            dump("xnew", XNEW.rearrange("p a b -> p (a b)"), F32, 16 * T)
        prenorm(l, 1, hT)

        ck(7)
        ACTT = view(33792, BF16, [128, NFF, T])
        UPG = [view(126720, F32, [128, 1060]), view(130960, F32, [128, 1060])]
        UPV = [view(135200, F32, [128, T]), view(139424, F32, [128, T])]
        CV = view(143648, F32, [128, 1060])
        SFFN = view(160320, F32, [128, 2, NFF])
        PG01 = view(160672, F32, [128, 2, NFF])
        PV01 = view(161024, F32, [128, 2, NFF])
        BH3 = view(161376, F32, [128, 2, NFF])
        STL = view(161728, F32, [128, 2, NFF])
        H3 = view(162080, F32, [128, 2, NFF])
        GH3B = view(147888, F32, [128, 4, 88])
        C01 = view(149296, F32, [128, 2, NFF])
        TQ = view(149648, F32, [128, NFF])
        dma("sp", SFFN, d_sffn[l])
        for u in UPG:
            K.rec("dve", lambda e, u=u: e.memset(u[:, 0:2], 0.0), [], [u[:, 0:2]])
        wf = pfv(l, "wdwf", 132).rearrange("p (j k) -> p j k", j=NFF)
        bf_ = pfv(l, "bdwf", NFF)
        pb_set([0, 1, 2, 3, 4, 5, 6, 7])
        for j in range(NFF):
            wb = w3(wget(f"UP{l}"), 16, 256)
            upg, upv = UPG[j % 2], UPV[j % 2]
            cp("dve", upg[:, 1026:1028], SFFN[:, :, j])
            for i, (lo, hi) in enumerate(tiles3):
                bg, bv = pb(), pb()
                for k in range(16):
                    mm(bank(bg, NT), wb[:, k, 0:128], hT[:, k, lo:hi], k == 0, k == 15)
                for k in range(16):
                    mm(bank(bv, NT), wb[:, k, 128:256], hT[:, k, lo:hi], k == 0, k == 15)
                for (a, b_, dlo) in segs(lo, hi, 2):
                    cp("act", upg[:, dlo:dlo + (b_ - a)], bank(bg, NT)[:, a:b_])
                cp("act", upv[:, lo:hi], bank(bv, NT))
            cp("dve", PG01[:, :, j], upg[:, 2:4])
            cp("dve", PV01[:, :, j], upv[:, 0:2])
            cp("dve", BH3[:, :, j], upg[:, 1024:1026])
            cp("dve", STL[:, :, j], upg[:, 1058:1060])
            ts("dve", CV[:, 0:1058], upg[:, 0:1058], wf[:, j, 0:1], bf_[:, j:j + 1], ALU.mult, ALU.add)
            stt("dve", CV[:, 0:1058], upg[:, 1:1059], wf[:, j, 1:2], CV[:, 0:1058], ALU.mult, ALU.add)
            stt("dve", CV[:, 0:1058], upg[:, 2:1060], wf[:, j, 2:3], CV[:, 0:1058], ALU.mult, ALU.add)
            act(CV[:, 0:1058], CV[:, 0:1058], AF.Silu)
            tt("dve", ACTT[:, j, 0:TP], CV[:, 0:TP], upv[:, 0:TP], ALU.mult)
            tt("dve", ACTT[:, j, TP:T], CV[:, 1026:1058], upv[:, TP:T], ALU.mult)
        dma("sp", d_bh3[:, :].rearrange("p (t j) -> p t j", t=2), BH3)
        allgather(d_bh3, d_gh3)
        stf = view(126720, F32, [128, 128])
        for which, src in ((0, BH3), (1, STL)):
            tr(bank(0, 128)[0:88, :], src.rearrange("p t j -> p (t j)"), identf[:])
            cp("dve", stf[0:88, :], bank(0, 128)[0:88, :])
            for t2_ in range(2):
                dma("sp", o_ffn[l, which, t2_, :].rearrange("(j p) -> j p", p=128), stf[44 * t2_:44 * t2_ + 44, :])

        def patch():
            dma("sp", GH3B, d_gh3.ap().rearrange("(r p) f -> p r f", p=128))
            h3f = H3.rearrange("p t j -> p (t j)")
            ts("dve", h3f, GH3B[:, 0, :], hsel[:, 0:1], None, ALU.mult)
            for r in range(1, 4):
                stt("dve", h3f, GH3B[:, r, :], hsel[:, r:r + 1], h3f, ALU.mult, ALU.add)
            w0, w1, w2 = wf[:, :, 0], wf[:, :, 1], wf[:, :, 2]
            h0, h1 = H3[:, 0, :], H3[:, 1, :]
            g0, g1 = PG01[:, 0, :], PG01[:, 1, :]
            c0, c1_ = C01[:, 0, :], C01[:, 1, :]
            tt("dve", c0, w0, h0, ALU.mult)
            tt("dve", TQ, w1, h1, ALU.mult)
            tt("dve", c0, c0, TQ, ALU.add)
            tt("dve", TQ, w2, g0, ALU.mult)
            tt("dve", c0, c0, TQ, ALU.add)
            tt("dve", c0, c0, bf_, ALU.add)
            tt("dve", c1_, w0, h1, ALU.mult)
            tt("dve", TQ, w1, g0, ALU.mult)
            tt("dve", c1_, c1_, TQ, ALU.add)
            tt("dve", TQ, w2, g1, ALU.mult)
            tt("dve", c1_, c1_, TQ, ALU.add)
            tt("dve", c1_, c1_, bf_, ALU.add)
            act(C01, C01, AF.Silu)
            tt("dve", C01, C01, PV01, ALU.mult)
            cp("dve", ACTT[:, :, 0:2].rearrange("p j t -> p t j"), C01)

        ck(8)
        YF = [view(126720, F32, [128, T]), view(130944, F32, [128, T])]
        pb_set([0, 1, 2, 3, 4])
        wstate["depth"] = 2
        for f in range(16):
            hb0 = wget(f"DN{l}").rearrange("p (k m) -> p k m", m=128)
            hb1 = wget(f"DN{l}").rearrange("p (k m) -> p k m", m=128)
            yf = YF[f % 2]
            for i in (1, 2, 0):
                lo, hi = tiles3[i]
                if f == 0 and i == 0:
                    patch()
                b = pb()
                for k in range(NFF):
                    hb = hb0 if k < 22 else hb1
                    mm(bank(b, NT), hb[:, k % 22, :], ACTT[:, k, lo:hi], k == 0, k == NFF - 1)
                yproj_evac(b, f, lo, hi, i, yf, f == 0, f == 15)
            dma("sp", ys[f][:, :], yf)
        wstate["depth"] = NB
        residual_pass(l, 1)
        ck(9)

    def final_out():
        pb_set([0, 1, 2, 3])
        for t9 in range(9):
            rows = 128 if t9 < 8 else 32
            tc0 = t9 * 128
            tok = TOK[t9 % 2]
            for g in range(4):
                b = pb()
                for j in range(4):
                    f = 4 * g + j
                    tr(bank(b, 128, 128 * j)[0:rows, :], XNEW[:, f, tc0:tc0 + rows], identf[:])
                cpa(tok[0:rows, 512 * g:512 * g + 512], bank(b)[0:rows, :])
            dma("sp", o_y[tc0:tc0 + rows, :], tok[0:rows, :])

    try:
        prenorm(0, 0, hT)
        ck(1)
        for l in range(L):
            layer(l)
            if l + 1 < L:
                prenorm(l + 1, 0, hT)
        final_out()
    except _Stop:
        if STAGE == 0:
            final_out()

    K.emit()
    es.close()
    nc._ext_in = set(K.ext_in)
    return nc


def _blk(w, cols):
    K_ = w.shape[0]
    kc = K_ // 128
    out = []
    for c in cols:
        sub = w[:, c]
        out.append(sub.reshape(kc, 128, len(c)).transpose(1, 0, 2).reshape(128, kc * len(c)))
    return np.ascontiguousarray(np.stack(out))


def prep_shared(inp):
    f = np.float32
    sh = {}
    w_in = inp["w_in"]
    ar = np.arange
    sh["wmod"] = np.stack([_blk(inp["w_mod"][l], [ar(256 * n, 256 * n + 256) for n in range(48)]) for l in range(L)])
    sh["wA"] = np.stack([_blk(w_in[l], [np.concatenate([ar(128 * j, 128 * j + 128), ar(1024 + 128 * j, 1024 + 128 * j + 128)])
                                       for j in range(8)]) for l in range(L)])
    sh["wB"] = np.stack([_blk(w_in[l], [ar(OFF_B + 256 * n, OFF_B + 256 * n + 256) for n in range(4)]) for l in range(L)])
    sh["wQ"] = np.stack([_blk(w_in[l], [ar(OFF_Q + 256 * n, OFF_Q + 256 * n + 256) for n in range(2)]) for l in range(L)])
    sh["wKVR"] = np.stack([_blk(w_in[l], [ar(OFF_KV, OFF_G)])[0] for l in range(L)])
    whd = np.zeros((L, 16, 128, 2048), f)
    wuvp = np.zeros((L, 4, 128, 4, 512), f)
    for l in range(L):
        wuq = inp["w_uq"][l].reshape(4, 128, 16, 192)
        wuk = inp["w_uk"][l]
        wuv = inp["w_uv"][l].reshape(4, 128, 16, 128)
        for h in range(16):
            qn = wuq[:, :, h, 0:128].transpose(1, 0, 2).reshape(128, 512)
            qr = wuq[:, :, h, 128:192]
            qs = np.concatenate([qr[..., 32:64], qr[..., 0:32]], axis=-1)
            whd[l, h, :, 0:512] = qn
            whd[l, h, :, 512:768] = qr.transpose(1, 0, 2).reshape(128, 256)
            whd[l, h, :, 768:1024] = qs.transpose(1, 0, 2).reshape(128, 256)
            whd[l, h, :, 1024:1536] = wuk[:, h, :].T
            uv = wuv[:, :, h, :].transpose(1, 0, 2).reshape(128, 512)
            whd[l, h, :, 1536:2048] = uv
            wuvp[l, h // 4, :, h % 4, :] = uv
    sh["wHD"] = whd
    sh["wUV"] = wuvp.reshape(L, 4, 128, 2048)
    wm4 = np.zeros((L, 16, 128, 9472), f)
    for l in range(L):
        pa = inp["w_pa"][l].reshape(8, 128, 16, 128)
        oc = inp["w_oc"][l].reshape(16, 128, 16, 128)
        pool = inp["w_pool"][l].reshape(4, 2, 128, 4, 128)
        wg = w_in[l][:, OFF_G:].reshape(16, 128, 3, 16, 128)
        for fch in range(16):
            o = 0
            blkA = pa[:, :, fch, :].transpose(1, 0, 2).reshape(128, 1024)
            wm4[l, fch, :, 0:1024] = blkA
            wm4[l, fch, :, 1024:3072] = wg[:, :, 0, fch, :].transpose(1, 0, 2).reshape(128, 2048)
            wm4[l, fch, :, 3072:3328] = pool[fch // 4, :, :, fch % 4, :].transpose(1, 0, 2).reshape(128, 256)
            wm4[l, fch, :, 3328:5376] = wg[:, :, 1, fch, :].transpose(1, 0, 2).reshape(128, 2048)
            wm4[l, fch, :, 5376:7424] = oc[:, :, fch, :].transpose(1, 0, 2).reshape(128, 2048)
            wm4[l, fch, :, 7424:9472] = wg[:, :, 2, fch, :].transpose(1, 0, 2).reshape(128, 2048)
    sh["wM4"] = wm4
    sh["wOUT"] = np.stack([_blk(inp["w_out"][l], [ar(256 * n, 256 * n + 256) for n in range(8)]) for l in range(L)])
    sh["wUP"] = np.stack([_blk(inp["w_up"][l], [np.concatenate([ar(128 * j, 128 * j + 128), ar(DFF + 128 * j, DFF + 128 * j + 128)])
                                              for j in range(NFF)]) for l in range(L)])
    wdn = np.zeros((L, 32, 128, 2816), f)
    for l in range(L):
        wd = inp["w_down"][l].reshape(2, 22, 128, 16, 128)
        for fch in range(16):
            for half in range(2):
                wdn[l, 2 * fch + half] = wd[half, :, :, fch, :].transpose(1, 0, 2).reshape(128, 2816)
    sh["wDN"] = wdn
    pfa = np.zeros((128, L * PFL), f)
    for l in range(L):
        def put(name, arr):
            pfa[:, l * PFL + PF[name]:l * PFL + PF[name] + arr.shape[1]] = arr
        fm = lambda v: v.reshape(-1, 128).T
        put("gpm", fm(inp["g_pre_mix"][l])); put("gpo", fm(inp["g_post_mix"][l]))
        put("gpf", fm(inp["g_pre_ffn"][l])); put("gpof", fm(inp["g_post_ffn"][l]))
        put("psc", fm(inp["pool_scale"][l])); put("bdwa", fm(inp["b_dwa"][l]))
        put("lng", fm(inp["ln_a_g"][l])); put("lnb", fm(inp["ln_a_b"][l]))
        put("gq", fm(inp["g_q_lat"][l])); put("bdwf", fm(inp["b_dwf"][l]))
        put("bmod", fm(inp["b_mod"][l]))
        put("wdwa", inp["w_dwa"][l].reshape(31, 8, 128).transpose(2, 1, 0).reshape(128, 248))
        put("wdwf", inp["w_dwf"][l].reshape(3, NFF, 128).transpose(2, 1, 0).reshape(128, 132))
    sh["pf"] = pfa
    sh["gkv"] = np.ascontiguousarray(np.broadcast_to(inp["g_kv_lat"][:, None, :], (L, 128, 512))).astype(f)
    return sh


def prep_core(inp, c):
    f = np.float32
    b, r = c // 4, c % 4
    m = {}
    m["xin"] = np.ascontiguousarray(np.concatenate([inp["x_prompt"][b, 1024 * r:1024 * r + 1024], inp["x_sample"][c]], axis=0))
    cv = np.stack([inp["c_prompt"][b], inp["c_sample"][c]], axis=0)
    m["cT"] = np.ascontiguousarray(cv.reshape(2, 16, 128).transpose(2, 1, 0))
    pos = np.concatenate([1024 * r + np.arange(1024), 4096 + np.arange(32)]).astype(f)
    inv = (np.float32(10000.0) ** (-np.arange(32, dtype=f) / np.float32(32))).astype(f)
    ang = (pos[:, None] * inv[None, :]).astype(f)
    cos, sin = np.cos(ang).astype(f), np.sin(ang).astype(f)
    ropeF = np.zeros((64, 2, T), f)
    ropeF[0:32, 0] = cos.T; ropeF[32:64, 0] = cos.T
    ropeF[0:32, 1] = -sin.T; ropeF[32:64, 1] = sin.T
    m["ropeF"] = ropeF
    ropeT = np.zeros((128, 9, 2, 64), f)
    cc2 = np.concatenate([cos, cos], axis=1)
    ss2 = np.concatenate([-sin, sin], axis=1)
    pad = np.zeros((9 * 128, 64), f)
    pad[:T] = cc2
    ropeT[:, :, 0, :] = pad.reshape(9, 128, 64).transpose(1, 0, 2)
    pad = np.zeros((9 * 128, 64), f)
    pad[:T] = ss2
    ropeT[:, :, 1, :] = pad.reshape(9, 128, 64).transpose(1, 0, 2)
    m["ropeT"] = ropeT
    qch = (1024 * r + np.arange(1024)) // 64
    cidx = np.arange(64)[:, None]
    m["qB"] = np.where(cidx > qch[None, :], -30000.0, 0.0).astype(f)
    m["khot"] = (cidx == qch[None, :]).astype(f)
    hs = np.zeros((128, 4), f)
    if r > 0:
        hs[:, r - 1] = 1.0
    m["hsel"] = hs
    ic = np.zeros((128, 4, 16), f)
    for g, w in enumerate((2, 4, 8, 16)):
        ic[:, g, :] = 1.0 / np.minimum(w, 1024 * r + np.arange(16) + 1).astype(f)
    m["icnt"] = ic
    m["sconv"] = np.ascontiguousarray(inp["state_conv"][:, c].reshape(L, 30, 8, 128).transpose(0, 3, 2, 1))
    m["spool"] = np.ascontiguousarray(inp["state_pool"][:, c].reshape(L, 15, 8, 128).transpose(0, 3, 2, 1))
    m["sffn"] = np.ascontiguousarray(inp["state_ffn"][:, c].reshape(L, 2, NFF, 128).transpose(0, 3, 1, 2))
    m["cacheT"] = np.ascontiguousarray(np.concatenate([inp["cache_ckv"][:, c].transpose(0, 2, 1),
                                                       inp["cache_krope"][:, c].transpose(0, 2, 1)], axis=1))
    m["cacheV"] = np.ascontiguousarray(inp["cache_ckv"][:, c])
    return m


WNAMES = ("wmod", "wA", "wB", "wQ", "wKVR", "wHD", "wUV", "wM4", "wOUT", "wUP", "wDN")


def shard_shared(sh, c):
    m = {"pf": sh["pf"], "gkv": sh["gkv"]}
    for name in WNAMES:
        a = sh[name]
        E = a.shape[-1]
        a2 = a.reshape(L, -1, E)
        for l in range(L):
            m[f"{name}{l}"] = a2[l]
    return m


_NC = None


def kernel(**inputs):
    global _NC
    inp = {k: np.asarray(v, dtype=np.float32) for k, v in inputs.items()}
    if _NC is None:
        _NC = build_program()
    nc = _NC
    sh = prep_shared(inp)
    in_maps = []
    for c in range(8):
        m = prep_core(inp, c)
        m.update(shard_shared(sh, c))
        m = {k: v for k, v in m.items() if k in nc._ext_in}
        in_maps.append(m)
    res = run_bass_kernel_spmd(nc, in_maps, core_ids=list(range(8)))
    R = res.results
    f = np.float32
    y_p = np.zeros((2, 4096, D), f); y_s = np.zeros((8, 32, D), f)
    p_ckv = np.zeros((L, 2, 4096, 512), f); p_kr = np.zeros((L, 2, 4096, 64), f)
    p_conv = np.zeros((L, 2, 30, 1024), f); p_pool = np.zeros((L, 2, 15, 1024), f); p_ffn = np.zeros((L, 2, 2, DFF), f)
    s_ckv = np.zeros((L, 8, 32, 512), f); s_kr = np.zeros((L, 8, 32, 64), f)
    s_conv = np.zeros((L, 8, 30, 1024), f); s_pool = np.zeros((L, 8, 15, 1024), f); s_ffn = np.zeros((L, 8, 2, DFF), f)
    for c in range(8):
        b, r = c // 4, c % 4
        o = R[c]
        y_p[b, 1024 * r:1024 * r + 1024] = o["o_y"][:1024]
        y_s[c] = o["o_y"][1024:]
        p_ckv[:, b, 1024 * r:1024 * r + 1024] = o["o_ckv"][:, :1024]
        s_ckv[:, c] = o["o_ckv"][:, 1024:]
        p_kr[:, b, 1024 * r:1024 * r + 1024] = o["o_kr"][:, :1024]
        s_kr[:, c] = o["o_kr"][:, 1024:]
        if r == 3:
            p_conv[:, b] = o["o_conv"][:, 0]
            p_pool[:, b] = o["o_pool"][:, 0]
            p_ffn[:, b] = o["o_ffn"][:, 0]
        s_conv[:, c] = o["o_conv"][:, 1]
        s_pool[:, c] = o["o_pool"][:, 1]
        s_ffn[:, c] = o["o_ffn"][:, 1]
    return (y_p, y_s, p_ckv, p_kr, p_conv, p_pool, p_ffn, s_ckv, s_kr, s_conv, s_pool, s_ffn)
```

```python
import numpy as np
from contextlib import ExitStack
import concourse.bass as bass
import concourse.mybir as mybir
from concourse.bass_utils import run_bass_kernel_spmd

F32 = mybir.dt.float32
BF16 = mybir.dt.bfloat16
AF = mybir.ActivationFunctionType
ALU = mybir.AluOpType
AX = mybir.AxisListType

L = 2
D = 2048
T = 1056
TP = 1024
TS = 32
NT = 352
DFF = 5632
NFF = 44
EPS = 1e-6
ATTN_SCALE = 192.0 ** -0.5
OFF_A, OFF_B, OFF_Q, OFF_KV, OFF_R, OFF_G = 0, 2048, 3072, 3584, 4096, 4160
PFL = 628
PF = dict(gpm=0, gpo=16, gpf=32, gpof=48, psc=64, bdwa=80, lng=88, lnb=96, gq=104, bdwf=108, bmod=152,
          wdwa=248, wdwf=496)
WCAP = 4096
NB = 3
ARENA = 163840


def dsize(dt):
    s = str(dt)
    if "64" in s:
        return 8
    if "32" in s:
        return 4
    if "16" in s:
        return 2
    return 1


class Op:
    __slots__ = ("eng", "seq", "fn", "waits", "kind", "gid", "qidx")


class KB:
    COMPUTE = ("pe", "act", "dve", "pool")
    EPOCH = 30000
    ND = {"sp": 16, "pool": 8}

    def __init__(self, nc):
        self.nc = nc
        self.ops = {e: [] for e in ("pe", "act", "dve", "pool", "sp")}
        self.state = {}
        self.ext_in = set()
        self.ext_out = set()
        self.known = {e: {} for e in self.ops}
        self.known_d = {e: set() for e in self.ops}
        self.needed = set()
        self.dmaq = {"sp": [], "pool": []}
        self.ccs = []
        self.gcount = 0
        self.allops = []

    def _span(self, ap):
        name = ap.tensor.name
        dsz = dsize(ap.dtype)
        dims = ap.ap
        off = ap.offset
        if name.startswith("sb") or name.startswith("ps"):
            row = dims[0][0]
            col = off % row if row > 0 else off
            ext = 1
            for st, cnt in dims[1:]:
                ext += (cnt - 1) * abs(st)
            page = 256 if name.startswith("sb") else 2048
            lo = col * dsz
            hi = (col + ext) * dsz - 1
        else:
            ext = 1
            for st, cnt in dims:
                ext += (cnt - 1) * abs(st)
            page = 65536
            lo = off * dsz
            hi = (off + ext) * dsz - 1
        return name, lo // page, hi // page

    def rec(self, eng, fn, ins, outs, kind="c"):
        op = Op()
        op.eng = eng
        op.fn = fn
        op.kind = kind
        op.seq = len(self.ops[eng])
        op.gid = self.gcount
        self.gcount += 1
        deps = set()
        for ap in ins:
            name, p0, p1 = self._span(ap)
            if name in self.ext_in:
                continue
            st = self.state.setdefault(name, {})
            isps = name.startswith("ps")
            for p in range(p0, p1 + 1):
                s = st.get(p)
                if s is not None and s[0] is not None:
                    deps.add(s[0])
                if isps and s is not None:
                    for e2, rop in s[1].items():
                        if e2 != eng:
                            deps.add(rop)
        for ap in outs:
            name, p0, p1 = self._span(ap)
            if name in self.ext_out:
                continue
            st = self.state.setdefault(name, {})
            for p in range(p0, p1 + 1):
                s = st.get(p)
                if s is not None:
                    if s[0] is not None:
                        deps.add(s[0])
                    deps.update(s[1].values())
                    deps.update(s[2])
        need_c = {}
        need_d = []
        for d in deps:
            if d.kind == "c":
                if eng == "pe" and d.eng == "pe":
                    continue
                if need_c.get(d.eng, -1) < d.seq:
                    need_c[d.eng] = d.seq
            else:
                need_d.append(d)
        waits = []
        kn = self.known[eng]
        for pe, sq in need_c.items():
            if kn.get(pe, -1) >= sq:
                continue
            kn[pe] = sq
            waits.append(("c", pe, sq))
            self.needed.add((pe, sq))
        kd = self.known_d[eng]
        for d in need_d:
            if d.gid in kd:
                continue
            kd.add(d.gid)
            waits.append(("d", d))
        op.waits = waits
        for ap in ins:
            name, p0, p1 = self._span(ap)
            if name in self.ext_in:
                continue
            st = self.state[name]
            for p in range(p0, p1 + 1):
                s = st.get(p)
                if s is None:
                    s = [None, {}, []]
                    st[p] = s
                if kind == "c":
                    s[1][eng] = op
                else:
                    s[2].append(op)
        for ap in outs:
            name, p0, p1 = self._span(ap)
            if name in self.ext_out:
                continue
            st = self.state[name]
            for p in range(p0, p1 + 1):
                st[p] = [op, {}, []]
        self.ops[eng].append(op)
        if kind == "d":
            op.qidx = len(self.dmaq[eng])
            self.dmaq[eng].append(op)
        elif kind == "cc":
            op.qidx = len(self.ccs)
            self.ccs.append(op)
        return op

    def emit(self):
        nc = self.nc
        with ExitStack() as es:
            signo = {}
            cnt = {}
            for e in self.COMPUTE:
                n = 0
                for op in self.ops[e]:
                    if op.kind == "c" and (e, op.seq) in self.needed:
                        signo[(e, op.seq)] = n
                        n += 1
                cnt[e] = n
            csem = {}
            for e in self.COMPUTE:
                ne = cnt[e] // self.EPOCH + 1
                csem[e] = [es.enter_context(nc.semaphore(f"c_{e}_{i}")) for i in range(ne)]
            dsem = {q: [es.enter_context(nc.semaphore(f"d_{q}_{i}")) for i in range(self.ND[q])] for q in self.dmaq}
            ccsem = [es.enter_context(nc.semaphore(f"cc_{i}")) for i in range(len(self.ccs))]

            def ev(w):
                if w[0] == "c":
                    n = signo[(w[1], w[2])]
                    return csem[w[1]][n // self.EPOCH], n % self.EPOCH + 1
                d = w[1]
                if d.kind == "cc":
                    return ccsem[d.qidx], 1
                nd = self.ND[d.eng]
                return dsem[d.eng][d.qidx % nd], 16 * (d.qidx // nd + 1)

            block = es.enter_context(nc.Block())

            def run(ename, e):
                for op in self.ops[ename]:
                    for w in op.waits:
                        s, v = ev(w)
                        e.wait_ge(s, v)
                    if op.kind == "d":
                        nd = self.ND[ename]
                        if op.qidx >= nd:
                            e.wait_ge(dsem[ename][op.qidx % nd], 16 * (op.qidx // nd))
                        op.fn(e).then_inc(dsem[ename][op.qidx % nd], 16)
                    elif op.kind == "cc":
                        op.fn(e).then_inc(ccsem[op.qidx])
                    else:
                        ins = op.fn(e)
                        key = (ename, op.seq)
                        if key in signo:
                            n = signo[key]
                            ins.then_inc(csem[ename][n // self.EPOCH], 1)
                if ename in self.dmaq:
                    nd = self.ND[ename]
                    q = self.dmaq[ename]
                    for i in range(min(nd, len(q))):
                        last = ((len(q) - 1 - i) // nd) * nd + i
                        e.wait_ge(dsem[ename][i], 16 * (last // nd + 1))
                if ename == "pool":
                    for i in range(len(self.ccs)):
                        e.wait_ge(ccsem[i], 1)

            @block.tensor
            def _(e):
                run("pe", e)

            @block.scalar
            def _(e):
                run("act", e)

            @block.vector
            def _(e):
                run("dve", e)

            @block.gpsimd
            def _(e):
                run("pool", e)

            @block.sync
            def _(e):
                run("sp", e)


class _Stop(Exception):
    pass


STAGE = None


def build_program():
    nc = bass.Bass("TRN2", target_bir_lowering=False)
    K = KB(nc)

    def ck(n):
        if STAGE is not None and n > STAGE:
            raise _Stop()

    import os
    DUMP = os.environ.get("DBG_DUMP", "") != ""

    def dump(name, ap2d, dt, ncols):
        if not DUMP:
            return
        K.ext_out.add("dbg_" + name)
        t_ = nc.dram_tensor("dbg_" + name, [128, ncols], dt, kind="ExternalOutput")
        K.rec("sp", lambda e: e.dma_start(out=t_[:, :], in_=ap2d), [ap2d], [t_[:, :]], kind="d")

    def din(name, shape, dt=F32):
        K.ext_in.add(name)
        return nc.dram_tensor(name, list(shape), dt, kind="ExternalInput")

    def dout(name, shape):
        K.ext_out.add(name)
        return nc.dram_tensor(name, list(shape), F32, kind="ExternalOutput")

    def dscr(name, shape, dt=F32):
        return nc.dram_tensor(name, list(shape), dt)

    d_xin = din("xin", [T, D])
    d_cT = din("cT", [128, 16, 2])
    d_pf = din("pf", [128, L * PFL])
    d_gkv = din("gkv", [L, 128, 512])
    d_ropeF = din("ropeF", [64, 2, T])
    d_ropeT = din("ropeT", [128, 9, 2, 64])
    d_qB = din("qB", [64, TP])
    d_khot = din("khot", [64, TP])
    d_hsel = din("hsel", [128, 4])
    d_icnt = din("icnt", [128, 4, 16])
    d_sconv = din("sconv", [L, 128, 8, 30])
    d_spool = din("spool", [L, 128, 8, 15])
    d_sffn = din("sffn", [L, 128, 2, NFF])
    if STAGE is None or STAGE >= 4:
        d_cacheT = din("cacheT", [L, 576, 4096])
        d_cacheV = din("cacheV", [L, 4096, 512])
    WSPEC = [("wmod", 48, 4096), ("wA", 8, 4096), ("wB", 4, 4096), ("wQ", 2, 4096), ("wKVR", 1, 9216),
             ("wHD", 16, 2048), ("wUV", 4, 2048), ("wM4", 16, 9472), ("wOUT", 8, 4096), ("wUP", NFF, 4096),
             ("wDN", 32, 2816)]
    FIRST = {"wmod": 0, "wKVR": 1, "wA": 2, "wB": 2, "wQ": 2, "wHD": 4, "wUV": 4, "wM4": 5, "wOUT": 6, "wUP": 7, "wDN": 8}
    w_full = {}
    for name, nblk, E in WSPEC:
        w_full[name] = []
        for l in range(L):
            need = STAGE is None or (l == 0 and STAGE >= FIRST[name])
            if need:
                w_full[name].append(din(f"{name}{l}", [nblk * 128, E]))
            else:
                w_full[name].append(nc.dram_tensor(f"d_wf_{name}{l}", [nblk * 128, E], F32))

    def gw(name, l, n):
        return w_full[name][l][n * 128:(n + 1) * 128, :]
    o_y = dout("o_y", [T, D])
    o_ckv = dout("o_ckv", [L, T, 512])
    o_kr = dout("o_kr", [L, T, 64])
    o_conv = dout("o_conv", [L, 2, 30, 1024])
    o_pool = dout("o_pool", [L, 2, 15, 1024])
    o_ffn = dout("o_ffn", [L, 2, 2, DFF])
    xs = [dscr(f"d_xs{f}", [128, T]) for f in range(16)]
    ys = [dscr(f"d_ys{f}", [128, T]) for f in range(16)]
    d_hsp = dscr("d_hsp", [128, 16 * T], BF16)
    d_sasp = dscr("d_sasp", [128, 8 * T], BF16)
    d_msp = dscr("d_msp", [128, 8 * T], BF16)
    d_bkv = [dscr(f"d_bkv{i}", [384, 1024], BF16) for i in range(3)]
    d_gkvb = [dscr(f"d_gkvb{i}", [1536, 1024], BF16) for i in range(3)]
    d_bh = dscr("d_bh", [128, 360])
    d_gh = dscr("d_gh", [512, 360])
    d_bh3 = dscr("d_bh3", [128, 88])
    d_gh3 = dscr("d_gh3", [512, 88])

    es = ExitStack()
    sbt = lambda name, shape, dt: es.enter_context(nc.sbuf_tensor(name, list(shape), dt))
    identf = sbt("sb_identf", [128, 128], F32)
    identb = sbt("sb_identb", [128, 128], BF16)
    ones = sbt("sb_ones", [128, 128], BF16)
    pf = sbt("sb_pf", [128, L * PFL], F32)
    modT = sbt("sb_mod", [128, L, 96, 2], F32)
    mv = sbt("sb_mv", [128, L, 6, 16, 2], F32)
    ropeF = sbt("sb_ropeF", [64, 2, T], F32)
    hsel = sbt("sb_hsel", [128, 4], F32)
    icnt = sbt("sb_icnt", [128, 4, 16], F32)
    cTf = sbt("sb_cTf", [128, 16, 2], F32)
    cTb = sbt("sb_cTb", [128, 16, 2], BF16)
    small = sbt("sb_small", [128, 64], F32)
    wbuf = sbt("sb_wbuf", [128, NB, WCAP], BF16)
    arena = sbt("sb_arena", [128, ARENA // 4], F32)
    ps = es.enter_context(nc.psum_tensor("ps_all", [128, 4096], F32))

    def view(off, dt, shape):
        dsz = dsize(dt)
        n = int(np.prod(shape[1:]))
        nbytes = n * dsz
        assert off % 4 == 0 and nbytes % 4 == 0 and off + nbytes <= ARENA, (off, shape)
        a = arena[:, off // 4:(off + nbytes) // 4]
        if dt == BF16:
            a = a.bitcast(BF16)
        if len(shape) == 3:
            a = a.rearrange("p (a b) -> p a b", a=shape[1])
        elif len(shape) == 4:
            a = a.rearrange("p (a b c) -> p a b c", a=shape[1], b=shape[2])
        if shape[0] != 128:
            a = a[0:shape[0]]
        return a

    def bank(b, n=512, lo=0):
        return ps[:, 512 * b + lo:512 * b + lo + n]

    def bank_bf(b, lo_bytes, shape):
        n = int(np.prod(shape[1:]))
        a = ps[:, 512 * b + lo_bytes // 4:512 * b + lo_bytes // 4 + n // 2].bitcast(BF16)
        if len(shape) == 3:
            a = a.rearrange("p (a b) -> p a b", a=shape[1])
        return a

    def mm(out, lhsT, rhs, start=True, stop=True):
        K.rec("pe", lambda e: e.matmul(out, lhsT=lhsT, rhs=rhs, start=start, stop=stop), [lhsT, rhs], [out])

    def tr(out, in_, ident):
        K.rec("pe", lambda e: e.transpose(out, in_, ident), [in_, ident], [out])

    def act(out, in_, func, scale=None, bias=None):
        ins = [in_]
        kw = {}
        if scale is not None:
            kw["scale"] = scale
            if not isinstance(scale, float):
                ins.append(scale)
        if bias is not None:
            kw["bias"] = bias
            if not isinstance(bias, float):
                ins.append(bias)
        K.rec("act", lambda e: e.activation(out=out, in_=in_, func=func, **kw), ins, [out])

    def tt(eng, out, in0, in1, op):
        K.rec(eng, lambda e: e.tensor_tensor(out=out, in0=in0, in1=in1, op=op), [in0, in1], [out])

    def ts(eng, out, in0, s1, s2, op0, op1=None):
        ins = [in0] + [s for s in (s1, s2) if s is not None and not isinstance(s, float)]
        if op1 is None:
            K.rec(eng, lambda e: e.tensor_scalar(out=out, in0=in0, scalar1=s1, scalar2=None, op0=op0), ins, [out])
        else:
            K.rec(eng, lambda e: e.tensor_scalar(out=out, in0=in0, scalar1=s1, scalar2=s2, op0=op0, op1=op1), ins, [out])

    def stt(eng, out, in0, scalar, in1, op0, op1):
        ins = [in0, in1] + ([] if isinstance(scalar, float) else [scalar])
        K.rec(eng, lambda e: e.scalar_tensor_tensor(out=out, in0=in0, scalar=scalar, in1=in1, op0=op0, op1=op1), ins, [out])

    def cp(eng, out, in_):
        if eng == "act":
            K.rec("act", lambda e: e.copy(out=out, in_=in_), [in_], [out])
        else:
            K.rec(eng, lambda e: e.tensor_copy(out=out, in_=in_), [in_], [out])

    def recip(out, in_):
        K.rec("dve", lambda e: e.reciprocal(out=out, in_=in_), [in_], [out])

    def dma(q, out, in_):
        return K.rec(q, lambda e: e.dma_start(out=out, in_=in_), [in_], [out], kind="d")

    def allgather(src, dst, groups=((0, 1, 2, 3), (4, 5, 6, 7))):
        assert src.ap().nbytes() if False else True
        K.rec("pool", lambda e: e.collective_compute("AllGather", ALU.bypass, replica_groups=[list(g) for g in groups],
                                                     ins=[src.ap().opt()], outs=[dst.ap().opt()]),
              [src.ap()], [dst.ap()], kind="cc")

    cpi = [0]

    def cpa(out, in_):
        cpi[0] += 1
        cp("act" if cpi[0] % 2 else "dve", out, in_)

    def rsqrt_to(dst, src, scale):
        ts("dve", dst, src, scale, EPS, ALU.mult, ALU.add)
        act(dst, dst, AF.Sqrt)
        recip(dst, dst)

    plan = []
    for n in range(16):
        plan.append(("mod0a", gw("wmod", 0, n), 4096))
    for l in range(L):
        for j in range(8):
            plan.append((f"A{l}", gw("wA", l, j), 4096))
        for j in range(4):
            plan.append((f"B{l}", gw("wB", l, j), 4096))
        for j in range(2):
            plan.append((f"Q{l}", gw("wQ", l, j), 4096))
        if l == 0:
            for n in range(16, 48):
                plan.append(("mod0b", gw("wmod", 0, n), 4096))
        if l == 0 and STAGE is None:
            for n in range(48):
                plan.append(("mod1", gw("wmod", 1, n), 4096))
        for h in range(16):
            plan.append((f"HD{l}", gw("wHD", l, h), 2048))
        for j in range(4):
            plan.append((f"UV{l}", gw("wUV", l, j), 2048))
        for f in range(16):
            plan.append((f"M4{l}", gw("wM4", l, f)[:, 0:3072], 3072))
            plan.append((f"M4{l}", gw("wM4", l, f)[:, 3072:5376], 2304))
            plan.append((f"M4{l}", gw("wM4", l, f)[:, 5376:9472], 4096))
        for j in range(8):
            plan.append((f"OUT{l}", gw("wOUT", l, j), 4096))
        for j in range(NFF):
            plan.append((f"UP{l}", gw("wUP", l, j), 4096))
        for j in range(32):
            plan.append((f"DN{l}", gw("wDN", l, j), 2816))
    wstate = {"issued": 0, "next": 0}

    def wget(tag):
        n = wstate["next"]
        assert plan[n][0] == tag, (plan[n][0], tag, n)
        while wstate["issued"] < min(len(plan), n + wstate.get("depth", NB)):
            m = wstate["issued"]
            _, src, E = plan[m]
            dma("pool", wbuf[:, m % NB, 0:E], src)
            wstate["issued"] += 1
        wstate["next"] += 1
        return wbuf[:, n % NB, 0:plan[n][2]]

    def w3(wb, k, m):
        return wb.rearrange("p (k m) -> p k m", k=k)

    pbs = {"list": list(range(8)), "i": 0}

    def pb_set(lst):
        pbs["list"] = list(lst)
        pbs["i"] = 0

    def pb():
        b = pbs["list"][pbs["i"] % len(pbs["list"])]
        pbs["i"] += 1
        return b

    tiles3 = [(0, 352), (352, 704), (704, 1056)]

    def segs(lo, hi, H):
        out = []
        if lo < TP:
            e = min(hi, TP)
            out.append((0, e - lo, H + lo))
        if hi > TP:
            s = max(lo, TP)
            out.append((s - lo, hi - lo, s + 2 * H))
        return out

    def pfv(l, name, n):
        o = l * PFL + PF[name]
        return pf[:, o:o + n]

    A0 = 0
    hT = view(A0, BF16, [128, 16, T])
    XNEW = view(33792, F32, [128, 16, T])
    QLAT = view(150528, BF16, [128, 4, T])
    KTSN = view(158976, BF16, [128, 5, 32])
    VSN = view(159296, BF16, [128, 512])
    OT = view(107520, BF16, [128, 16, T])
    BC0 = view(140864, F32, [128, 1088])
    BC1 = view(145216, F32, [128, 1088])
    SM2 = view(160320, F32, [128, 880])

    K.rec("pool", lambda e: e.memset(identf[:], 0.0), [], [identf[:]])
    K.rec("pool", lambda e: e.affine_select(out=identf[:], in_=identf[:], pattern=[[-1, 128]], compare_op=ALU.not_equal,
                                            fill=1.0, base=0, channel_multiplier=1), [identf[:]], [identf[:]])
    cp("dve", identb[:], identf[:])
    K.rec("dve", lambda e: e.memset(ones[:], 1.0), [], [ones[:]])
    dma("sp", pf[:], d_pf[:, :])
    dma("sp", cTf[:], d_cT[:, :, :])
    dma("sp", ropeF[:], d_ropeF[:, :, :])
    dma("sp", hsel[:], d_hsel[:, :])
    dma("sp", icnt[:], d_icnt[:, :, :])
    act(cTb[:], cTf[:], AF.Silu)

    def mod_compute(l, n0, n1, tag):
        bk = 7
        for n in range(n0, n1):
            wb = w3(wget(tag), 16, 256)
            for m in range(2):
                ch = 2 * n + m
                for k in range(16):
                    mm(bank(bk, 2, 2 * ch), wb[:, k, 128 * m:128 * m + 128], cTb[:, k, :], k == 0, k == 15)
        psm = bank(bk, 192).rearrange("p (c s) -> p c s", s=2)
        c0, c1 = 2 * n0, 2 * n1
        for s in range(2):
            tt("dve", modT[:, l, c0:c1, s], psm[:, c0:c1, s], pfv(l, "bmod", 96)[:, c0:c1], ALU.add)
        for s in range(2):
            tmp = small[:, 0:16]
            if c0 == 0:
                ts("dve", tmp, modT[:, l, 16:32, s], 1.0, None, ALU.add)
                tt("dve", mv[:, l, 0, :, s], tmp, pfv(l, "gpm", 16), ALU.mult)
                cp("dve", mv[:, l, 1, :, s], modT[:, l, 0:16, s])
            if c1 == 96:
                tt("dve", mv[:, l, 2, :, s], modT[:, l, 32:48, s], pfv(l, "gpo", 16), ALU.mult)
                ts("dve", tmp, modT[:, l, 64:80, s], 1.0, None, ALU.add)
                tt("dve", mv[:, l, 3, :, s], tmp, pfv(l, "gpf", 16), ALU.mult)
                cp("dve", mv[:, l, 4, :, s], modT[:, l, 48:64, s])
                tt("dve", mv[:, l, 5, :, s], modT[:, l, 80:96, s], pfv(l, "gpof", 16), ALU.mult)


    TOK = [view(101376, F32, [128, D]), view(109568, F32, [128, D])]
    pb_set([0, 1, 2, 3])
    for t9 in range(9):
        rows = 128 if t9 < 8 else 32
        tc0 = t9 * 128
        tok = TOK[t9 % 2]
        dma("sp", tok[0:rows, :], d_xin[tc0:tc0 + rows, :])
        for g in range(4):
            b = pb()
            for j in range(4):
                f = 4 * g + j
                tr(bank(b, rows, 128 * j), tok[0:rows, 128 * f:128 * f + 128], identf[0:rows, 0:rows])
            src = bank(b).rearrange("p (j t) -> p j t", j=4)[:, :, 0:rows]
            cpa(XNEW[:, 4 * g:4 * g + 4, tc0:tc0 + rows], src)

    mod_compute(0, 0, 16, "mod0a")

    def prenorm(l, sub, dst):
        SQ = [view(101376, BF16, [128, T]), view(103488, BF16, [128, T])]
        TMPF = [view(105600, F32, [128, T]), view(109824, F32, [128, T])]
        ssb = [5, 6, 7]
        for f in range(16):
            sq = SQ[f % 2]
            act(sq, XNEW[:, f, :], AF.Square)
            for i, (lo, hi) in enumerate(tiles3):
                mm(bank(ssb[i], NT), ones[:], sq[:, lo:hi], f == 0, f == 15)
            dma("sp", xs[f][:, :], XNEW[:, f, :])
        for i, (lo, hi) in enumerate(tiles3):
            rsqrt_to(BC0[:, lo:hi], bank(ssb[i], NT), 1.0 / D)
        ia, ib = (0, 1) if sub == 0 else (3, 4)
        for f in range(16):
            tmp = TMPF[f % 2]
            tt("dve", tmp, XNEW[:, f, :], BC0[:, 0:T], ALU.mult)
            ts("dve", dst[:, f, 0:TP], tmp[:, 0:TP], mv[:, l, ia, f, 0:1], mv[:, l, ib, f, 0:1], ALU.mult, ALU.add)
            ts("dve", dst[:, f, TP:T], tmp[:, TP:T], mv[:, l, ia, f, 1:2], mv[:, l, ib, f, 1:2], ALU.mult, ALU.add)

    def tail_out(src_fn, nch, w, dst):
        st = view(101376, F32, [128, 1024])
        for j in range(nch):
            tr(bank(2 + j // 4, 128, 128 * (j % 4))[0:w, :], src_fn(j), identf[:])
        for half in range(nch // 4):
            cpa(st[0:w, 512 * half:512 * half + 512], bank(2 + half)[0:w, :])
        dma("sp", dst, st[0:w, 0:nch * 128])

    def residual_pass(l, sub):
        YB = [view(101376, F32, [128, T]), view(105600, F32, [128, T])]
        TMPF = [view(109824, F32, [128, T]), view(114048, F32, [128, T])]
        for i, (lo, hi) in enumerate(tiles3):
            rsqrt_to(BC0[:, lo:hi], bank(5 + i, NT), 1.0 / D)
        ig = 2 if sub == 0 else 5
        for f in range(16):
            yb = YB[f % 2]
            tmp = TMPF[f % 2]
            dma("sp", yb, ys[f][:, :])
            dma("sp", XNEW[:, f, :], xs[f][:, :])
            tt("dve", tmp, yb, BC0[:, 0:T], ALU.mult)
            stt("dve", XNEW[:, f, 0:TP], tmp[:, 0:TP], mv[:, l, ig, f, 0:1], XNEW[:, f, 0:TP], ALU.mult, ALU.add)
            stt("dve", XNEW[:, f, TP:T], tmp[:, TP:T], mv[:, l, ig, f, 1:2], XNEW[:, f, TP:T], ALU.mult, ALU.add)

    def yproj_evac(b, f, lo, hi, i, YF, first, last):
        SQt = [view(141312, BF16, [128, NT]), view(142016, BF16, [128, NT])]
        sq = SQt[(f * 3 + i) % 2]
        cp("act", YF[:, lo:hi], bank(b, NT))
        act(sq, bank(b, NT), AF.Square)
        mm(bank(5 + i, NT), ones[:], sq, first, last)

    def layer(l):
        UAT = view(33792, BF16, [128, 8, 1116])
        ZBT = view(51648, BF16, [128, 8, 1086])
        KTST = view(85920, BF16, [128, 5, T])
        VST = view(96480, BF16, [128, 9, 512])
        WKVR = view(105696, BF16, [128, 16, 576])
        ZQST = view(105696, F32, [128, 4, T])
        UATAIL = view(124128, F32, [128, 8, 60])
        ZBTAIL = view(126048, F32, [128, 8, 30])
        GHB = view(127008, F32, [128, 4, 360])
        ROPET = view(132768, F32, [128, 9, 2, 64])
        GKV = view(137376, F32, [128, 512])
        SCONV = view(139424, F32, [128, 8, 30])
        SPOOL = view(140384, F32, [128, 8, 15])
        JUNK = [view(69024, F32, [128, 512]), view(71072, F32, [128, 512])]
        CKVF = [view(73120, F32, [128, 512]), view(75168, F32, [128, 512])]
        KRU = [view(77216, F32, [128, 64]), view(77472, F32, [128, 64])]
        KRV = [view(77728, F32, [128, 64]), view(77984, F32, [128, 64])]
        KRF = [view(78240, F32, [128, 64]), view(78496, F32, [128, 64])]
        KRB = [view(78752, BF16, [128, 64]), view(78880, BF16, [128, 64])]
        SGT = [view(79008, F32, [128, NT]), view(80416, F32, [128, NT])]
        SQT = [view(81824, BF16, [128, NT]), view(82528, BF16, [128, NT])]

        dma("pool", WKVR.rearrange("p k m -> p (k m)"), gw("wKVR", l, 0))
        dma("sp", ROPET, d_ropeT[:, :, :, :])
        dma("sp", GKV, d_gkv[l])
        dma("pool", KTST[64:128, 4, 0:TP], d_khot[:, :])
        dma("sp", SCONV, d_sconv[l])
        dma("sp", SPOOL, d_spool[l])
        for t9 in range(9):
            rows = 128 if t9 < 8 else 32
            tc0 = t9 * 128
            i2 = t9 % 2
            bx, by = (0, 1) if i2 == 0 else (2, 3)
            psx = bank(bx)[0:rows, :]
            psy = bank(by, 64)[0:rows, :]
            for k in range(16):
                mm(psx, hT[:, k, tc0:tc0 + rows], WKVR[:, k, 0:512], k == 0, k == 15)
            for k in range(16):
                mm(psy, hT[:, k, tc0:tc0 + rows], WKVR[:, k, 512:576], k == 0, k == 15)
            junk = JUNK[i2][0:rows]
            ssk = small[0:rows, 16 + t9:17 + t9]
            act(junk, psx, AF.Square)
            K.rec("dve", lambda e, ssk=ssk, junk=junk: e.reduce_sum(out=ssk, in_=junk, axis=AX.X), [junk], [ssk])
            rsqrt_to(ssk, ssk, 1.0 / 512)
            ckvf = CKVF[i2][0:rows]
            stt("dve", ckvf, psx, ssk, GKV[0:rows], ALU.mult, ALU.mult)
            dma("sp", o_ckv[l, tc0:tc0 + rows, :], ckvf)
            cp("act", VST[0:rows, t9, :], ckvf)
            pst = bank_bf(4 + i2, 0, [128, 4, 128])
            for c in range(4):
                tr(pst[:, c, 0:rows], VST[0:rows, t9, 128 * c:128 * c + 128], identb[0:rows, 0:rows])
            cpa(KTST[:, 0:4, tc0:tc0 + rows], pst[:, :, 0:rows])
            kru, krv, krf, krb = KRU[i2][0:rows], KRV[i2][0:rows], KRF[i2][0:rows], KRB[i2][0:rows]
            tt("dve", kru, psy, ROPET[0:rows, t9, 0, :], ALU.mult)
            tt("dve", krv[:, 0:32], psy[:, 32:64], ROPET[0:rows, t9, 1, 0:32], ALU.mult)
            tt("dve", krv[:, 32:64], psy[:, 0:32], ROPET[0:rows, t9, 1, 32:64], ALU.mult)
            tt("dve", krf, kru, krv, ALU.add)
            dma("sp", o_kr[l, tc0:tc0 + rows, :], krf)
            cp("act", krb, krf)
            pst2 = bank_bf(6 + i2, 0, [128, 128])
            tr(pst2[0:64, 0:rows], krb, identb[0:rows, 0:rows])
            cpa(KTST[0:64, 4, tc0:tc0 + rows], pst2[0:64, 0:rows])
        ck(1.5)
        dma("sp", d_bkv[0][0:384, :].rearrange("(c p) k -> p c k", p=128), KTST[:, 0:3, 0:TP])
        dma("sp", d_bkv[1][0:256, :].rearrange("(c p) k -> p c k", p=128), KTST[:, 3:5, 0:TP])
        dma("sp", d_bkv[1][256:384, :].rearrange("r (x d) -> (r x) d", d=512).rearrange("(t p) d -> p t d", p=128),
            VST[:, 0:2, :])
        dma("sp", d_bkv[2][0:384, :].rearrange("r (x d) -> (r x) d", d=512).rearrange("(t p) d -> p t d", p=128),
            VST[:, 2:8, :])
        cp("dve", KTSN, KTST[:, :, TP:T])
        cp("dve", VSN[0:32, :], VST[0:32, 8, :])
        ck(1.7)
        for i3 in range(3):
            allgather(d_bkv[i3], d_gkvb[i3])

        ck(2)
        pb_set([0, 1, 2, 3, 4, 5, 6, 7])
        for j in range(8):
            wb = w3(wget(f"A{l}"), 16, 256)
            for i, (lo, hi) in enumerate(tiles3):
                bu, bg = pb(), pb()
                for k in range(16):
                    mm(bank(bu, NT), wb[:, k, 0:128], hT[:, k, lo:hi], k == 0, k == 15)
                for k in range(16):
                    mm(bank(bg, NT), wb[:, k, 128:256], hT[:, k, lo:hi], k == 0, k == 15)
                sg = SGT[i % 2]
                act(sg, bank(bg, NT), AF.Sigmoid)
                for (a, b_, dlo) in segs(lo, hi, 30):
                    tt("dve", UAT[:, j, dlo:dlo + (b_ - a)], bank(bu, NT)[:, a:b_], sg[:, a:b_], ALU.mult)
                if i == 2:
                    tt("dve", UATAIL[:, j, 0:30], bank(bu, NT)[:, 290:320], sg[:, 290:320], ALU.mult)
                    tt("dve", UATAIL[:, j, 30:60], bank(bu, NT)[:, 322:352], sg[:, 322:352], ALU.mult)
        for n in range(4):
            wb = w3(wget(f"B{l}"), 16, 256)
            for m in range(2):
                ch = 2 * n + m
                for i, (lo, hi) in enumerate(tiles3):
                    b = pb()
                    for k in range(16):
                        mm(bank(b, NT), wb[:, k, 128 * m:128 * m + 128], hT[:, k, lo:hi], k == 0, k == 15)
                    for (a, b_, dlo) in segs(lo, hi, 15):
                        cpa(ZBT[:, ch, dlo:dlo + (b_ - a)], bank(b, NT)[:, a:b_])
                    if i == 2:
                        cp("dve", ZBTAIL[:, ch, 0:15], bank(b, NT)[:, 305:320])
                        cp("dve", ZBTAIL[:, ch, 15:30], bank(b, NT)[:, 337:352])
        dma("sp", d_bh[:, 0:240].rearrange("p (j t) -> p j t", j=8), UATAIL[:, :, 0:30])
        dma("sp", d_bh[:, 240:360].rearrange("p (j t) -> p j t", j=8), ZBTAIL[:, :, 0:15])
        allgather(d_bh, d_gh)
        tail_out(lambda j: UATAIL[:, j, 0:30], 8, 30, o_conv[l, 0])
        tail_out(lambda j: UATAIL[:, j, 30:60], 8, 30, o_conv[l, 1])
        tail_out(lambda j: ZBTAIL[:, j, 0:15], 8, 15, o_pool[l, 0])
        tail_out(lambda j: ZBTAIL[:, j, 15:30], 8, 15, o_pool[l, 1])

        pb_set([0, 1, 2, 3, 4])
        for n in range(2):
            wb = w3(wget(f"Q{l}"), 16, 256)
            for m in range(2):
                ch = 2 * n + m
                for i, (lo, hi) in enumerate(tiles3):
                    b = pb()
                    for k in range(16):
                        mm(bank(b, NT), wb[:, k, 128 * m:128 * m + 128], hT[:, k, lo:hi], k == 0, k == 15)
                    cp("act", ZQST[:, ch, lo:hi], bank(b, NT))
                    sq = SQT[i % 2]
                    act(sq, bank(b, NT), AF.Square)
                    mm(bank(5 + i, NT), ones[:], sq, ch == 0, ch == 3)
        for i, (lo, hi) in enumerate(tiles3):
            rsqrt_to(BC0[:, lo:hi], bank(5 + i, NT), 1.0 / 512)
        for ch in range(4):
            stt("dve", QLAT[:, ch, :], ZQST[:, ch, :], pfv(l, "gq", 4)[:, ch:ch + 1], BC0[:, 0:T], ALU.mult, ALU.mult)
        dma("sp", d_hsp[:, :], hT.rearrange("p a b -> p (a b)"))
        if l == 0:
            dump("hT", hT.rearrange("p a b -> p (a b)"), BF16, 16 * T)
            dump("qlat", QLAT.rearrange("p a b -> p (a b)"), BF16, 4 * T)
        if l == 0:
            mod_compute(0, 16, 48, "mod0b")
        if l == 0 and STAGE is None:
            mod_compute(1, 0, 48, "mod1")

        ck(3)
        HT = view(69024, F32, [128, 360])
        dma("sp", GHB, d_gh.ap().rearrange("(r p) f -> p r f", p=128))
        ts("dve", HT, GHB[:, 0, :], hsel[:, 0:1], None, ALU.mult)
        for r in range(1, 4):
            stt("dve", HT, GHB[:, r, :], hsel[:, r:r + 1], HT, ALU.mult, ALU.add)
        cp("dve", UAT[:, :, 0:30], HT[:, 0:240].rearrange("p (j t) -> p j t", j=8))
        cp("dve", ZBT[:, :, 0:15], HT[:, 240:360].rearrange("p (j t) -> p j t", j=8))
        cp("dve", UAT[:, :, 1054:1084], SCONV)
        cp("dve", ZBT[:, :, 1039:1054], SPOOL)
        MT = view(120672, BF16, [128, 8, T])
        SAT = view(103776, BF16, [128, 8, T])
        AT = view(69024, F32, [128, 8, 1086])
        P = [view(15872, F32, [128, 2, 1086]), view(24560, F32, [128, 2, 1086])]
        T16 = view(33248, F32, [128, 16])
        for g in range(4):
            src = ZBT[:, 2 * g:2 * g + 2, :]
            w = 2 ** (g + 1)
            cur = src
            for i in range(g + 1):
                st = 2 ** i
                dst = P[i % 2]
                tt("dve", dst[:, :, st:1086], cur[:, :, st:1086], cur[:, :, 0:1086 - st], ALU.add)
                cur = dst
            stt("dve", MT[:, 2 * g:2 * g + 2, 0:TP], cur[:, :, 15:15 + TP], 1.0 / w, src[:, :, 15:15 + TP], ALU.mult, ALU.subtract)
            stt("dve", MT[:, 2 * g:2 * g + 2, TP:T], cur[:, :, 1054:1086], 1.0 / w, src[:, :, 1054:1086], ALU.mult, ALU.subtract)
            for c2 in range(2):
                tt("dve", T16, cur[:, c2, 15:31], icnt[:, g, :], ALU.mult)
                tt("dve", MT[:, 2 * g + c2, 0:16], T16, src[:, c2, 15:31], ALU.subtract)
        DG = [view(0, BF16, [128, 31, 128]), view(7936, BF16, [128, 31, 128])]
        SQc = [view(160320, BF16, [128, 362]), view(161044, BF16, [128, 362])]
        ABc = [view(161768, BF16, [128, 362]), view(162492, BF16, [128, 362])]
        ctiles = [(0, 362), (362, 724), (724, 1086)]
        wd = pfv(l, "wdwa", 248).rearrange("p (j k) -> p j k", j=8)
        for j in range(8):
            dg = DG[j % 2]
            for k in range(31):
                if k % 2 == 0:
                    ts("dve", dg[:, k, :], identb[:], wd[:, j, k:k + 1], None, ALU.mult)
                else:
                    act(dg[:, k, :], identb[:], AF.Copy, scale=wd[:, j, k:k + 1])
            for i, (lo, hi) in enumerate(ctiles):
                b = i if j % 2 == 0 else 3 + i
                b = [0, 1][(3 * j + i) % 2]
                for k in range(31):
                    mm(bank(b, 362), dg[:, k, :], UAT[:, j, lo + k:lo + k + 362], k == 0, k == 30)
                ts("dve", AT[:, j, lo:hi], bank(b, 362), pfv(l, "bdwa", 8)[:, j:j + 1], None, ALU.add)
                sq, ab = SQc[i % 2], ABc[i % 2]
                act(sq, AT[:, j, lo:hi], AF.Square)
                cp("act", ab, AT[:, j, lo:hi])
                mm(bank(2 + i, 362), ones[:], sq, j == 0, j == 7)
                mm(bank(5 + i, 362), ones[:], ab, j == 0, j == 7)
        TMPL = view(0, F32, [128, 1088])
        for i, (lo, hi) in enumerate(ctiles):
            ts("dve", BC1[:, lo:hi], bank(5 + i, 362), 1.0 / 1024, None, ALU.mult)
            tt("dve", TMPL[:, lo:hi], BC1[:, lo:hi], BC1[:, lo:hi], ALU.mult)
            stt("dve", BC0[:, lo:hi], bank(2 + i, 362), 1.0 / 1024, TMPL[:, lo:hi], ALU.mult, ALU.subtract)
            ts("dve", BC0[:, lo:hi], BC0[:, lo:hi], EPS, None, ALU.add)
            act(BC0[:, lo:hi], BC0[:, lo:hi], AF.Sqrt)
            recip(BC0[:, lo:hi], BC0[:, lo:hi])
        for j in range(8):
            tt("dve", AT[:, j, :], AT[:, j, :], BC1[:, 0:1086], ALU.subtract)
            tt("dve", AT[:, j, :], AT[:, j, :], BC0[:, 0:1086], ALU.mult)
            ts("dve", AT[:, j, :], AT[:, j, :], pfv(l, "lng", 8)[:, j:j + 1], pfv(l, "lnb", 8)[:, j:j + 1], ALU.mult, ALU.add)
            act(SAT[:, j, 0:TP], AT[:, j, 0:TP], AF.Silu)
            act(SAT[:, j, TP:T], AT[:, j, 1054:1086], AF.Silu)
        if l == 0:
            dump("sat", SAT.rearrange("p a b -> p (a b)"), BF16, 8 * T)
            dump("mt", MT.rearrange("p a b -> p (a b)"), BF16, 8 * T)
        dma("sp", d_sasp[:, :], SAT.rearrange("p a b -> p (a b)"))
        dma("sp", d_msp[:, :], MT.rearrange("p a b -> p (a b)"))

        ck(4)
        QF = [view(0, BF16, [128, 5, TP]), view(10240, BF16, [128, 5, TP])]
        QH1 = view(20480, BF16, [128, T])
        QH = [QH1, QH1]
        ACCS = view(22592, F32, [128, 512])
        PT = [view(24704 + 1024 * i, BF16, [128, 512]) for i in range(4)]
        ONORM = view(28800, BF16, [128, 4, 512])
        RS = view(32896, F32, [128, 8])
        KT = view(33792, BF16, [128, 5, 4096])
        VV = view(74752, BF16, [128, 32, 512])
        ONT = view(141312, BF16, [128, 4, 512])
        QS = view(145408, BF16, [128, 5, 512])
        RT1 = view(160320, F32, [128, NT])
        RT2 = view(161728, F32, [128, NT])
        dma("pool", QF[0][64:128, 4, :], d_qB[:, :])
        dma("pool", QF[1][64:128, 4, :], d_qB[:, :])
        for g in range(4):
            r0 = 384 * g
            dma("sp", KT[:, 0:3, 1024 * g:1024 * g + 1024], d_gkvb[0][r0:r0 + 384, :].rearrange("(c p) k -> p c k", p=128))
            dma("sp", KT[:, 3:5, 1024 * g:1024 * g + 1024], d_gkvb[1][r0:r0 + 256, :].rearrange("(c p) k -> p c k", p=128))
            dma("sp", VV[:, 8 * g:8 * g + 2, :],
                d_gkvb[1][r0 + 256:r0 + 384, :].rearrange("r (x d) -> (r x) d", d=512).rearrange("(t p) d -> p t d", p=128))
            dma("sp", VV[:, 8 * g + 2:8 * g + 8, :],
                d_gkvb[2][r0:r0 + 384, :].rearrange("r (x d) -> (r x) d", d=512).rearrange("(t p) d -> p t d", p=128))
        UB = 7
        SUMS = bank(6, 8, 0)
        onesf = small[:, 40:41]
        K.rec("dve", lambda e: e.memset(onesf, 1.0), [], [onesf])

        def units(h):
            wb = wget(f"HD{l}")
            hp = h % 2
            wqN = wb[:, 0:512].rearrange("p (k m) -> p k m", k=4)
            wqR = wb[:, 512:768].rearrange("p (k m) -> p k m", k=4)
            wqS = wb[:, 768:1024].rearrange("p (k m) -> p k m", k=4)
            wuk = wb[:, 1024:1536]
            wuv = wb[:, 1536:2048].rearrange("p (k m) -> p k m", k=4)
            us = []
            for i, (lo, hi) in enumerate(tiles3):
                def u1(lo=lo, hi=hi):
                    b = bank(UB, NT)
                    for k in range(4):
                        mm(b, wqN[:, k, :], QLAT[:, k, lo:hi], k == 0, k == 3)
                    cp("act", QH[hp][:, lo:hi], b)
                us.append(u1)
                for rc in range(4):
                    def u2(lo=lo, hi=hi, rc=rc, i=i):
                        b = bank(UB, NT)
                        mm(b, wuk[:, 128 * rc:128 * rc + 128], QH[hp][:, lo:hi], True, True)
                        pe_ = min(hi, TP)
                        cp("act", QF[hp][:, rc, lo:pe_], b[:, 0:pe_ - lo])
                        if i == 2:
                            cp(os.environ.get("DBG_QSE", "dve"), QS[:, rc, 32 * h:32 * h + 32], b[:, 320:352])
                    us.append(u2)

                def u3a(lo=lo, hi=hi):
                    b = bank(UB, NT)[0:64, :]
                    for k in range(4):
                        mm(b, wqR[:, k, :], QLAT[:, k, lo:hi], k == 0, k == 3)
                    tt("dve", RT1[0:64, :], b, ropeF[:, 0, lo:hi], ALU.mult)
                us.append(u3a)

                def u3b(lo=lo, hi=hi, i=i):
                    b = bank(UB, NT)[0:64, :]
                    for k in range(4):
                        mm(b, wqS[:, k, :], QLAT[:, k, lo:hi], k == 0, k == 3)
                    tt("dve", RT2[0:64, :], b, ropeF[:, 1, lo:hi], ALU.mult)
                    pe_ = min(hi, TP)
                    tt("dve", QF[hp][0:64, 4, lo:pe_], RT1[0:64, 0:pe_ - lo], RT2[0:64, 0:pe_ - lo], ALU.add)
                    if i == 2:
                        tt("dve", QS[0:64, 4, 32 * h:32 * h + 32], RT1[0:64, 320:352], RT2[0:64, 320:352], ALU.add)
                us.append(u3b)
            return us, wuv

        def attention(qchunk, ktiles, par, hook):
            n = len(ktiles)

            def pv(kt):
                _, vap, nk = ktiles[kt]
                p_ = PT[kt % 4]
                for s in range(4):
                    mm(bank(2 + s), p_[0:nk, 128 * s:128 * s + 128], vap, kt == 0, kt == n - 1)
                if kt == 0:
                    cp("dve", ACCS[0:nk, :], p_[0:nk, :])
                else:
                    tt("dve", ACCS[0:nk, :], ACCS[0:nk, :], p_[0:nk, :], ALU.add)
            for kt in range(n):
                kfn, _, nk = ktiles[kt]
                sb_ = bank(kt % 2)[0:nk, :]
                for c in range(5):
                    mm(sb_, kfn(c), qchunk(c), c == 0, c == 4)
                if kt >= 2:
                    pv(kt - 2)
                act(PT[kt % 4][0:nk, :], sb_, AF.Exp, scale=ATTN_SCALE)
                hook(kt)
            pv(n - 2)
            pv(n - 1)
            for s in range(4):
                mm(SUMS[:, 4 * par + s:4 * par + s + 1], ACCS[:, 128 * s:128 * s + 128], onesf, True, True)

        def tail_evac(par):
            recip(RS[:, 4 * par:4 * par + 4], SUMS[:, 4 * par:4 * par + 4])
            for s in range(4):
                if s % 2 == 0:
                    ts("dve", ONORM[:, s, :], bank(2 + s), RS[:, 4 * par + s:4 * par + s + 1], None, ALU.mult)
                else:
                    act(ONORM[:, s, :], bank(2 + s), AF.Copy, scale=RS[:, 4 * par + s:4 * par + s + 1])

        def tail_tr():
            tb = bank_bf(7, 0, [128, 4, 128])
            for s in range(4):
                for rc in range(4):
                    tr(tb[:, rc, :], ONORM[:, s, 128 * rc:128 * rc + 128], identb[:])
                cpa(ONT[:, :, 128 * s:128 * s + 128], tb)

        def tail_pe(h, qb, wuv):
            tail_tr()
            for half in range(2):
                wbk = bank(7, 256, 256)
                for rc in range(4):
                    mm(wbk, wuv[:, rc, :], ONT[:, rc, 256 * half:256 * half + 256], rc == 0, rc == 3)
                cpa(OT[:, h, 512 * qb + 256 * half:512 * qb + 256 * half + 256], wbk)

        import os
        NH = int(os.environ.get("DBG_NH", "16"))
        NOS = os.environ.get("DBG_NOS", "") != ""
        ck(4.1)
        wstate["depth"] = 2
        us0, wuv_cur = units(0)
        NU = int(os.environ.get("DBG_NU", "99"))
        for u in us0[:NU]:
            u()
        ck(4.2)
        pend = []
        it = 0
        nxt = {}
        for h in range(NH):
            usn, wuv_next = [], None
            hp = h % 2
            for qb in range(2):
                par = it % 2
                it += 1
                q0 = 512 * qb
                ktl = [((lambda c, kt=kt: KT[:, c, 128 * kt:128 * kt + 128]), VV[:, kt, :], 128) for kt in range(32)]
                upos = [0]

                def hook(kt, qb=qb, h=h):
                    if kt == 4 and pend:
                        pend.pop(0)()
                    if kt == 5 and qb == 0 and h < NH - 1:
                        u_, w_ = units(h + 1)
                        usn.extend(u_)
                        nxt["wuv"] = w_
                    tot = qb * 32 + kt
                    want = (tot * len(usn)) // 60 if usn else 0
                    while usn and upos_g[0] < min(want, len(usn)):
                        usn[upos_g[0]]()
                        upos_g[0] += 1
                if qb == 0:
                    upos_g = [0]
                attention(lambda c: QF[hp][:, c, q0:q0 + 512], ktl, par, hook)
                tail_evac(par)
                pend.append(lambda h=h, qb=qb, wuv=wuv_cur: tail_pe(h, qb, wuv))
            while usn and upos_g[0] < len(usn):
                usn[upos_g[0]]()
                upos_g[0] += 1
            wuv_cur = nxt.get("wuv")
        while pend:
            pend.pop(0)()
        if NOS:
            raise _Stop()
        for g in range(4):
            dma("pool", KT[:, 0:4, 1024 * g:1024 * g + 1024],
                d_cacheT[l, 0:512, 1024 * g:1024 * g + 1024].rearrange("(c p) k -> p c k", p=128))
            dma("pool", KT[0:64, 4, 1024 * g:1024 * g + 1024], d_cacheT[l, 512:576, 1024 * g:1024 * g + 1024])
            dma("pool", VV[:, 8 * g:8 * g + 8, :],
                d_cacheV[l, 1024 * g:1024 * g + 1024, :].rearrange("(t p) d -> p t d", p=128))

        def kf_cache(kt):
            return lambda c: (KT[:, c, 128 * kt:128 * kt + 128] if c < 4 else KT[0:64, 4, 128 * kt:128 * kt + 128])
        ktl = [(kf_cache(kt), VV[:, kt, :], 128) for kt in range(32)]
        ktl.append(((lambda c: (KTSN[:, c, :] if c < 4 else KTSN[0:64, 4, :])), VSN[0:32, :], 32))
        par = it % 2
        attention(lambda c: (QS[:, c, :] if c < 4 else QS[0:64, 4, :]), ktl, par, lambda kt: None)
        tail_evac(par)
        tail_tr()
        for j in range(4):
            wb = wget(f"UV{l}").rearrange("p (h k m) -> p h k m", h=4, k=4)
            for hl in range(4):
                h = 4 * j + hl
                wbk = bank(7, 32, 256)
                for rc in range(4):
                    mm(wbk, wb[:, hl, rc, :], ONT[:, rc, 128 * j + 32 * hl:128 * j + 32 * hl + 32], rc == 0, rc == 3)
                cpa(OT[:, h, TP:T], wbk)

        if l == 0:
            dump("ot", OT.rearrange("p a b -> p (a b)"), BF16, 16 * T)
        ck(5)
        wstate["depth"] = NB
        SAT2 = view(33792, BF16, [128, 8, T])
        MT2 = view(50688, BF16, [128, 8, T])
        MERGED = view(67584, BF16, [128, 16, T])
        dma("sp", hT.rearrange("p a b -> p (a b)"), d_hsp[:, :])
        dma("sp", SAT2.rearrange("p a b -> p (a b)"), d_sasp[:, :])
        dma("sp", MT2.rearrange("p a b -> p (a b)"), d_msp[:, :])
        SG = [view(101376, F32, [128, NT]), view(102784, F32, [128, NT])]
        ACC = [view(141312 + 1408 * i, F32, [128, NT]) for i in range(3)]
        T2 = [view(145536, F32, [128, NT]), view(146944, F32, [128, NT])]
        pb_set([0, 1, 2, 3, 4, 5, 6, 7])
        psc = pfv(l, "psc", 16)
        for f in range(16):
            g4 = f // 4
            for pair in range(3):
                cw = wget(f"M4{l}").rearrange("p (k m) -> p k m", m=128)
                c1 = c2 = c3 = cw
                for i, (lo, hi) in enumerate(tiles3):
                    bo, bg = pb(), pb()
                    if pair == 0:
                        for k in range(8):
                            mm(bank(bo, NT), c1[:, k, :], SAT2[:, k, lo:hi], k == 0, k == 7)
                        gwt, gof = c1, 8
                    elif pair == 1:
                        for k in range(2):
                            mm(bank(bo, NT), c2[:, k, :], MT2[:, 2 * g4 + k, lo:hi], k == 0, k == 1)
                        gwt, gof = c2, 2
                    else:
                        for k in range(16):
                            mm(bank(bo, NT), c3[:, k, :], OT[:, k, lo:hi], k == 0, k == 15)
                        gwt, gof = c3, 16
                    for k in range(16):
                        mm(bank(bg, NT), gwt[:, gof + k, :], hT[:, k, lo:hi], k == 0, k == 15)
                    sg = SG[(pair * 3 + i) % 2]
                    act(sg, bank(bg, NT), AF.Sigmoid)
                    if pair == 0:
                        tt("dve", ACC[i], bank(bo, NT), sg, ALU.mult)
                    elif pair == 1:
                        t2 = T2[i % 2]
                        stt("dve", t2, bank(bo, NT), psc[:, f:f + 1], sg, ALU.mult, ALU.mult)
                        tt("dve", ACC[i], ACC[i], t2, ALU.add)
                    else:
                        t2 = T2[i % 2]
                        tt("dve", t2, bank(bo, NT), sg, ALU.mult)
                        tt("dve", MERGED[:, f, lo:hi], ACC[i], t2, ALU.add)

        if l == 0:
            dump("merged", MERGED.rearrange("p a b -> p (a b)"), BF16, 16 * T)
        ck(6)
        YF = [view(101376, F32, [128, T]), view(105600, F32, [128, T])]
        pb_set([0, 1, 2, 3, 4])
        for n in range(8):
            wb = w3(wget(f"OUT{l}"), 16, 256)
            for m in range(2):
                f = 2 * n + m
                yf = YF[f % 2]
                for i, (lo, hi) in enumerate(tiles3):
                    b = pb()
                    for k in range(16):
                        mm(bank(b, NT), wb[:, k, 128 * m:128 * m + 128], MERGED[:, k, lo:hi], k == 0, k == 15)
                    yproj_evac(b, f, lo, hi, i, yf, f == 0, f == 15)
                dma("sp", ys[f][:, :], yf)
        residual_pass(l, 0)
        if l == 0:
            dump("xnew", XNEW.rearrange("p a b -> p (a b)"), F32, 16 * T)
        prenorm(l, 1, hT)

        ck(7)
        ACTT = view(33792, BF16, [128, NFF, T])
        UPG = [view(126720, F32, [128, 1060]), view(130960, F32, [128, 1060])]
        UPV = [view(135200, F32, [128, T]), view(139424, F32, [128, T])]
        CV = view(143648, F32, [128, 1060])
        SFFN = view(160320, F32, [128, 2, NFF])
        PG01 = view(160672, F32, [128, 2, NFF])
        PV01 = view(161024, F32, [128, 2, NFF])
        BH3 = view(161376, F32, [128, 2, NFF])
        STL = view(161728, F32, [128, 2, NFF])
        H3 = view(162080, F32, [128, 2, NFF])
        GH3B = view(147888, F32, [128, 4, 88])
        C01 = view(149296, F32, [128, 2, NFF])
        TQ = view(149648, F32, [128, NFF])
        dma("sp", SFFN, d_sffn[l])
        for u in UPG:
            K.rec("dve", lambda e, u=u: e.memset(u[:, 0:2], 0.0), [], [u[:, 0:2]])
        wf = pfv(l, "wdwf", 132).rearrange("p (j k) -> p j k", j=NFF)
        bf_ = pfv(l, "bdwf", NFF)
        pb_set([0, 1, 2, 3, 4, 5, 6, 7])
        for j in range(NFF):
            wb = w3(wget(f"UP{l}"), 16, 256)
            upg, upv = UPG[j % 2], UPV[j % 2]
            cp("dve", upg[:, 1026:1028], SFFN[:, :, j])
            for i, (lo, hi) in enumerate(tiles3):
                bg, bv = pb(), pb()
                for k in range(16):
                    mm(bank(bg, NT), wb[:, k, 0:128], hT[:, k, lo:hi], k == 0, k == 15)
                for k in range(16):
                    mm(bank(bv, NT), wb[:, k, 128:256], hT[:, k, lo:hi], k == 0, k == 15)
                for (a, b_, dlo) in segs(lo, hi, 2):
                    cp("act", upg[:, dlo:dlo + (b_ - a)], bank(bg, NT)[:, a:b_])
                cp("act", upv[:, lo:hi], bank(bv, NT))
            cp("dve", PG01[:, :, j], upg[:, 2:4])
            cp("dve", PV01[:, :, j], upv[:, 0:2])
            cp("dve", BH3[:, :, j], upg[:, 1024:1026])
            cp("dve", STL[:, :, j], upg[:, 1058:1060])
            ts("dve", CV[:, 0:1058], upg[:, 0:1058], wf[:, j, 0:1], bf_[:, j:j + 1], ALU.mult, ALU.add)
            stt("dve", CV[:, 0:1058], upg[:, 1:1059], wf[:, j, 1:2], CV[:, 0:1058], ALU.mult, ALU.add)
            stt("dve", CV[:, 0:1058], upg[:, 2:1060], wf[:, j, 2:3], CV[:, 0:1058], ALU.mult, ALU.add)
            act(CV[:, 0:1058], CV[:, 0:1058], AF.Silu)
            tt("dve", ACTT[:, j, 0:TP], CV[:, 0:TP], upv[:, 0:TP], ALU.mult)
            tt("dve", ACTT[:, j, TP:T], CV[:, 1026:1058], upv[:, TP:T], ALU.mult)
        dma("sp", d_bh3[:, :].rearrange("p (t j) -> p t j", t=2), BH3)
        allgather(d_bh3, d_gh3)
        stf = view(126720, F32, [128, 128])
        for which, src in ((0, BH3), (1, STL)):
            tr(bank(0, 128)[0:88, :], src.rearrange("p t j -> p (t j)"), identf[:])
            cp("dve", stf[0:88, :], bank(0, 128)[0:88, :])
            for t2_ in range(2):
                dma("sp", o_ffn[l, which, t2_, :].rearrange("(j p) -> j p", p=128), stf[44 * t2_:44 * t2_ + 44, :])

        def patch():
            dma("sp", GH3B, d_gh3.ap().rearrange("(r p) f -> p r f", p=128))
            h3f = H3.rearrange("p t j -> p (t j)")
            ts("dve", h3f, GH3B[:, 0, :], hsel[:, 0:1], None, ALU.mult)
            for r in range(1, 4):
                stt("dve", h3f, GH3B[:, r, :], hsel[:, r:r + 1], h3f, ALU.mult, ALU.add)
            w0, w1, w2 = wf[:, :, 0], wf[:, :, 1], wf[:, :, 2]
            h0, h1 = H3[:, 0, :], H3[:, 1, :]
            g0, g1 = PG01[:, 0, :], PG01[:, 1, :]
            c0, c1_ = C01[:, 0, :], C01[:, 1, :]
            tt("dve", c0, w0, h0, ALU.mult)
            tt("dve", TQ, w1, h1, ALU.mult)
            tt("dve", c0, c0, TQ, ALU.add)
            tt("dve", TQ, w2, g0, ALU.mult)
            tt("dve", c0, c0, TQ, ALU.add)
            tt("dve", c0, c0, bf_, ALU.add)
            tt("dve", c1_, w0, h1, ALU.mult)
            tt("dve", TQ, w1, g0, ALU.mult)
            tt("dve", c1_, c1_, TQ, ALU.add)
            tt("dve", TQ, w2, g1, ALU.mult)
            tt("dve", c1_, c1_, TQ, ALU.add)
            tt("dve", c1_, c1_, bf_, ALU.add)
            act(C01, C01, AF.Silu)
            tt("dve", C01, C01, PV01, ALU.mult)
            cp("dve", ACTT[:, :, 0:2].rearrange("p j t -> p t j"), C01)

        ck(8)
        YF = [view(126720, F32, [128, T]), view(130944, F32, [128, T])]
        pb_set([0, 1, 2, 3, 4])
        wstate["depth"] = 2
        for f in range(16):
            hb0 = wget(f"DN{l}").rearrange("p (k m) -> p k m", m=128)
            hb1 = wget(f"DN{l}").rearrange("p (k m) -> p k m", m=128)
            yf = YF[f % 2]
            for i in (1, 2, 0):
                lo, hi = tiles3[i]
                if f == 0 and i == 0:
                    patch()
                b = pb()
                for k in range(NFF):
                    hb = hb0 if k < 22 else hb1
                    mm(bank(b, NT), hb[:, k % 22, :], ACTT[:, k, lo:hi], k == 0, k == NFF - 1)
                yproj_evac(b, f, lo, hi, i, yf, f == 0, f == 15)
            dma("sp", ys[f][:, :], yf)
        wstate["depth"] = NB
        residual_pass(l, 1)
        ck(9)

    def final_out():
        pb_set([0, 1, 2, 3])
        for t9 in range(9):
            rows = 128 if t9 < 8 else 32
            tc0 = t9 * 128
            tok = TOK[t9 % 2]
            for g in range(4):
                b = pb()
                for j in range(4):
                    f = 4 * g + j
                    tr(bank(b, 128, 128 * j)[0:rows, :], XNEW[:, f, tc0:tc0 + rows], identf[:])
                cpa(tok[0:rows, 512 * g:512 * g + 512], bank(b)[0:rows, :])
            dma("sp", o_y[tc0:tc0 + rows, :], tok[0:rows, :])

    try:
        prenorm(0, 0, hT)
        ck(1)
        for l in range(L):
            layer(l)
            if l + 1 < L:
                prenorm(l + 1, 0, hT)
        final_out()
    except _Stop:
        if STAGE == 0:
            final_out()

    K.emit()
    es.close()
    nc._ext_in = set(K.ext_in)
    return nc


def _blk(w, cols):
    K_ = w.shape[0]
    kc = K_ // 128
    out = []
    for c in cols:
        sub = w[:, c]
        out.append(sub.reshape(kc, 128, len(c)).transpose(1, 0, 2).reshape(128, kc * len(c)))
    return np.ascontiguousarray(np.stack(out))


def prep_shared(inp):
    f = np.float32
    sh = {}
    w_in = inp["w_in"]
    ar = np.arange
    sh["wmod"] = np.stack([_blk(inp["w_mod"][l], [ar(256 * n, 256 * n + 256) for n in range(48)]) for l in range(L)])
    sh["wA"] = np.stack([_blk(w_in[l], [np.concatenate([ar(128 * j, 128 * j + 128), ar(1024 + 128 * j, 1024 + 128 * j + 128)])
                                       for j in range(8)]) for l in range(L)])
    sh["wB"] = np.stack([_blk(w_in[l], [ar(OFF_B + 256 * n, OFF_B + 256 * n + 256) for n in range(4)]) for l in range(L)])
    sh["wQ"] = np.stack([_blk(w_in[l], [ar(OFF_Q + 256 * n, OFF_Q + 256 * n + 256) for n in range(2)]) for l in range(L)])
    sh["wKVR"] = np.stack([_blk(w_in[l], [ar(OFF_KV, OFF_G)])[0] for l in range(L)])
    whd = np.zeros((L, 16, 128, 2048), f)
    wuvp = np.zeros((L, 4, 128, 4, 512), f)
    for l in range(L):
        wuq = inp["w_uq"][l].reshape(4, 128, 16, 192)
        wuk = inp["w_uk"][l]
        wuv = inp["w_uv"][l].reshape(4, 128, 16, 128)
        for h in range(16):
            qn = wuq[:, :, h, 0:128].transpose(1, 0, 2).reshape(128, 512)
            qr = wuq[:, :, h, 128:192]
            qs = np.concatenate([qr[..., 32:64], qr[..., 0:32]], axis=-1)
            whd[l, h, :, 0:512] = qn
            whd[l, h, :, 512:768] = qr.transpose(1, 0, 2).reshape(128, 256)
            whd[l, h, :, 768:1024] = qs.transpose(1, 0, 2).reshape(128, 256)
            whd[l, h, :, 1024:1536] = wuk[:, h, :].T
            uv = wuv[:, :, h, :].transpose(1, 0, 2).reshape(128, 512)
            whd[l, h, :, 1536:2048] = uv
            wuvp[l, h // 4, :, h % 4, :] = uv
    sh["wHD"] = whd
    sh["wUV"] = wuvp.reshape(L, 4, 128, 2048)
    wm4 = np.zeros((L, 16, 128, 9472), f)
    for l in range(L):
        pa = inp["w_pa"][l].reshape(8, 128, 16, 128)
        oc = inp["w_oc"][l].reshape(16, 128, 16, 128)
        pool = inp["w_pool"][l].reshape(4, 2, 128, 4, 128)
        wg = w_in[l][:, OFF_G:].reshape(16, 128, 3, 16, 128)
        for fch in range(16):
            o = 0
            blkA = pa[:, :, fch, :].transpose(1, 0, 2).reshape(128, 1024)
            wm4[l, fch, :, 0:1024] = blkA
            wm4[l, fch, :, 1024:3072] = wg[:, :, 0, fch, :].transpose(1, 0, 2).reshape(128, 2048)
            wm4[l, fch, :, 3072:3328] = pool[fch // 4, :, :, fch % 4, :].transpose(1, 0, 2).reshape(128, 256)
            wm4[l, fch, :, 3328:5376] = wg[:, :, 1, fch, :].transpose(1, 0, 2).reshape(128, 2048)
            wm4[l, fch, :, 5376:7424] = oc[:, :, fch, :].transpose(1, 0, 2).reshape(128, 2048)
            wm4[l, fch, :, 7424:9472] = wg[:, :, 2, fch, :].transpose(1, 0, 2).reshape(128, 2048)
    sh["wM4"] = wm4
    sh["wOUT"] = np.stack([_blk(inp["w_out"][l], [ar(256 * n, 256 * n + 256) for n in range(8)]) for l in range(L)])
    sh["wUP"] = np.stack([_blk(inp["w_up"][l], [np.concatenate([ar(128 * j, 128 * j + 128), ar(DFF + 128 * j, DFF + 128 * j + 128)])
                                              for j in range(NFF)]) for l in range(L)])
    wdn = np.zeros((L, 32, 128, 2816), f)
    for l in range(L):
        wd = inp["w_down"][l].reshape(2, 22, 128, 16, 128)
        for fch in range(16):
            for half in range(2):
                wdn[l, 2 * fch + half] = wd[half, :, :, fch, :].transpose(1, 0, 2).reshape(128, 2816)
    sh["wDN"] = wdn
    pfa = np.zeros((128, L * PFL), f)
    for l in range(L):
        def put(name, arr):
            pfa[:, l * PFL + PF[name]:l * PFL + PF[name] + arr.shape[1]] = arr
        fm = lambda v: v.reshape(-1, 128).T
        put("gpm", fm(inp["g_pre_mix"][l])); put("gpo", fm(inp["g_post_mix"][l]))
        put("gpf", fm(inp["g_pre_ffn"][l])); put("gpof", fm(inp["g_post_ffn"][l]))
        put("psc", fm(inp["pool_scale"][l])); put("bdwa", fm(inp["b_dwa"][l]))
        put("lng", fm(inp["ln_a_g"][l])); put("lnb", fm(inp["ln_a_b"][l]))
        put("gq", fm(inp["g_q_lat"][l])); put("bdwf", fm(inp["b_dwf"][l]))
        put("bmod", fm(inp["b_mod"][l]))
        put("wdwa", inp["w_dwa"][l].reshape(31, 8, 128).transpose(2, 1, 0).reshape(128, 248))
        put("wdwf", inp["w_dwf"][l].reshape(3, NFF, 128).transpose(2, 1, 0).reshape(128, 132))
    sh["pf"] = pfa
    sh["gkv"] = np.ascontiguousarray(np.broadcast_to(inp["g_kv_lat"][:, None, :], (L, 128, 512))).astype(f)
    return sh


def prep_core(inp, c):
    f = np.float32
    b, r = c // 4, c % 4
    m = {}
    m["xin"] = np.ascontiguousarray(np.concatenate([inp["x_prompt"][b, 1024 * r:1024 * r + 1024], inp["x_sample"][c]], axis=0))
    cv = np.stack([inp["c_prompt"][b], inp["c_sample"][c]], axis=0)
    m["cT"] = np.ascontiguousarray(cv.reshape(2, 16, 128).transpose(2, 1, 0))
    pos = np.concatenate([1024 * r + np.arange(1024), 4096 + np.arange(32)]).astype(f)
    inv = (np.float32(10000.0) ** (-np.arange(32, dtype=f) / np.float32(32))).astype(f)
    ang = (pos[:, None] * inv[None, :]).astype(f)
    cos, sin = np.cos(ang).astype(f), np.sin(ang).astype(f)
    ropeF = np.zeros((64, 2, T), f)
    ropeF[0:32, 0] = cos.T; ropeF[32:64, 0] = cos.T
    ropeF[0:32, 1] = -sin.T; ropeF[32:64, 1] = sin.T
    m["ropeF"] = ropeF
    ropeT = np.zeros((128, 9, 2, 64), f)
    cc2 = np.concatenate([cos, cos], axis=1)
    ss2 = np.concatenate([-sin, sin], axis=1)
    pad = np.zeros((9 * 128, 64), f)
    pad[:T] = cc2
    ropeT[:, :, 0, :] = pad.reshape(9, 128, 64).transpose(1, 0, 2)
    pad = np.zeros((9 * 128, 64), f)
    pad[:T] = ss2
    ropeT[:, :, 1, :] = pad.reshape(9, 128, 64).transpose(1, 0, 2)
    m["ropeT"] = ropeT
    qch = (1024 * r + np.arange(1024)) // 64
    cidx = np.arange(64)[:, None]
    m["qB"] = np.where(cidx > qch[None, :], -30000.0, 0.0).astype(f)
    m["khot"] = (cidx == qch[None, :]).astype(f)
    hs = np.zeros((128, 4), f)
    if r > 0:
        hs[:, r - 1] = 1.0
    m["hsel"] = hs
    ic = np.zeros((128, 4, 16), f)
    for g, w in enumerate((2, 4, 8, 16)):
        ic[:, g, :] = 1.0 / np.minimum(w, 1024 * r + np.arange(16) + 1).astype(f)
    m["icnt"] = ic
    m["sconv"] = np.ascontiguousarray(inp["state_conv"][:, c].reshape(L, 30, 8, 128).transpose(0, 3, 2, 1))
    m["spool"] = np.ascontiguousarray(inp["state_pool"][:, c].reshape(L, 15, 8, 128).transpose(0, 3, 2, 1))
    m["sffn"] = np.ascontiguousarray(inp["state_ffn"][:, c].reshape(L, 2, NFF, 128).transpose(0, 3, 1, 2))
    m["cacheT"] = np.ascontiguousarray(np.concatenate([inp["cache_ckv"][:, c].transpose(0, 2, 1),
                                                       inp["cache_krope"][:, c].transpose(0, 2, 1)], axis=1))
    m["cacheV"] = np.ascontiguousarray(inp["cache_ckv"][:, c])
    return m


WNAMES = ("wmod", "wA", "wB", "wQ", "wKVR", "wHD", "wUV", "wM4", "wOUT", "wUP", "wDN")


def shard_shared(sh, c):
    m = {"pf": sh["pf"], "gkv": sh["gkv"]}
    for name in WNAMES:
        a = sh[name]
        E = a.shape[-1]
        a2 = a.reshape(L, -1, E)
        for l in range(L):
            m[f"{name}{l}"] = a2[l]
    return m


_NC = None


def kernel(**inputs):
    global _NC
    inp = {k: np.asarray(v, dtype=np.float32) for k, v in inputs.items()}
    if _NC is None:
        _NC = build_program()
    nc = _NC
    sh = prep_shared(inp)
    in_maps = []
    for c in range(8):
        m = prep_core(inp, c)
        m.update(shard_shared(sh, c))
        m = {k: v for k, v in m.items() if k in nc._ext_in}
        in_maps.append(m)
    res = run_bass_kernel_spmd(nc, in_maps, core_ids=list(range(8)))
    R = res.results
    f = np.float32
    y_p = np.zeros((2, 4096, D), f); y_s = np.zeros((8, 32, D), f)
    p_ckv = np.zeros((L, 2, 4096, 512), f); p_kr = np.zeros((L, 2, 4096, 64), f)
    p_conv = np.zeros((L, 2, 30, 1024), f); p_pool = np.zeros((L, 2, 15, 1024), f); p_ffn = np.zeros((L, 2, 2, DFF), f)
    s_ckv = np.zeros((L, 8, 32, 512), f); s_kr = np.zeros((L, 8, 32, 64), f)
    s_conv = np.zeros((L, 8, 30, 1024), f); s_pool = np.zeros((L, 8, 15, 1024), f); s_ffn = np.zeros((L, 8, 2, DFF), f)
    for c in range(8):
        b, r = c // 4, c % 4
        o = R[c]
        y_p[b, 1024 * r:1024 * r + 1024] = o["o_y"][:1024]
        y_s[c] = o["o_y"][1024:]
        p_ckv[:, b, 1024 * r:1024 * r + 1024] = o["o_ckv"][:, :1024]
        s_ckv[:, c] = o["o_ckv"][:, 1024:]
        p_kr[:, b, 1024 * r:1024 * r + 1024] = o["o_kr"][:, :1024]
        s_kr[:, c] = o["o_kr"][:, 1024:]
        if r == 3:
            p_conv[:, b] = o["o_conv"][:, 0]
            p_pool[:, b] = o["o_pool"][:, 0]
            p_ffn[:, b] = o["o_ffn"][:, 0]
        s_conv[:, c] = o["o_conv"][:, 1]
        s_pool[:, c] = o["o_pool"][:, 1]
        s_ffn[:, c] = o["o_ffn"][:, 1]
    return (y_p, y_s, p_ckv, p_kr, p_conv, p_pool, p_ffn, s_ckv, s_kr, s_conv, s_pool, s_ffn)
```

```python
import numpy as np
from contextlib import ExitStack
import concourse.bass as bass
import concourse.mybir as mybir
from concourse.bass_utils import run_bass_kernel_spmd

F32 = mybir.dt.float32
BF16 = mybir.dt.bfloat16
AF = mybir.ActivationFunctionType
ALU = mybir.AluOpType
AX = mybir.AxisListType

L = 2
D = 2048
T = 1056
TP = 1024
TS = 32
NT = 352
DFF = 5632
NFF = 44
EPS = 1e-6
ATTN_SCALE = 192.0 ** -0.5
OFF_A, OFF_B, OFF_Q, OFF_KV, OFF_R, OFF_G = 0, 2048, 3072, 3584, 4096, 4160
PFL = 628
PF = dict(gpm=0, gpo=16, gpf=32, gpof=48, psc=64, bdwa=80, lng=88, lnb=96, gq=104, bdwf=108, bmod=152,
          wdwa=248, wdwf=496)
WCAP = 4096
NB = 3
ARENA = 163840


def dsize(dt):
    s = str(dt)
    if "64" in s:
        return 8
    if "32" in s:
        return 4
    if "16" in s:
        return 2
    return 1


class Op:
    __slots__ = ("eng", "seq", "fn", "waits", "kind", "gid", "qidx")


class KB:
    COMPUTE = ("pe", "act", "dve", "pool")
    EPOCH = 30000
    ND = {"sp": 16, "pool": 8}

    def __init__(self, nc):
        self.nc = nc
        self.ops = {e: [] for e in ("pe", "act", "dve", "pool", "sp")}
        self.state = {}
        self.ext_in = set()
        self.ext_out = set()
        self.known = {e: {} for e in self.ops}
        self.known_d = {e: set() for e in self.ops}
        self.needed = set()
        self.dmaq = {"sp": [], "pool": []}
        self.ccs = []
        self.gcount = 0
        self.allops = []

    def _span(self, ap):
        name = ap.tensor.name
        dsz = dsize(ap.dtype)
        dims = ap.ap
        off = ap.offset
        if name.startswith("sb") or name.startswith("ps"):
            row = dims[0][0]
            col = off % row if row > 0 else off
            ext = 1
            for st, cnt in dims[1:]:
                ext += (cnt - 1) * abs(st)
            page = 256 if name.startswith("sb") else 2048
            lo = col * dsz
            hi = (col + ext) * dsz - 1
        else:
            ext = 1
            for st, cnt in dims:
                ext += (cnt - 1) * abs(st)
            page = 65536
            lo = off * dsz
            hi = (off + ext) * dsz - 1
        return name, lo // page, hi // page

    def rec(self, eng, fn, ins, outs, kind="c"):
        op = Op()
        op.eng = eng
        op.fn = fn
        op.kind = kind
        op.seq = len(self.ops[eng])
        op.gid = self.gcount
        self.gcount += 1
        deps = set()
        for ap in ins:
            name, p0, p1 = self._span(ap)
            if name in self.ext_in:
                continue
            st = self.state.setdefault(name, {})
            isps = name.startswith("ps")
            for p in range(p0, p1 + 1):
                s = st.get(p)
                if s is not None and s[0] is not None:
                    deps.add(s[0])
                if isps and s is not None:
                    for e2, rop in s[1].items():
                        if e2 != eng:
                            deps.add(rop)
        for ap in outs:
            name, p0, p1 = self._span(ap)
            if name in self.ext_out:
                continue
            st = self.state.setdefault(name, {})
            for p in range(p0, p1 + 1):
                s = st.get(p)
                if s is not None:
                    if s[0] is not None:
                        deps.add(s[0])
                    deps.update(s[1].values())
                    deps.update(s[2])
        need_c = {}
        need_d = []
        for d in deps:
            if d.kind == "c":
                if eng == "pe" and d.eng == "pe":
                    continue
                if need_c.get(d.eng, -1) < d.seq:
                    need_c[d.eng] = d.seq
            else:
                need_d.append(d)
        waits = []
        kn = self.known[eng]
        for pe, sq in need_c.items():
            if kn.get(pe, -1) >= sq:
                continue
            kn[pe] = sq
            waits.append(("c", pe, sq))
            self.needed.add((pe, sq))
        kd = self.known_d[eng]
        for d in need_d:
            if d.gid in kd:
                continue
            kd.add(d.gid)
            waits.append(("d", d))
        op.waits = waits
        for ap in ins:
            name, p0, p1 = self._span(ap)
            if name in self.ext_in:
                continue
            st = self.state[name]
            for p in range(p0, p1 + 1):
                s = st.get(p)
                if s is None:
                    s = [None, {}, []]
                    st[p] = s
                if kind == "c":
                    s[1][eng] = op
                else:
                    s[2].append(op)
        for ap in outs:
            name, p0, p1 = self._span(ap)
            if name in self.ext_out:
                continue
            st = self.state[name]
            for p in range(p0, p1 + 1):
                st[p] = [op, {}, []]
        self.ops[eng].append(op)
        if kind == "d":
            op.qidx = len(self.dmaq[eng])
            self.dmaq[eng].append(op)
        elif kind == "cc":
            op.qidx = len(self.ccs)
            self.ccs.append(op)
        return op

    def emit(self):
        nc = self.nc
        with ExitStack() as es:
            signo = {}
            cnt = {}
            for e in self.COMPUTE:
                n = 0
                for op in self.ops[e]:
                    if op.kind == "c" and (e, op.seq) in self.needed:
                        signo[(e, op.seq)] = n
                        n += 1
                cnt[e] = n
            csem = {}
            for e in self.COMPUTE:
                ne = cnt[e] // self.EPOCH + 1
                csem[e] = [es.enter_context(nc.semaphore(f"c_{e}_{i}")) for i in range(ne)]
            dsem = {q: [es.enter_context(nc.semaphore(f"d_{q}_{i}")) for i in range(self.ND[q])] for q in self.dmaq}
            ccsem = [es.enter_context(nc.semaphore(f"cc_{i}")) for i in range(len(self.ccs))]

            def ev(w):
                if w[0] == "c":
                    n = signo[(w[1], w[2])]
                    return csem[w[1]][n // self.EPOCH], n % self.EPOCH + 1
                d = w[1]
                if d.kind == "cc":
                    return ccsem[d.qidx], 1
                nd = self.ND[d.eng]
                return dsem[d.eng][d.qidx % nd], 16 * (d.qidx // nd + 1)

            block = es.enter_context(nc.Block())

            def run(ename, e):
                for op in self.ops[ename]:
                    for w in op.waits:
                        s, v = ev(w)
                        e.wait_ge(s, v)
                    if op.kind == "d":
                        nd = self.ND[ename]
                        if op.qidx >= nd:
                            e.wait_ge(dsem[ename][op.qidx % nd], 16 * (op.qidx // nd))
                        op.fn(e).then_inc(dsem[ename][op.qidx % nd], 16)
                    elif op.kind == "cc":
                        op.fn(e).then_inc(ccsem[op.qidx])
                    else:
                        ins = op.fn(e)
                        key = (ename, op.seq)
                        if key in signo:
                            n = signo[key]
                            ins.then_inc(csem[ename][n // self.EPOCH], 1)
                if ename in self.dmaq:
                    nd = self.ND[ename]
                    q = self.dmaq[ename]
                    for i in range(min(nd, len(q))):
                        last = ((len(q) - 1 - i) // nd) * nd + i
                        e.wait_ge(dsem[ename][i], 16 * (last // nd + 1))
                if ename == "pool":
                    for i in range(len(self.ccs)):
                        e.wait_ge(ccsem[i], 1)

            @block.tensor
            def _(e):
                run("pe", e)

            @block.scalar
            def _(e):
                run("act", e)

            @block.vector
            def _(e):
                run("dve", e)

            @block.gpsimd
            def _(e):
                run("pool", e)

            @block.sync
            def _(e):
                run("sp", e)


class _Stop(Exception):
    pass


STAGE = None


def build_program():
    nc = bass.Bass("TRN2", target_bir_lowering=False)
    K = KB(nc)

    def ck(n):
        if STAGE is not None and n > STAGE:
            raise _Stop()

    import os
    DUMP = os.environ.get("DBG_DUMP", "") != ""

    def dump(name, ap2d, dt, ncols):
        if not DUMP:
            return
        K.ext_out.add("dbg_" + name)
        t_ = nc.dram_tensor("dbg_" + name, [128, ncols], dt, kind="ExternalOutput")
        K.rec("sp", lambda e: e.dma_start(out=t_[:, :], in_=ap2d), [ap2d], [t_[:, :]], kind="d")

    def din(name, shape, dt=F32):
        K.ext_in.add(name)
        return nc.dram_tensor(name, list(shape), dt, kind="ExternalInput")

    def dout(name, shape):
        K.ext_out.add(name)
        return nc.dram_tensor(name, list(shape), F32, kind="ExternalOutput")

    def dscr(name, shape, dt=F32):
        return nc.dram_tensor(name, list(shape), dt)

    d_xin = din("xin", [T, D])
    d_cT = din("cT", [128, 16, 2])
    d_pf = din("pf", [128, L * PFL])
    d_gkv = din("gkv", [L, 128, 512])
    d_ropeF = din("ropeF", [64, 2, T])
    d_ropeT = din("ropeT", [128, 9, 2, 64])
    d_qB = din("qB", [64, TP])
    d_khot = din("khot", [64, TP])
    d_hsel = din("hsel", [128, 4])
    d_icnt = din("icnt", [128, 4, 16])
    d_sconv = din("sconv", [L, 128, 8, 30])
    d_spool = din("spool", [L, 128, 8, 15])
    d_sffn = din("sffn", [L, 128, 2, NFF])
    if STAGE is None or STAGE >= 4:
        d_cacheT = din("cacheT", [L, 576, 4096])
        d_cacheV = din("cacheV", [L, 4096, 512])
    WSPEC = [("wmod", 48, 4096), ("wA", 8, 4096), ("wB", 4, 4096), ("wQ", 2, 4096), ("wKVR", 1, 9216),
             ("wHD", 16, 2048), ("wUV", 4, 2048), ("wM4", 16, 9472), ("wOUT", 8, 4096), ("wUP", NFF, 4096),
             ("wDN", 32, 2816)]
    FIRST = {"wmod": 0, "wKVR": 1, "wA": 2, "wB": 2, "wQ": 2, "wHD": 4, "wUV": 4, "wM4": 5, "wOUT": 6, "wUP": 7, "wDN": 8}
    w_full = {}
    for name, nblk, E in WSPEC:
        w_full[name] = []
        for l in range(L):
            need = STAGE is None or (l == 0 and STAGE >= FIRST[name])
            if need:
                w_full[name].append(din(f"{name}{l}", [nblk * 128, E]))
            else:
                w_full[name].append(nc.dram_tensor(f"d_wf_{name}{l}", [nblk * 128, E], F32))

    def gw(name, l, n):
        return w_full[name][l][n * 128:(n + 1) * 128, :]
    o_y = dout("o_y", [T, D])
    o_ckv = dout("o_ckv", [L, T, 512])
    o_kr = dout("o_kr", [L, T, 64])
    o_conv = dout("o_conv", [L, 2, 30, 1024])
    o_pool = dout("o_pool", [L, 2, 15, 1024])
    o_ffn = dout("o_ffn", [L, 2, 2, DFF])
    xs = [dscr(f"d_xs{f}", [128, T]) for f in range(16)]
    ys = [dscr(f"d_ys{f}", [128, T]) for f in range(16)]
    d_hsp = dscr("d_hsp", [128, 16 * T], BF16)
    d_sasp = dscr("d_sasp", [128, 8 * T], BF16)
    d_msp = dscr("d_msp", [128, 8 * T], BF16)
    d_bkv = [dscr(f"d_bkv{i}", [384, 1024], BF16) for i in range(3)]
    d_gkvb = [dscr(f"d_gkvb{i}", [1536, 1024], BF16) for i in range(3)]
    d_bh = dscr("d_bh", [128, 360])
    d_gh = dscr("d_gh", [512, 360])
    d_bh3 = dscr("d_bh3", [128, 88])
    d_gh3 = dscr("d_gh3", [512, 88])

    es = ExitStack()
    sbt = lambda name, shape, dt: es.enter_context(nc.sbuf_tensor(name, list(shape), dt))
    identf = sbt("sb_identf", [128, 128], F32)
    identb = sbt("sb_identb", [128, 128], BF16)
    ones = sbt("sb_ones", [128, 128], BF16)
    pf = sbt("sb_pf", [128, L * PFL], F32)
    modT = sbt("sb_mod", [128, L, 96, 2], F32)
    mv = sbt("sb_mv", [128, L, 6, 16, 2], F32)
    ropeF = sbt("sb_ropeF", [64, 2, T], F32)
    hsel = sbt("sb_hsel", [128, 4], F32)
    icnt = sbt("sb_icnt", [128, 4, 16], F32)
    cTf = sbt("sb_cTf", [128, 16, 2], F32)
    cTb = sbt("sb_cTb", [128, 16, 2], BF16)
    small = sbt("sb_small", [128, 64], F32)
    wbuf = sbt("sb_wbuf", [128, NB, WCAP], BF16)
    mbuf = sbt("sb_mbuf", [128, 2048], BF16)
    arena = sbt("sb_arena", [128, ARENA // 4], F32)
    ps = es.enter_context(nc.psum_tensor("ps_all", [128, 4096], F32))

    def view(off, dt, shape):
        dsz = dsize(dt)
        n = int(np.prod(shape[1:]))
        nbytes = n * dsz
        assert off % 4 == 0 and nbytes % 4 == 0 and off + nbytes <= ARENA, (off, shape)
        a = arena[:, off // 4:(off + nbytes) // 4]
        if dt == BF16:
            a = a.bitcast(BF16)
        if len(shape) == 3:
            a = a.rearrange("p (a b) -> p a b", a=shape[1])
        elif len(shape) == 4:
            a = a.rearrange("p (a b c) -> p a b c", a=shape[1], b=shape[2])
        if shape[0] != 128:
            a = a[0:shape[0]]
        return a

    def bank(b, n=512, lo=0):
        return ps[:, 512 * b + lo:512 * b + lo + n]

    def bank_bf(b, lo_bytes, shape):
        n = int(np.prod(shape[1:]))
        a = ps[:, 512 * b + lo_bytes // 4:512 * b + lo_bytes // 4 + n // 2].bitcast(BF16)
        if len(shape) == 3:
            a = a.rearrange("p (a b) -> p a b", a=shape[1])
        return a

    def mm(out, lhsT, rhs, start=True, stop=True):
        K.rec("pe", lambda e: e.matmul(out, lhsT=lhsT, rhs=rhs, start=start, stop=stop), [lhsT, rhs], [out])

    def tr(out, in_, ident):
        K.rec("pe", lambda e: e.transpose(out, in_, ident), [in_, ident], [out])

    def act(out, in_, func, scale=None, bias=None):
        ins = [in_]
        kw = {}
        if scale is not None:
            kw["scale"] = scale
            if not isinstance(scale, float):
                ins.append(scale)
        if bias is not None:
            kw["bias"] = bias
            if not isinstance(bias, float):
                ins.append(bias)
        K.rec("act", lambda e: e.activation(out=out, in_=in_, func=func, **kw), ins, [out])

    def tt(eng, out, in0, in1, op):
        K.rec(eng, lambda e: e.tensor_tensor(out=out, in0=in0, in1=in1, op=op), [in0, in1], [out])

    def ts(eng, out, in0, s1, s2, op0, op1=None):
        ins = [in0] + [s for s in (s1, s2) if s is not None and not isinstance(s, float)]
        if op1 is None:
            K.rec(eng, lambda e: e.tensor_scalar(out=out, in0=in0, scalar1=s1, scalar2=None, op0=op0), ins, [out])
        else:
            K.rec(eng, lambda e: e.tensor_scalar(out=out, in0=in0, scalar1=s1, scalar2=s2, op0=op0, op1=op1), ins, [out])

    def stt(eng, out, in0, scalar, in1, op0, op1):
        ins = [in0, in1] + ([] if isinstance(scalar, float) else [scalar])
        K.rec(eng, lambda e: e.scalar_tensor_tensor(out=out, in0=in0, scalar=scalar, in1=in1, op0=op0, op1=op1), ins, [out])

    def cp(eng, out, in_):
        if eng == "act":
            K.rec("act", lambda e: e.copy(out=out, in_=in_), [in_], [out])
        else:
            K.rec(eng, lambda e: e.tensor_copy(out=out, in_=in_), [in_], [out])

    def recip(out, in_):
        K.rec("dve", lambda e: e.reciprocal(out=out, in_=in_), [in_], [out])

    def dma(q, out, in_):
        return K.rec(q, lambda e: e.dma_start(out=out, in_=in_), [in_], [out], kind="d")

    def allgather(src, dst, groups=((0, 1, 2, 3), (4, 5, 6, 7))):
        assert src.ap().nbytes() if False else True
        K.rec("pool", lambda e: e.collective_compute("AllGather", ALU.bypass, replica_groups=[list(g) for g in groups],
                                                     ins=[src.ap().opt()], outs=[dst.ap().opt()]),
              [src.ap()], [dst.ap()], kind="cc")

    cpi = [0]

    def cpa(out, in_):
        cpi[0] += 1
        cp("act" if cpi[0] % 2 else "dve", out, in_)

    def rsqrt_to(dst, src, scale):
        ts("dve", dst, src, scale, EPS, ALU.mult, ALU.add)
        act(dst, dst, AF.Sqrt)
        recip(dst, dst)

    plan = []
    for n in range(16):
        plan.append(("mod0a", gw("wmod", 0, n), 4096))
    for l in range(L):
        for j in range(8):
            plan.append((f"A{l}", gw("wA", l, j), 4096))
        for j in range(4):
            plan.append((f"B{l}", gw("wB", l, j), 4096))
        for j in range(2):
            plan.append((f"Q{l}", gw("wQ", l, j), 4096))
        for h in range(16):
            plan.append((f"HD{l}", gw("wHD", l, h), 2048))
        for j in range(4):
            plan.append((f"UV{l}", gw("wUV", l, j), 2048))
        for f in range(16):
            plan.append((f"M4{l}", gw("wM4", l, f)[:, 0:3072], 3072))
            plan.append((f"M4{l}", gw("wM4", l, f)[:, 3072:5376], 2304))
            plan.append((f"M4{l}", gw("wM4", l, f)[:, 5376:9472], 4096))
        for j in range(8):
            plan.append((f"OUT{l}", gw("wOUT", l, j), 4096))
        for j in range(NFF):
            plan.append((f"UP{l}", gw("wUP", l, j), 4096))
        for j in range(32):
            plan.append((f"DN{l}", gw("wDN", l, j), 2816))
    wstate = {"issued": 0, "next": 0}

    def wget(tag):
        n = wstate["next"]
        assert plan[n][0] == tag, (plan[n][0], tag, n)
        while wstate["issued"] < min(len(plan), n + wstate.get("depth", NB)):
            m = wstate["issued"]
            _, src, E = plan[m]
            dma("pool", wbuf[:, m % NB, 0:E], src)
            wstate["issued"] += 1
        wstate["next"] += 1
        return wbuf[:, n % NB, 0:plan[n][2]]

    def w3(wb, k, m):
        return wb.rearrange("p (k m) -> p k m", k=k)

    pbs = {"list": list(range(8)), "i": 0}

    def pb_set(lst):
        pbs["list"] = list(lst)
        pbs["i"] = 0

    def pb():
        b = pbs["list"][pbs["i"] % len(pbs["list"])]
        pbs["i"] += 1
        return b

    tiles3 = [(0, 352), (352, 704), (704, 1056)]

    def segs(lo, hi, H):
        out = []
        if lo < TP:
            e = min(hi, TP)
            out.append((0, e - lo, H + lo))
        if hi > TP:
            s = max(lo, TP)
            out.append((s - lo, hi - lo, s + 2 * H))
        return out

    def pfv(l, name, n):
        o = l * PFL + PF[name]
        return pf[:, o:o + n]

    A0 = 0
    hT = view(A0, BF16, [128, 16, T])
    XNEW = view(33792, F32, [128, 16, T])
    QLAT = view(150528, BF16, [128, 4, T])
    KTSN = view(158976, BF16, [128, 5, 32])
    VSN = view(159296, BF16, [128, 512])
    OT = view(107520, BF16, [128, 16, T])
    BC0 = view(140864, F32, [128, 1088])
    BC1 = view(145216, F32, [128, 1088])
    SM2 = view(160320, F32, [128, 880])

    K.rec("pool", lambda e: e.memset(identf[:], 0.0), [], [identf[:]])
    K.rec("pool", lambda e: e.affine_select(out=identf[:], in_=identf[:], pattern=[[-1, 128]], compare_op=ALU.not_equal,
                                            fill=1.0, base=0, channel_multiplier=1), [identf[:]], [identf[:]])
    cp("dve", identb[:], identf[:])
    K.rec("dve", lambda e: e.memset(ones[:], 1.0), [], [ones[:]])
    dma("sp", pf[:], d_pf[:, :])
    dma("sp", cTf[:], d_cT[:, :, :])
    dma("sp", ropeF[:], d_ropeF[:, :, :])
    dma("sp", hsel[:], d_hsel[:, :])
    dma("sp", icnt[:], d_icnt[:, :, :])
    act(cTb[:], cTf[:], AF.Silu)

    def mod_blocks(l, ns, tag):
        bk = 7
        for n in ns:
            wb = w3(wget(tag), 16, 256)
            for m in range(2):
                ch = 2 * n + m
                for k in range(16):
                    mm(bank(bk, 2, 2 * ch), wb[:, k, 128 * m:128 * m + 128], cTb[:, k, :], k == 0, k == 15)

    def mod_compute(l, n0, n1, tag):
        mod_blocks(l, range(n0, n1), tag)
        mod_finish(l, n0, n1)

    def mod_finish(l, n0, n1, psv=None):
        bk = 7
        c0, c1 = 2 * n0, 2 * n1
        if psv is None:
            psv = bank(bk, 192).rearrange("p (c s) -> p c s", s=2)[:, c0:c1, :]
        for s in range(2):
            tt("dve", modT[:, l, c0:c1, s], psv[:, :, s], pfv(l, "bmod", 96)[:, c0:c1], ALU.add)
        for s in range(2):
            tmp = small[:, 0:16]
            if c0 == 0:
                ts("dve", tmp, modT[:, l, 16:32, s], 1.0, None, ALU.add)
                tt("dve", mv[:, l, 0, :, s], tmp, pfv(l, "gpm", 16), ALU.mult)
                cp("dve", mv[:, l, 1, :, s], modT[:, l, 0:16, s])
            if c1 == 96:
                tt("dve", mv[:, l, 2, :, s], modT[:, l, 32:48, s], pfv(l, "gpo", 16), ALU.mult)
                ts("dve", tmp, modT[:, l, 64:80, s], 1.0, None, ALU.add)
                tt("dve", mv[:, l, 3, :, s], tmp, pfv(l, "gpf", 16), ALU.mult)
                cp("dve", mv[:, l, 4, :, s], modT[:, l, 48:64, s])
                tt("dve", mv[:, l, 5, :, s], modT[:, l, 80:96, s], pfv(l, "gpof", 16), ALU.mult)


    TOK = [view(101376, F32, [128, D]), view(109568, F32, [128, D])]
    pb_set([0, 1, 2, 3])
    for t9 in range(9):
        rows = 128 if t9 < 8 else 32
        tc0 = t9 * 128
        tok = TOK[t9 % 2]
        dma("sp", tok[0:rows, :], d_xin[tc0:tc0 + rows, :])
        for g in range(4):
            b = pb()
            for j in range(4):
                f = 4 * g + j
                tr(bank(b, rows, 128 * j), tok[0:rows, 128 * f:128 * f + 128], identf[0:rows, 0:rows])
            src = bank(b).rearrange("p (j t) -> p j t", j=4)[:, :, 0:rows]
            cpa(XNEW[:, 4 * g:4 * g + 4, tc0:tc0 + rows], src)

    mod_compute(0, 0, 16, "mod0a")

    def prenorm(l, sub, dst):
        SQ = [view(101376, BF16, [128, T]), view(103488, BF16, [128, T])]
        TMPF = [view(105600, F32, [128, T]), view(109824, F32, [128, T])]
        ssb = [5, 6, 7]
        for f in range(16):
            sq = SQ[f % 2]
            act(sq, XNEW[:, f, :], AF.Square)
            for i, (lo, hi) in enumerate(tiles3):
                mm(bank(ssb[i], NT), ones[:], sq[:, lo:hi], f == 0, f == 15)
            dma("sp", xs[f][:, :], XNEW[:, f, :])
        for i, (lo, hi) in enumerate(tiles3):
            rsqrt_to(BC0[:, lo:hi], bank(ssb[i], NT), 1.0 / D)
        ia, ib = (0, 1) if sub == 0 else (3, 4)
        for f in range(16):
            tmp = TMPF[f % 2]
            tt("dve", tmp, XNEW[:, f, :], BC0[:, 0:T], ALU.mult)
            ts("dve", dst[:, f, 0:TP], tmp[:, 0:TP], mv[:, l, ia, f, 0:1], mv[:, l, ib, f, 0:1], ALU.mult, ALU.add)
            ts("dve", dst[:, f, TP:T], tmp[:, TP:T], mv[:, l, ia, f, 1:2], mv[:, l, ib, f, 1:2], ALU.mult, ALU.add)

    def tail_out(src_fn, nch, w, dst):
        st = view(101376, F32, [128, 1024])
        for j in range(nch):
            tr(bank(2 + j // 4, 128, 128 * (j % 4))[0:w, :], src_fn(j), identf[:])
        for half in range(nch // 4):
            cpa(st[0:w, 512 * half:512 * half + 512], bank(2 + half)[0:w, :])
        dma("sp", dst, st[0:w, 0:nch * 128])

    def residual_pass(l, sub):
        YB = [view(101376, F32, [128, T]), view(105600, F32, [128, T])]
        TMPF = [view(109824, F32, [128, T]), view(114048, F32, [128, T])]
        for i, (lo, hi) in enumerate(tiles3):
            rsqrt_to(BC0[:, lo:hi], bank(5 + i, NT), 1.0 / D)
        ig = 2 if sub == 0 else 5
        for f in range(16):
            yb = YB[f % 2]
            tmp = TMPF[f % 2]
            dma("sp", yb, ys[f][:, :])
            dma("sp", XNEW[:, f, :], xs[f][:, :])
            tt("dve", tmp, yb, BC0[:, 0:T], ALU.mult)
            stt("dve", XNEW[:, f, 0:TP], tmp[:, 0:TP], mv[:, l, ig, f, 0:1], XNEW[:, f, 0:TP], ALU.mult, ALU.add)
            stt("dve", XNEW[:, f, TP:T], tmp[:, TP:T], mv[:, l, ig, f, 1:2], XNEW[:, f, TP:T], ALU.mult, ALU.add)

    def yproj_evac(b, f, lo, hi, i, YF, first, last):
        SQt = [view(141312, BF16, [128, NT]), view(142016, BF16, [128, NT])]
        sq = SQt[(f * 3 + i) % 2]
        cp("act", YF[:, lo:hi], bank(b, NT))
        act(sq, bank(b, NT), AF.Square)
        mm(bank(5 + i, NT), ones[:], sq, first, last)

    def layer(l):
        UAT = view(33792, BF16, [128, 8, 1116])
        ZBT = view(51648, BF16, [128, 8, 1086])
        KTST = view(85920, BF16, [128, 5, T])
        VST = view(96480, BF16, [128, 9, 512])
        WKVR = view(105696, BF16, [128, 16, 576])
        ZQST = view(105696, F32, [128, 4, T])
        UATAIL = view(124128, F32, [128, 8, 60])
        ZBTAIL = view(126048, F32, [128, 8, 30])
        GHB = view(127008, F32, [128, 4, 360])
        ROPET = view(132768, F32, [128, 9, 2, 64])
        GKV = view(137376, F32, [128, 512])
        SCONV = view(139424, F32, [128, 8, 30])
        SPOOL = view(140384, F32, [128, 8, 15])
        JUNK = [view(69024, F32, [128, 512]), view(71072, F32, [128, 512])]
        CKVF = [view(73120, F32, [128, 512]), view(75168, F32, [128, 512])]
        KRU = [view(77216, F32, [128, 64]), view(77472, F32, [128, 64])]
        KRV = [view(77728, F32, [128, 64]), view(77984, F32, [128, 64])]
        KRF = [view(78240, F32, [128, 64]), view(78496, F32, [128, 64])]
        KRB = [view(78752, BF16, [128, 64]), view(78880, BF16, [128, 64])]
        SGT = [view(79008, F32, [128, NT]), view(80416, F32, [128, NT])]
        SQT = [view(81824, BF16, [128, NT]), view(82528, BF16, [128, NT])]

        dma("pool", WKVR.rearrange("p k m -> p (k m)"), gw("wKVR", l, 0))
        dma("sp", ROPET, d_ropeT[:, :, :, :])
        dma("sp", GKV, d_gkv[l])
        dma("pool", KTST[64:128, 4, 0:TP], d_khot[:, :])
        dma("sp", SCONV, d_sconv[l])
        dma("sp", SPOOL, d_spool[l])
        for t9 in range(9):
            rows = 128 if t9 < 8 else 32
            tc0 = t9 * 128
            i2 = t9 % 2
            bx, by = (0, 1) if i2 == 0 else (2, 3)
            psx = bank(bx)[0:rows, :]
            psy = bank(by, 64)[0:rows, :]
            for k in range(16):
                mm(psx, hT[:, k, tc0:tc0 + rows], WKVR[:, k, 0:512], k == 0, k == 15)
            for k in range(16):
                mm(psy, hT[:, k, tc0:tc0 + rows], WKVR[:, k, 512:576], k == 0, k == 15)
            junk = JUNK[i2][0:rows]
            ssk = small[0:rows, 16 + t9:17 + t9]
            act(junk, psx, AF.Square)
            K.rec("dve", lambda e, ssk=ssk, junk=junk: e.reduce_sum(out=ssk, in_=junk, axis=AX.X), [junk], [ssk])
            rsqrt_to(ssk, ssk, 1.0 / 512)
            ckvf = CKVF[i2][0:rows]
            stt("dve", ckvf, psx, ssk, GKV[0:rows], ALU.mult, ALU.mult)
            dma("sp", o_ckv[l, tc0:tc0 + rows, :], ckvf)
            cp("act", VST[0:rows, t9, :], ckvf)
            pst = bank_bf(4 + i2, 0, [128, 4, 128])
            for c in range(4):
                tr(pst[:, c, 0:rows], VST[0:rows, t9, 128 * c:128 * c + 128], identb[0:rows, 0:rows])
            cpa(KTST[:, 0:4, tc0:tc0 + rows], pst[:, :, 0:rows])
            kru, krv, krf, krb = KRU[i2][0:rows], KRV[i2][0:rows], KRF[i2][0:rows], KRB[i2][0:rows]
            tt("dve", kru, psy, ROPET[0:rows, t9, 0, :], ALU.mult)
            tt("dve", krv[:, 0:32], psy[:, 32:64], ROPET[0:rows, t9, 1, 0:32], ALU.mult)
            tt("dve", krv[:, 32:64], psy[:, 0:32], ROPET[0:rows, t9, 1, 32:64], ALU.mult)
            tt("dve", krf, kru, krv, ALU.add)
            dma("sp", o_kr[l, tc0:tc0 + rows, :], krf)
            cp("act", krb, krf)
            pst2 = bank_bf(6 + i2, 0, [128, 128])
            tr(pst2[0:64, 0:rows], krb, identb[0:rows, 0:rows])
            cpa(KTST[0:64, 4, tc0:tc0 + rows], pst2[0:64, 0:rows])
        ck(1.5)
        dma("sp", d_bkv[0][0:384, :].rearrange("(c p) k -> p c k", p=128), KTST[:, 0:3, 0:TP])
        dma("sp", d_bkv[1][0:256, :].rearrange("(c p) k -> p c k", p=128), KTST[:, 3:5, 0:TP])
        dma("sp", d_bkv[1][256:384, :].rearrange("r (x d) -> (r x) d", d=512).rearrange("(t p) d -> p t d", p=128),
            VST[:, 0:2, :])
        dma("sp", d_bkv[2][0:384, :].rearrange("r (x d) -> (r x) d", d=512).rearrange("(t p) d -> p t d", p=128),
            VST[:, 2:8, :])
        cp("dve", KTSN, KTST[:, :, TP:T])
        cp("dve", VSN[0:32, :], VST[0:32, 8, :])
        ck(1.7)
        for i3 in range(3):
            allgather(d_bkv[i3], d_gkvb[i3])

        ck(2)
        pb_set([0, 1, 2, 3, 4, 5, 6, 7])
        for j in range(8):
            wb = w3(wget(f"A{l}"), 16, 256)
            for i, (lo, hi) in enumerate(tiles3):
                bu, bg = pb(), pb()
                for k in range(16):
                    mm(bank(bu, NT), wb[:, k, 0:128], hT[:, k, lo:hi], k == 0, k == 15)
                for k in range(16):
                    mm(bank(bg, NT), wb[:, k, 128:256], hT[:, k, lo:hi], k == 0, k == 15)
                sg = SGT[i % 2]
                act(sg, bank(bg, NT), AF.Sigmoid)
                for (a, b_, dlo) in segs(lo, hi, 30):
                    tt("dve", UAT[:, j, dlo:dlo + (b_ - a)], bank(bu, NT)[:, a:b_], sg[:, a:b_], ALU.mult)
                if i == 2:
                    tt("dve", UATAIL[:, j, 0:30], bank(bu, NT)[:, 290:320], sg[:, 290:320], ALU.mult)
                    tt("dve", UATAIL[:, j, 30:60], bank(bu, NT)[:, 322:352], sg[:, 322:352], ALU.mult)
        for n in range(4):
            wb = w3(wget(f"B{l}"), 16, 256)
            for m in range(2):
                ch = 2 * n + m
                for i, (lo, hi) in enumerate(tiles3):
                    b = pb()
                    for k in range(16):
                        mm(bank(b, NT), wb[:, k, 128 * m:128 * m + 128], hT[:, k, lo:hi], k == 0, k == 15)
                    for (a, b_, dlo) in segs(lo, hi, 15):
                        cpa(ZBT[:, ch, dlo:dlo + (b_ - a)], bank(b, NT)[:, a:b_])
                    if i == 2:
                        cp("dve", ZBTAIL[:, ch, 0:15], bank(b, NT)[:, 305:320])
                        cp("dve", ZBTAIL[:, ch, 15:30], bank(b, NT)[:, 337:352])
        dma("sp", d_bh[:, 0:240].rearrange("p (j t) -> p j t", j=8), UATAIL[:, :, 0:30])
        dma("sp", d_bh[:, 240:360].rearrange("p (j t) -> p j t", j=8), ZBTAIL[:, :, 0:15])
        allgather(d_bh, d_gh)
        tail_out(lambda j: UATAIL[:, j, 0:30], 8, 30, o_conv[l, 0])
        tail_out(lambda j: UATAIL[:, j, 30:60], 8, 30, o_conv[l, 1])
        tail_out(lambda j: ZBTAIL[:, j, 0:15], 8, 15, o_pool[l, 0])
        tail_out(lambda j: ZBTAIL[:, j, 15:30], 8, 15, o_pool[l, 1])

        pb_set([0, 1, 2, 3, 4])
        for n in range(2):
            wb = w3(wget(f"Q{l}"), 16, 256)
            for m in range(2):
                ch = 2 * n + m
                for i, (lo, hi) in enumerate(tiles3):
                    b = pb()
                    for k in range(16):
                        mm(bank(b, NT), wb[:, k, 128 * m:128 * m + 128], hT[:, k, lo:hi], k == 0, k == 15)
                    cp("act", ZQST[:, ch, lo:hi], bank(b, NT))
                    sq = SQT[i % 2]
                    act(sq, bank(b, NT), AF.Square)
                    mm(bank(5 + i, NT), ones[:], sq, ch == 0, ch == 3)
        for i, (lo, hi) in enumerate(tiles3):
            rsqrt_to(BC0[:, lo:hi], bank(5 + i, NT), 1.0 / 512)
        for ch in range(4):
            stt("dve", QLAT[:, ch, :], ZQST[:, ch, :], pfv(l, "gq", 4)[:, ch:ch + 1], BC0[:, 0:T], ALU.mult, ALU.mult)
        dma("sp", d_hsp[:, :], hT.rearrange("p a b -> p (a b)"))
        if l == 0:
            dump("hT", hT.rearrange("p a b -> p (a b)"), BF16, 16 * T)
            dump("qlat", QLAT.rearrange("p a b -> p (a b)"), BF16, 4 * T)

        ck(3)
        HT = view(69024, F32, [128, 360])
        dma("sp", GHB, d_gh.ap().rearrange("(r p) f -> p r f", p=128))
        ts("dve", HT, GHB[:, 0, :], hsel[:, 0:1], None, ALU.mult)
        for r in range(1, 4):
            stt("dve", HT, GHB[:, r, :], hsel[:, r:r + 1], HT, ALU.mult, ALU.add)
        cp("dve", UAT[:, :, 0:30], HT[:, 0:240].rearrange("p (j t) -> p j t", j=8))
        cp("dve", ZBT[:, :, 0:15], HT[:, 240:360].rearrange("p (j t) -> p j t", j=8))
        cp("dve", UAT[:, :, 1054:1084], SCONV)
        cp("dve", ZBT[:, :, 1039:1054], SPOOL)
        MT = view(120672, BF16, [128, 8, T])
        SAT = view(103776, BF16, [128, 8, T])
        AT = view(69024, F32, [128, 8, 1086])
        P = [view(15872, F32, [128, 2, 1086]), view(24560, F32, [128, 2, 1086])]
        T16 = view(33248, F32, [128, 16])
        for g in range(4):
            src = ZBT[:, 2 * g:2 * g + 2, :]
            w = 2 ** (g + 1)
            cur = src
            for i in range(g + 1):
                st = 2 ** i
                dst = P[i % 2]
                tt("dve", dst[:, :, st:1086], cur[:, :, st:1086], cur[:, :, 0:1086 - st], ALU.add)
                cur = dst
            stt("dve", MT[:, 2 * g:2 * g + 2, 0:TP], cur[:, :, 15:15 + TP], 1.0 / w, src[:, :, 15:15 + TP], ALU.mult, ALU.subtract)
            stt("dve", MT[:, 2 * g:2 * g + 2, TP:T], cur[:, :, 1054:1086], 1.0 / w, src[:, :, 1054:1086], ALU.mult, ALU.subtract)
            for c2 in range(2):
                tt("dve", T16, cur[:, c2, 15:31], icnt[:, g, :], ALU.mult)
                tt("dve", MT[:, 2 * g + c2, 0:16], T16, src[:, c2, 15:31], ALU.subtract)
        DG = [view(0, BF16, [128, 31, 128]), view(7936, BF16, [128, 31, 128])]
        SQc = [view(160320, BF16, [128, 362]), view(161044, BF16, [128, 362])]
        ABc = [view(161768, BF16, [128, 362]), view(162492, BF16, [128, 362])]
        ctiles = [(0, 362), (362, 724), (724, 1086)]
        wd = pfv(l, "wdwa", 248).rearrange("p (j k) -> p j k", j=8)
        for j in range(8):
            dg = DG[j % 2]
            for k in range(31):
                if k % 2 == 0:
                    ts("dve", dg[:, k, :], identb[:], wd[:, j, k:k + 1], None, ALU.mult)
                else:
                    act(dg[:, k, :], identb[:], AF.Copy, scale=wd[:, j, k:k + 1])
            for i, (lo, hi) in enumerate(ctiles):
                b = i if j % 2 == 0 else 3 + i
                b = [0, 1][(3 * j + i) % 2]
                for k in range(31):
                    mm(bank(b, 362), dg[:, k, :], UAT[:, j, lo + k:lo + k + 362], k == 0, k == 30)
                ts("dve", AT[:, j, lo:hi], bank(b, 362), pfv(l, "bdwa", 8)[:, j:j + 1], None, ALU.add)
                sq, ab = SQc[i % 2], ABc[i % 2]
                act(sq, AT[:, j, lo:hi], AF.Square)
                cp("act", ab, AT[:, j, lo:hi])
                mm(bank(2 + i, 362), ones[:], sq, j == 0, j == 7)
                mm(bank(5 + i, 362), ones[:], ab, j == 0, j == 7)
        TMPL = view(0, F32, [128, 1088])
        for i, (lo, hi) in enumerate(ctiles):
            ts("dve", BC1[:, lo:hi], bank(5 + i, 362), 1.0 / 1024, None, ALU.mult)
            tt("dve", TMPL[:, lo:hi], BC1[:, lo:hi], BC1[:, lo:hi], ALU.mult)
            stt("dve", BC0[:, lo:hi], bank(2 + i, 362), 1.0 / 1024, TMPL[:, lo:hi], ALU.mult, ALU.subtract)
            ts("dve", BC0[:, lo:hi], BC0[:, lo:hi], EPS, None, ALU.add)
            act(BC0[:, lo:hi], BC0[:, lo:hi], AF.Sqrt)
            recip(BC0[:, lo:hi], BC0[:, lo:hi])
        for j in range(8):
            tt("dve", AT[:, j, :], AT[:, j, :], BC1[:, 0:1086], ALU.subtract)
            tt("dve", AT[:, j, :], AT[:, j, :], BC0[:, 0:1086], ALU.mult)
            ts("dve", AT[:, j, :], AT[:, j, :], pfv(l, "lng", 8)[:, j:j + 1], pfv(l, "lnb", 8)[:, j:j + 1], ALU.mult, ALU.add)
            act(SAT[:, j, 0:TP], AT[:, j, 0:TP], AF.Silu)
            act(SAT[:, j, TP:T], AT[:, j, 1054:1086], AF.Silu)
        if l == 0:
            dump("sat", SAT.rearrange("p a b -> p (a b)"), BF16, 8 * T)
            dump("mt", MT.rearrange("p a b -> p (a b)"), BF16, 8 * T)
        dma("sp", d_sasp[:, :], SAT.rearrange("p a b -> p (a b)"))
        dma("sp", d_msp[:, :], MT.rearrange("p a b -> p (a b)"))

        ck(4)
        QF = [view(0, BF16, [128, 5, TP]), view(10240, BF16, [128, 5, TP])]
        QH1 = view(20480, BF16, [128, T])
        QH = [QH1, QH1]
        ACCS = view(22592, F32, [128, 512])
        PT = [view(24704 + 1024 * i, BF16, [128, 512]) for i in range(4)]
        ONORM = view(28800, BF16, [128, 4, 512])
        RS = view(32896, F32, [128, 8])
        KT = view(33792, BF16, [128, 5, 4096])
        VV = view(74752, BF16, [128, 32, 512])
        ONT = view(141312, BF16, [128, 4, 512])
        QS = view(145408, BF16, [128, 5, 512])
        RT1 = view(160320, F32, [128, NT])
        RT2 = view(161728, F32, [128, NT])
        dma("pool", QF[0][64:128, 4, :], d_qB[:, :])
        dma("pool", QF[1][64:128, 4, :], d_qB[:, :])
        for g in range(4):
            r0 = 384 * g
            dma("sp", KT[:, 0:3, 1024 * g:1024 * g + 1024], d_gkvb[0][r0:r0 + 384, :].rearrange("(c p) k -> p c k", p=128))
            dma("sp", KT[:, 3:5, 1024 * g:1024 * g + 1024], d_gkvb[1][r0:r0 + 256, :].rearrange("(c p) k -> p c k", p=128))
            dma("sp", VV[:, 8 * g:8 * g + 2, :],
                d_gkvb[1][r0 + 256:r0 + 384, :].rearrange("r (x d) -> (r x) d", d=512).rearrange("(t p) d -> p t d", p=128))
            dma("sp", VV[:, 8 * g + 2:8 * g + 8, :],
                d_gkvb[2][r0:r0 + 384, :].rearrange("r (x d) -> (r x) d", d=512).rearrange("(t p) d -> p t d", p=128))
        UB = 7
        SUMS = bank(6, 8, 0)
        onesf = small[:, 40:41]
        K.rec("dve", lambda e: e.memset(onesf, 1.0), [], [onesf])

        def units(h):
            wb = wget(f"HD{l}")
            hp = h % 2
            wqN = wb[:, 0:512].rearrange("p (k m) -> p k m", k=4)
            wqR = wb[:, 512:768].rearrange("p (k m) -> p k m", k=4)
            wqS = wb[:, 768:1024].rearrange("p (k m) -> p k m", k=4)
            wuk = wb[:, 1024:1536]
            wuv = wb[:, 1536:2048].rearrange("p (k m) -> p k m", k=4)
            us = []
            for i, (lo, hi) in enumerate(tiles3):
                def u1(lo=lo, hi=hi):
                    b = bank(UB, NT)
                    for k in range(4):
                        mm(b, wqN[:, k, :], QLAT[:, k, lo:hi], k == 0, k == 3)
                    cp("act", QH[hp][:, lo:hi], b)
                us.append(u1)
                for rc in range(4):
                    def u2(lo=lo, hi=hi, rc=rc, i=i):
                        b = bank(UB, NT)
                        mm(b, wuk[:, 128 * rc:128 * rc + 128], QH[hp][:, lo:hi], True, True)
                        pe_ = min(hi, TP)
                        cp("act", QF[hp][:, rc, lo:pe_], b[:, 0:pe_ - lo])
                        if i == 2:
                            cp(os.environ.get("DBG_QSE", "dve"), QS[:, rc, 32 * h:32 * h + 32], b[:, 320:352])
                    us.append(u2)

                def u3a(lo=lo, hi=hi):
                    b = bank(UB, NT)[0:64, :]
                    for k in range(4):
                        mm(b, wqR[:, k, :], QLAT[:, k, lo:hi], k == 0, k == 3)
                    tt("dve", RT1[0:64, :], b, ropeF[:, 0, lo:hi], ALU.mult)
                us.append(u3a)

                def u3b(lo=lo, hi=hi, i=i):
                    b = bank(UB, NT)[0:64, :]
                    for k in range(4):
                        mm(b, wqS[:, k, :], QLAT[:, k, lo:hi], k == 0, k == 3)
                    tt("dve", RT2[0:64, :], b, ropeF[:, 1, lo:hi], ALU.mult)
                    pe_ = min(hi, TP)
                    tt("dve", QF[hp][0:64, 4, lo:pe_], RT1[0:64, 0:pe_ - lo], RT2[0:64, 0:pe_ - lo], ALU.add)
                    if i == 2:
                        tt("dve", QS[0:64, 4, 32 * h:32 * h + 32], RT1[0:64, 320:352], RT2[0:64, 320:352], ALU.add)
                us.append(u3b)
            return us, wuv

        def attention(qchunk, ktiles, par, hook):
            n = len(ktiles)

            def pv(kt):
                _, vap, nk = ktiles[kt]
                p_ = PT[kt % 4]
                for s in range(4):
                    mm(bank(2 + s), p_[0:nk, 128 * s:128 * s + 128], vap, kt == 0, kt == n - 1)
                if kt == 0:
                    cp("dve", ACCS[0:nk, :], p_[0:nk, :])
                else:
                    tt("dve", ACCS[0:nk, :], ACCS[0:nk, :], p_[0:nk, :], ALU.add)
            for kt in range(n):
                kfn, _, nk = ktiles[kt]
                sb_ = bank(kt % 2)[0:nk, :]
                for c in range(5):
                    mm(sb_, kfn(c), qchunk(c), c == 0, c == 4)
                if kt >= 2:
                    pv(kt - 2)
                act(PT[kt % 4][0:nk, :], sb_, AF.Exp, scale=ATTN_SCALE)
                hook(kt)
            pv(n - 2)
            pv(n - 1)
            for s in range(4):
                mm(SUMS[:, 4 * par + s:4 * par + s + 1], ACCS[:, 128 * s:128 * s + 128], onesf, True, True)

        def tail_evac(par):
            recip(RS[:, 4 * par:4 * par + 4], SUMS[:, 4 * par:4 * par + 4])
            for s in range(4):
                if s % 2 == 0:
                    ts("dve", ONORM[:, s, :], bank(2 + s), RS[:, 4 * par + s:4 * par + s + 1], None, ALU.mult)
                else:
                    act(ONORM[:, s, :], bank(2 + s), AF.Copy, scale=RS[:, 4 * par + s:4 * par + s + 1])

        def tail_tr():
            tb = bank_bf(7, 0, [128, 4, 128])
            for s in range(4):
                for rc in range(4):
                    tr(tb[:, rc, :], ONORM[:, s, 128 * rc:128 * rc + 128], identb[:])
                cpa(ONT[:, :, 128 * s:128 * s + 128], tb)

        def tail_pe(h, qb, wuv):
            tail_tr()
            for half in range(2):
                wbk = bank(7, 256, 256)
                for rc in range(4):
                    mm(wbk, wuv[:, rc, :], ONT[:, rc, 256 * half:256 * half + 256], rc == 0, rc == 3)
                cpa(OT[:, h, 512 * qb + 256 * half:512 * qb + 256 * half + 256], wbk)

        import os
        NH = int(os.environ.get("DBG_NH", "16"))
        NOS = os.environ.get("DBG_NOS", "") != ""
        ck(4.1)
        wstate["depth"] = 2
        modq = []
        if l == 0 and STAGE is None:
            modq = [(0, n, m, 2 * n - 32 + m) for n in range(16, 48) for m in range(2)] + \
                   [(1, n, m, 64 + 2 * n + m) for n in range(48) for m in range(2)]

        def mod_stream_one():
            lm, n, m, cc = modq.pop(0)
            src = gw("wmod", lm, n).rearrange("p (k m) -> p k m", m=256)[:, :, 128 * m:128 * m + 128]
            wbm = mbuf[:, :].rearrange("p (k m) -> p k m", m=128)
            dma("pool", wbm, src)
            for k in range(16):
                mm(bank(6, 2, 64 + 2 * cc), wbm[:, k, :], cTb[:, k, :], k == 0, k == 15)
        us0, wuv_cur = units(0)
        NU = int(os.environ.get("DBG_NU", "99"))
        for u in us0[:NU]:
            u()
        ck(4.2)
        pend = []
        it = 0
        nxt = {}
        for h in range(NH):
            usn, wuv_next = [], None
            hp = h % 2
            for qb in range(2):
                par = it % 2
                it += 1
                q0 = 512 * qb
                ktl = [((lambda c, kt=kt: KT[:, c, 128 * kt:128 * kt + 128]), VV[:, kt, :], 128) for kt in range(32)]
                upos = [0]

                def hook(kt, qb=qb, h=h):
                    if kt == 4 and pend:
                        pend.pop(0)()
                    if modq and kt % 6 == 1:
                        mod_stream_one()
                    if kt == 5 and qb == 0 and h < NH - 1:
                        u_, w_ = units(h + 1)
                        usn.extend(u_)
                        nxt["wuv"] = w_
                    tot = qb * 32 + kt
                    want = (tot * len(usn)) // 60 if usn else 0
                    while usn and upos_g[0] < min(want, len(usn)):
                        usn[upos_g[0]]()
                        upos_g[0] += 1
                if qb == 0:
                    upos_g = [0]
                attention(lambda c: QF[hp][:, c, q0:q0 + 512], ktl, par, hook)
                tail_evac(par)
                pend.append(lambda h=h, qb=qb, wuv=wuv_cur: tail_pe(h, qb, wuv))
            while usn and upos_g[0] < len(usn):
                usn[upos_g[0]]()
                upos_g[0] += 1
            wuv_cur = nxt.get("wuv")
        while pend:
            pend.pop(0)()
        if l == 0 and STAGE is None:
            while modq:
                mod_stream_one()
            psall = bank(6, 320, 64).rearrange("p (c s) -> p c s", s=2)
            mod_finish(0, 16, 48, psall[:, 0:64, :])
            mod_finish(1, 0, 48, psall[:, 64:160, :])
        if NOS:
            raise _Stop()
        for g in range(4):
            dma("pool", KT[:, 0:4, 1024 * g:1024 * g + 1024],
                d_cacheT[l, 0:512, 1024 * g:1024 * g + 1024].rearrange("(c p) k -> p c k", p=128))
            dma("pool", KT[0:64, 4, 1024 * g:1024 * g + 1024], d_cacheT[l, 512:576, 1024 * g:1024 * g + 1024])
            dma("pool", VV[:, 8 * g:8 * g + 8, :],
                d_cacheV[l, 1024 * g:1024 * g + 1024, :].rearrange("(t p) d -> p t d", p=128))

        def kf_cache(kt):
            return lambda c: (KT[:, c, 128 * kt:128 * kt + 128] if c < 4 else KT[0:64, 4, 128 * kt:128 * kt + 128])
        ktl = [(kf_cache(kt), VV[:, kt, :], 128) for kt in range(32)]
        ktl.append(((lambda c: (KTSN[:, c, :] if c < 4 else KTSN[0:64, 4, :])), VSN[0:32, :], 32))
        par = it % 2
        attention(lambda c: (QS[:, c, :] if c < 4 else QS[0:64, 4, :]), ktl, par, lambda kt: None)
        tail_evac(par)
        tail_tr()
        for j in range(4):
            wb = wget(f"UV{l}").rearrange("p (h k m) -> p h k m", h=4, k=4)
            for hl in range(4):
                h = 4 * j + hl
                wbk = bank(7, 32, 256)
                for rc in range(4):
                    mm(wbk, wb[:, hl, rc, :], ONT[:, rc, 128 * j + 32 * hl:128 * j + 32 * hl + 32], rc == 0, rc == 3)
                cpa(OT[:, h, TP:T], wbk)

        if l == 0:
            dump("ot", OT.rearrange("p a b -> p (a b)"), BF16, 16 * T)
        ck(5)
        wstate["depth"] = NB
        SAT2 = view(33792, BF16, [128, 8, T])
        MT2 = view(50688, BF16, [128, 8, T])
        MERGED = view(67584, BF16, [128, 16, T])
        dma("sp", hT.rearrange("p a b -> p (a b)"), d_hsp[:, :])
        dma("sp", SAT2.rearrange("p a b -> p (a b)"), d_sasp[:, :])
        dma("sp", MT2.rearrange("p a b -> p (a b)"), d_msp[:, :])
        SG = [view(101376, F32, [128, NT]), view(102784, F32, [128, NT])]
        ACC = [view(141312 + 1408 * i, F32, [128, NT]) for i in range(3)]
        T2 = [view(145536, F32, [128, NT]), view(146944, F32, [128, NT])]
        pb_set([0, 1, 2, 3, 4, 5, 6, 7])
        psc = pfv(l, "psc", 16)
        for f in range(16):
            g4 = f // 4
            for pair in range(3):
                cw = wget(f"M4{l}").rearrange("p (k m) -> p k m", m=128)
                c1 = c2 = c3 = cw
                for i, (lo, hi) in enumerate(tiles3):
                    bo, bg = pb(), pb()
                    if pair == 0:
                        for k in range(8):
                            mm(bank(bo, NT), c1[:, k, :], SAT2[:, k, lo:hi], k == 0, k == 7)
                        gwt, gof = c1, 8
                    elif pair == 1:
                        for k in range(2):
                            mm(bank(bo, NT), c2[:, k, :], MT2[:, 2 * g4 + k, lo:hi], k == 0, k == 1)
                        gwt, gof = c2, 2
                    else:
                        for k in range(16):
                            mm(bank(bo, NT), c3[:, k, :], OT[:, k, lo:hi], k == 0, k == 15)
                        gwt, gof = c3, 16
                    for k in range(16):
                        mm(bank(bg, NT), gwt[:, gof + k, :], hT[:, k, lo:hi], k == 0, k == 15)
                    sg = SG[(pair * 3 + i) % 2]
                    act(sg, bank(bg, NT), AF.Sigmoid)
                    if pair == 0:
                        tt("dve", ACC[i], bank(bo, NT), sg, ALU.mult)
                    elif pair == 1:
                        t2 = T2[i % 2]
                        stt("dve", t2, bank(bo, NT), psc[:, f:f + 1], sg, ALU.mult, ALU.mult)
                        tt("dve", ACC[i], ACC[i], t2, ALU.add)
                    else:
                        t2 = T2[i % 2]
                        tt("dve", t2, bank(bo, NT), sg, ALU.mult)
                        tt("dve", MERGED[:, f, lo:hi], ACC[i], t2, ALU.add)

        if l == 0:
            dump("merged", MERGED.rearrange("p a b -> p (a b)"), BF16, 16 * T)
        ck(6)
        YF = [view(101376, F32, [128, T]), view(105600, F32, [128, T])]
        pb_set([0, 1, 2, 3, 4])
        for n in range(8):
            wb = w3(wget(f"OUT{l}"), 16, 256)
            for m in range(2):
                f = 2 * n + m
                yf = YF[f % 2]
                for i, (lo, hi) in enumerate(tiles3):
                    b = pb()
                    for k in range(16):
                        mm(bank(b, NT), wb[:, k, 128 * m:128 * m + 128], MERGED[:, k, lo:hi], k == 0, k == 15)
                    yproj_evac(b, f, lo, hi, i, yf, f == 0, f == 15)
                dma("sp", ys[f][:, :], yf)
        residual_pass(l, 0)
        if l == 0:
            dump("xnew", XNEW.rearrange("p a b -> p (a b)"), F32, 16 * T)
        prenorm(l, 1, hT)

        ck(7)
        ACTT = view(33792, BF16, [128, NFF, T])
        UPG = [view(126720, F32, [128, 1060]), view(130960, F32, [128, 1060])]
        UPV = [view(135200, F32, [128, T]), view(139424, F32, [128, T])]
        CV = view(143648, F32, [128, 1060])
        SFFN = view(160320, F32, [128, 2, NFF])
        PG01 = view(160672, F32, [128, 2, NFF])
        PV01 = view(161024, F32, [128, 2, NFF])
        BH3 = view(161376, F32, [128, 2, NFF])
        STL = view(161728, F32, [128, 2, NFF])
        H3 = view(162080, F32, [128, 2, NFF])
        GH3B = view(147888, F32, [128, 4, 88])
        C01 = view(149296, F32, [128, 2, NFF])
        TQ = view(149648, F32, [128, NFF])
        dma("sp", SFFN, d_sffn[l])
        for u in UPG:
            K.rec("dve", lambda e, u=u: e.memset(u[:, 0:2], 0.0), [], [u[:, 0:2]])
        wf = pfv(l, "wdwf", 132).rearrange("p (j k) -> p j k", j=NFF)
        bf_ = pfv(l, "bdwf", NFF)
        pb_set([0, 1, 2, 3, 4, 5, 6, 7])
        for j in range(NFF):
            wb = w3(wget(f"UP{l}"), 16, 256)
            upg, upv = UPG[j % 2], UPV[j % 2]
            cp("dve", upg[:, 1026:1028], SFFN[:, :, j])
            for i, (lo, hi) in enumerate(tiles3):
                bg, bv = pb(), pb()
                for k in range(16):
                    mm(bank(bg, NT), wb[:, k, 0:128], hT[:, k, lo:hi], k == 0, k == 15)
                for k in range(16):
                    mm(bank(bv, NT), wb[:, k, 128:256], hT[:, k, lo:hi], k == 0, k == 15)
                for (a, b_, dlo) in segs(lo, hi, 2):
                    cp("act", upg[:, dlo:dlo + (b_ - a)], bank(bg, NT)[:, a:b_])
                cp("act", upv[:, lo:hi], bank(bv, NT))
            cp("dve", PG01[:, :, j], upg[:, 2:4])
            cp("dve", PV01[:, :, j], upv[:, 0:2])
            cp("dve", BH3[:, :, j], upg[:, 1024:1026])
            cp("dve", STL[:, :, j], upg[:, 1058:1060])
            ts("dve", CV[:, 0:1058], upg[:, 0:1058], wf[:, j, 0:1], bf_[:, j:j + 1], ALU.mult, ALU.add)
            stt("dve", CV[:, 0:1058], upg[:, 1:1059], wf[:, j, 1:2], CV[:, 0:1058], ALU.mult, ALU.add)
            stt("dve", CV[:, 0:1058], upg[:, 2:1060], wf[:, j, 2:3], CV[:, 0:1058], ALU.mult, ALU.add)
            act(CV[:, 0:1058], CV[:, 0:1058], AF.Silu)
            tt("dve", ACTT[:, j, 0:TP], CV[:, 0:TP], upv[:, 0:TP], ALU.mult)
            tt("dve", ACTT[:, j, TP:T], CV[:, 1026:1058], upv[:, TP:T], ALU.mult)
        dma("sp", d_bh3[:, :].rearrange("p (t j) -> p t j", t=2), BH3)
        allgather(d_bh3, d_gh3)
        stf = view(126720, F32, [128, 128])
        for which, src in ((0, BH3), (1, STL)):
            tr(bank(0, 128)[0:88, :], src.rearrange("p t j -> p (t j)"), identf[:])
            cp("dve", stf[0:88, :], bank(0, 128)[0:88, :])
            for t2_ in range(2):
                dma("sp", o_ffn[l, which, t2_, :].rearrange("(j p) -> j p", p=128), stf[44 * t2_:44 * t2_ + 44, :])

        def patch():
            dma("sp", GH3B, d_gh3.ap().rearrange("(r p) f -> p r f", p=128))
            h3f = H3.rearrange("p t j -> p (t j)")
            ts("dve", h3f, GH3B[:, 0, :], hsel[:, 0:1], None, ALU.mult)
            for r in range(1, 4):
                stt("dve", h3f, GH3B[:, r, :], hsel[:, r:r + 1], h3f, ALU.mult, ALU.add)
            w0, w1, w2 = wf[:, :, 0], wf[:, :, 1], wf[:, :, 2]
            h0, h1 = H3[:, 0, :], H3[:, 1, :]
            g0, g1 = PG01[:, 0, :], PG01[:, 1, :]
            c0, c1_ = C01[:, 0, :], C01[:, 1, :]
            tt("dve", c0, w0, h0, ALU.mult)
            tt("dve", TQ, w1, h1, ALU.mult)
            tt("dve", c0, c0, TQ, ALU.add)
            tt("dve", TQ, w2, g0, ALU.mult)
            tt("dve", c0, c0, TQ, ALU.add)
            tt("dve", c0, c0, bf_, ALU.add)
            tt("dve", c1_, w0, h1, ALU.mult)
            tt("dve", TQ, w1, g0, ALU.mult)
            tt("dve", c1_, c1_, TQ, ALU.add)
            tt("dve", TQ, w2, g1, ALU.mult)
            tt("dve", c1_, c1_, TQ, ALU.add)
            tt("dve", c1_, c1_, bf_, ALU.add)
            act(C01, C01, AF.Silu)
            tt("dve", C01, C01, PV01, ALU.mult)
            cp("dve", ACTT[:, :, 0:2].rearrange("p j t -> p t j"), C01)

        ck(8)
        YF = [view(126720, F32, [128, T]), view(130944, F32, [128, T])]
        pb_set([0, 1, 2, 3, 4])
        wstate["depth"] = 2
        for f in range(16):
            hb0 = wget(f"DN{l}").rearrange("p (k m) -> p k m", m=128)
            hb1 = wget(f"DN{l}").rearrange("p (k m) -> p k m", m=128)
            yf = YF[f % 2]
            for i in (1, 2, 0):
                lo, hi = tiles3[i]
                if f == 0 and i == 0:
                    patch()
                b = pb()
                for k in range(NFF):
                    hb = hb0 if k < 22 else hb1
                    mm(bank(b, NT), hb[:, k % 22, :], ACTT[:, k, lo:hi], k == 0, k == NFF - 1)
                yproj_evac(b, f, lo, hi, i, yf, f == 0, f == 15)
            dma("sp", ys[f][:, :], yf)
        wstate["depth"] = NB
        residual_pass(l, 1)
        ck(9)

    def final_out():
        pb_set([0, 1, 2, 3])
        for t9 in range(9):
            rows = 128 if t9 < 8 else 32
            tc0 = t9 * 128
            tok = TOK[t9 % 2]
            for g in range(4):
                b = pb()
                for j in range(4):
                    f = 4 * g + j
                    tr(bank(b, 128, 128 * j)[0:rows, :], XNEW[:, f, tc0:tc0 + rows], identf[:])
                cpa(tok[0:rows, 512 * g:512 * g + 512], bank(b)[0:rows, :])
            dma("sp", o_y[tc0:tc0 + rows, :], tok[0:rows, :])

    try:
        prenorm(0, 0, hT)
        ck(1)
        for l in range(L):
            layer(l)
            if l + 1 < L:
                prenorm(l + 1, 0, hT)
        final_out()
    except _Stop:
        if STAGE == 0:
            final_out()

    K.emit()
    es.close()
    nc._ext_in = set(K.ext_in)
    return nc


def _blk(w, cols):
    K_ = w.shape[0]
    kc = K_ // 128
    out = []
    for c in cols:
        sub = w[:, c]
        out.append(sub.reshape(kc, 128, len(c)).transpose(1, 0, 2).reshape(128, kc * len(c)))
    return np.ascontiguousarray(np.stack(out))


def prep_shared(inp):
    f = np.float32
    sh = {}
    w_in = inp["w_in"]
    ar = np.arange
    sh["wmod"] = np.stack([_blk(inp["w_mod"][l], [ar(256 * n, 256 * n + 256) for n in range(48)]) for l in range(L)])
    sh["wA"] = np.stack([_blk(w_in[l], [np.concatenate([ar(128 * j, 128 * j + 128), ar(1024 + 128 * j, 1024 + 128 * j + 128)])
                                       for j in range(8)]) for l in range(L)])
    sh["wB"] = np.stack([_blk(w_in[l], [ar(OFF_B + 256 * n, OFF_B + 256 * n + 256) for n in range(4)]) for l in range(L)])
    sh["wQ"] = np.stack([_blk(w_in[l], [ar(OFF_Q + 256 * n, OFF_Q + 256 * n + 256) for n in range(2)]) for l in range(L)])
    sh["wKVR"] = np.stack([_blk(w_in[l], [ar(OFF_KV, OFF_G)])[0] for l in range(L)])
    whd = np.zeros((L, 16, 128, 2048), f)
    wuvp = np.zeros((L, 4, 128, 4, 512), f)
    for l in range(L):
        wuq = inp["w_uq"][l].reshape(4, 128, 16, 192)
        wuk = inp["w_uk"][l]
        wuv = inp["w_uv"][l].reshape(4, 128, 16, 128)
        for h in range(16):
            qn = wuq[:, :, h, 0:128].transpose(1, 0, 2).reshape(128, 512)
            qr = wuq[:, :, h, 128:192]
            qs = np.concatenate([qr[..., 32:64], qr[..., 0:32]], axis=-1)
            whd[l, h, :, 0:512] = qn
            whd[l, h, :, 512:768] = qr.transpose(1, 0, 2).reshape(128, 256)
            whd[l, h, :, 768:1024] = qs.transpose(1, 0, 2).reshape(128, 256)
            whd[l, h, :, 1024:1536] = wuk[:, h, :].T
            uv = wuv[:, :, h, :].transpose(1, 0, 2).reshape(128, 512)
            whd[l, h, :, 1536:2048] = uv
            wuvp[l, h // 4, :, h % 4, :] = uv
    sh["wHD"] = whd
    sh["wUV"] = wuvp.reshape(L, 4, 128, 2048)
    wm4 = np.zeros((L, 16, 128, 9472), f)
    for l in range(L):
        pa = inp["w_pa"][l].reshape(8, 128, 16, 128)
        oc = inp["w_oc"][l].reshape(16, 128, 16, 128)
        pool = inp["w_pool"][l].reshape(4, 2, 128, 4, 128)
        wg = w_in[l][:, OFF_G:].reshape(16, 128, 3, 16, 128)
        for fch in range(16):
            o = 0
            blkA = pa[:, :, fch, :].transpose(1, 0, 2).reshape(128, 1024)
            wm4[l, fch, :, 0:1024] = blkA
            wm4[l, fch, :, 1024:3072] = wg[:, :, 0, fch, :].transpose(1, 0, 2).reshape(128, 2048)
            wm4[l, fch, :, 3072:3328] = pool[fch // 4, :, :, fch % 4, :].transpose(1, 0, 2).reshape(128, 256)
            wm4[l, fch, :, 3328:5376] = wg[:, :, 1, fch, :].transpose(1, 0, 2).reshape(128, 2048)
            wm4[l, fch, :, 5376:7424] = oc[:, :, fch, :].transpose(1, 0, 2).reshape(128, 2048)
            wm4[l, fch, :, 7424:9472] = wg[:, :, 2, fch, :].transpose(1, 0, 2).reshape(128, 2048)
    sh["wM4"] = wm4
    sh["wOUT"] = np.stack([_blk(inp["w_out"][l], [ar(256 * n, 256 * n + 256) for n in range(8)]) for l in range(L)])
    sh["wUP"] = np.stack([_blk(inp["w_up"][l], [np.concatenate([ar(128 * j, 128 * j + 128), ar(DFF + 128 * j, DFF + 128 * j + 128)])
                                              for j in range(NFF)]) for l in range(L)])
    wdn = np.zeros((L, 32, 128, 2816), f)
    for l in range(L):
        wd = inp["w_down"][l].reshape(2, 22, 128, 16, 128)
        for fch in range(16):
            for half in range(2):
                wdn[l, 2 * fch + half] = wd[half, :, :, fch, :].transpose(1, 0, 2).reshape(128, 2816)
    sh["wDN"] = wdn
    pfa = np.zeros((128, L * PFL), f)
    for l in range(L):
        def put(name, arr):
            pfa[:, l * PFL + PF[name]:l * PFL + PF[name] + arr.shape[1]] = arr
        fm = lambda v: v.reshape(-1, 128).T
        put("gpm", fm(inp["g_pre_mix"][l])); put("gpo", fm(inp["g_post_mix"][l]))
        put("gpf", fm(inp["g_pre_ffn"][l])); put("gpof", fm(inp["g_post_ffn"][l]))
        put("psc", fm(inp["pool_scale"][l])); put("bdwa", fm(inp["b_dwa"][l]))
        put("lng", fm(inp["ln_a_g"][l])); put("lnb", fm(inp["ln_a_b"][l]))
        put("gq", fm(inp["g_q_lat"][l])); put("bdwf", fm(inp["b_dwf"][l]))
        put("bmod", fm(inp["b_mod"][l]))
        put("wdwa", inp["w_dwa"][l].reshape(31, 8, 128).transpose(2, 1, 0).reshape(128, 248))
        put("wdwf", inp["w_dwf"][l].reshape(3, NFF, 128).transpose(2, 1, 0).reshape(128, 132))
    sh["pf"] = pfa
    sh["gkv"] = np.ascontiguousarray(np.broadcast_to(inp["g_kv_lat"][:, None, :], (L, 128, 512))).astype(f)
    return sh


def prep_core(inp, c):
    f = np.float32
    b, r = c // 4, c % 4
    m = {}
    m["xin"] = np.ascontiguousarray(np.concatenate([inp["x_prompt"][b, 1024 * r:1024 * r + 1024], inp["x_sample"][c]], axis=0))
    cv = np.stack([inp["c_prompt"][b], inp["c_sample"][c]], axis=0)
    m["cT"] = np.ascontiguousarray(cv.reshape(2, 16, 128).transpose(2, 1, 0))
    pos = np.concatenate([1024 * r + np.arange(1024), 4096 + np.arange(32)]).astype(f)
    inv = (np.float32(10000.0) ** (-np.arange(32, dtype=f) / np.float32(32))).astype(f)
    ang = (pos[:, None] * inv[None, :]).astype(f)
    cos, sin = np.cos(ang).astype(f), np.sin(ang).astype(f)
    ropeF = np.zeros((64, 2, T), f)
    ropeF[0:32, 0] = cos.T; ropeF[32:64, 0] = cos.T
    ropeF[0:32, 1] = -sin.T; ropeF[32:64, 1] = sin.T
    m["ropeF"] = ropeF
    ropeT = np.zeros((128, 9, 2, 64), f)
    cc2 = np.concatenate([cos, cos], axis=1)
    ss2 = np.concatenate([-sin, sin], axis=1)
    pad = np.zeros((9 * 128, 64), f)
    pad[:T] = cc2
    ropeT[:, :, 0, :] = pad.reshape(9, 128, 64).transpose(1, 0, 2)
    pad = np.zeros((9 * 128, 64), f)
    pad[:T] = ss2
    ropeT[:, :, 1, :] = pad.reshape(9, 128, 64).transpose(1, 0, 2)
    m["ropeT"] = ropeT
    qch = (1024 * r + np.arange(1024)) // 64
    cidx = np.arange(64)[:, None]
    m["qB"] = np.where(cidx > qch[None, :], -30000.0, 0.0).astype(f)
    m["khot"] = (cidx == qch[None, :]).astype(f)
    hs = np.zeros((128, 4), f)
    if r > 0:
        hs[:, r - 1] = 1.0
    m["hsel"] = hs
    ic = np.zeros((128, 4, 16), f)
    for g, w in enumerate((2, 4, 8, 16)):
        ic[:, g, :] = 1.0 / np.minimum(w, 1024 * r + np.arange(16) + 1).astype(f)
    m["icnt"] = ic
    m["sconv"] = np.ascontiguousarray(inp["state_conv"][:, c].reshape(L, 30, 8, 128).transpose(0, 3, 2, 1))
    m["spool"] = np.ascontiguousarray(inp["state_pool"][:, c].reshape(L, 15, 8, 128).transpose(0, 3, 2, 1))
    m["sffn"] = np.ascontiguousarray(inp["state_ffn"][:, c].reshape(L, 2, NFF, 128).transpose(0, 3, 1, 2))
    m["cacheT"] = np.ascontiguousarray(np.concatenate([inp["cache_ckv"][:, c].transpose(0, 2, 1),
                                                       inp["cache_krope"][:, c].transpose(0, 2, 1)], axis=1))
    m["cacheV"] = np.ascontiguousarray(inp["cache_ckv"][:, c])
    return m


WNAMES = ("wmod", "wA", "wB", "wQ", "wKVR", "wHD", "wUV", "wM4", "wOUT", "wUP", "wDN")


def shard_shared(sh, c):
    m = {"pf": sh["pf"], "gkv": sh["gkv"]}
    for name in WNAMES:
        a = sh[name]
        E = a.shape[-1]
        a2 = a.reshape(L, -1, E)
        for l in range(L):
            m[f"{name}{l}"] = a2[l]
    return m


_NC = None


def kernel(**inputs):
    global _NC
    inp = {k: np.asarray(v, dtype=np.float32) for k, v in inputs.items()}
    if _NC is None:
        _NC = build_program()
    nc = _NC
    sh = prep_shared(inp)
    in_maps = []
    for c in range(8):
        m = prep_core(inp, c)
        m.update(shard_shared(sh, c))
        m = {k: v for k, v in m.items() if k in nc._ext_in}
        in_maps.append(m)
    res = run_bass_kernel_spmd(nc, in_maps, core_ids=list(range(8)))
    R = res.results
    f = np.float32
    y_p = np.zeros((2, 4096, D), f); y_s = np.zeros((8, 32, D), f)
    p_ckv = np.zeros((L, 2, 4096, 512), f); p_kr = np.zeros((L, 2, 4096, 64), f)
    p_conv = np.zeros((L, 2, 30, 1024), f); p_pool = np.zeros((L, 2, 15, 1024), f); p_ffn = np.zeros((L, 2, 2, DFF), f)
    s_ckv = np.zeros((L, 8, 32, 512), f); s_kr = np.zeros((L, 8, 32, 64), f)
    s_conv = np.zeros((L, 8, 30, 1024), f); s_pool = np.zeros((L, 8, 15, 1024), f); s_ffn = np.zeros((L, 8, 2, DFF), f)
    for c in range(8):
        b, r = c // 4, c % 4
        o = R[c]
        y_p[b, 1024 * r:1024 * r + 1024] = o["o_y"][:1024]
        y_s[c] = o["o_y"][1024:]
        p_ckv[:, b, 1024 * r:1024 * r + 1024] = o["o_ckv"][:, :1024]
        s_ckv[:, c] = o["o_ckv"][:, 1024:]
        p_kr[:, b, 1024 * r:1024 * r + 1024] = o["o_kr"][:, :1024]
        s_kr[:, c] = o["o_kr"][:, 1024:]
        if r == 3:
            p_conv[:, b] = o["o_conv"][:, 0]
            p_pool[:, b] = o["o_pool"][:, 0]
            p_ffn[:, b] = o["o_ffn"][:, 0]
        s_conv[:, c] = o["o_conv"][:, 1]
        s_pool[:, c] = o["o_pool"][:, 1]
        s_ffn[:, c] = o["o_ffn"][:, 1]
    return (y_p, y_s, p_ckv, p_kr, p_conv, p_pool, p_ffn, s_ckv, s_kr, s_conv, s_pool, s_ffn)
```

```python
import numpy as np
from contextlib import ExitStack
import concourse.bass as bass
import concourse.mybir as mybir
from concourse.bass_utils import run_bass_kernel_spmd

F32 = mybir.dt.float32
BF16 = mybir.dt.bfloat16
AF = mybir.ActivationFunctionType
ALU = mybir.AluOpType
AX = mybir.AxisListType

L = 2
D = 2048
T = 1056
TP = 1024
TS = 32
NT = 352
DFF = 5632
NFF = 44
EPS = 1e-6
ATTN_SCALE = 192.0 ** -0.5
OFF_A, OFF_B, OFF_Q, OFF_KV, OFF_R, OFF_G = 0, 2048, 3072, 3584, 4096, 4160
PFL = 628
PF = dict(gpm=0, gpo=16, gpf=32, gpof=48, psc=64, bdwa=80, lng=88, lnb=96, gq=104, bdwf=108, bmod=152,
          wdwa=248, wdwf=496)
WCAP = 4096
NB = 3
ARENA = 163840


def dsize(dt):
    s = str(dt)
    if "64" in s:
        return 8
    if "32" in s:
        return 4
    if "16" in s:
        return 2
    return 1


class Op:
    __slots__ = ("eng", "seq", "fn", "waits", "kind", "gid", "qidx")


class KB:
    COMPUTE = ("pe", "act", "dve", "pool")
    EPOCH = 30000
    ND = {"sp": 16, "pool": 8}

    def __init__(self, nc):
        self.nc = nc
        self.ops = {e: [] for e in ("pe", "act", "dve", "pool", "sp")}
        self.state = {}
        self.ext_in = set()
        self.ext_out = set()
        self.known = {e: {} for e in self.ops}
        self.known_d = {e: set() for e in self.ops}
        self.needed = set()
        self.dmaq = {"sp": [], "pool": []}
        self.ccs = []
        self.gcount = 0
        self.allops = []

    def _span(self, ap):
        name = ap.tensor.name
        dsz = dsize(ap.dtype)
        dims = ap.ap
        off = ap.offset
        if name.startswith("sb") or name.startswith("ps"):
            row = dims[0][0]
            col = off % row if row > 0 else off
            ext = 1
            for st, cnt in dims[1:]:
                ext += (cnt - 1) * abs(st)
            page = 256 if name.startswith("sb") else 2048
            lo = col * dsz
            hi = (col + ext) * dsz - 1
        else:
            ext = 1
            for st, cnt in dims:
                ext += (cnt - 1) * abs(st)
            page = 65536
            lo = off * dsz
            hi = (off + ext) * dsz - 1
        return name, lo // page, hi // page

    def rec(self, eng, fn, ins, outs, kind="c"):
        op = Op()
        op.eng = eng
        op.fn = fn
        op.kind = kind
        op.seq = len(self.ops[eng])
        op.gid = self.gcount
        self.gcount += 1
        deps = set()
        for ap in ins:
            name, p0, p1 = self._span(ap)
            if name in self.ext_in:
                continue
            st = self.state.setdefault(name, {})
            isps = name.startswith("ps")
            for p in range(p0, p1 + 1):
                s = st.get(p)
                if s is not None and s[0] is not None:
                    deps.add(s[0])
                if isps and s is not None:
                    for e2, rop in s[1].items():
                        if e2 != eng:
                            deps.add(rop)
        for ap in outs:
            name, p0, p1 = self._span(ap)
            if name in self.ext_out:
                continue
            st = self.state.setdefault(name, {})
            for p in range(p0, p1 + 1):
                s = st.get(p)
                if s is not None:
                    if s[0] is not None:
                        deps.add(s[0])
                    deps.update(s[1].values())
                    deps.update(s[2])
        need_c = {}
        need_d = []
        for d in deps:
            if d.kind == "c":
                if eng == "pe" and d.eng == "pe":
                    continue
                if need_c.get(d.eng, -1) < d.seq:
                    need_c[d.eng] = d.seq
            else:
                need_d.append(d)
        waits = []
        kn = self.known[eng]
        for pe, sq in need_c.items():
            if kn.get(pe, -1) >= sq:
                continue
            kn[pe] = sq
            waits.append(("c", pe, sq))
            self.needed.add((pe, sq))
        kd = self.known_d[eng]
        for d in need_d:
            if d.gid in kd:
                continue
            kd.add(d.gid)
            waits.append(("d", d))
        op.waits = waits
        for ap in ins:
            name, p0, p1 = self._span(ap)
            if name in self.ext_in:
                continue
            st = self.state[name]
            for p in range(p0, p1 + 1):
                s = st.get(p)
                if s is None:
                    s = [None, {}, []]
                    st[p] = s
                if kind == "c":
                    s[1][eng] = op
                else:
                    s[2].append(op)
        for ap in outs:
            name, p0, p1 = self._span(ap)
            if name in self.ext_out:
                continue
            st = self.state[name]
            for p in range(p0, p1 + 1):
                st[p] = [op, {}, []]
        self.ops[eng].append(op)
        if kind == "d":
            op.qidx = len(self.dmaq[eng])
            self.dmaq[eng].append(op)
        elif kind == "cc":
            op.qidx = len(self.ccs)
            self.ccs.append(op)
        return op

    def emit(self):
        nc = self.nc
        with ExitStack() as es:
            signo = {}
            cnt = {}
            for e in self.COMPUTE:
                n = 0
                for op in self.ops[e]:
                    if op.kind == "c" and (e, op.seq) in self.needed:
                        signo[(e, op.seq)] = n
                        n += 1
                cnt[e] = n
            csem = {}
            for e in self.COMPUTE:
                ne = cnt[e] // self.EPOCH + 1
                csem[e] = [es.enter_context(nc.semaphore(f"c_{e}_{i}")) for i in range(ne)]
            dsem = {q: [es.enter_context(nc.semaphore(f"d_{q}_{i}")) for i in range(self.ND[q])] for q in self.dmaq}
            ccsem = [es.enter_context(nc.semaphore(f"cc_{i}")) for i in range(len(self.ccs))]

            def ev(w):
                if w[0] == "c":
                    n = signo[(w[1], w[2])]
                    return csem[w[1]][n // self.EPOCH], n % self.EPOCH + 1
                d = w[1]
                if d.kind == "cc":
                    return ccsem[d.qidx], 1
                nd = self.ND[d.eng]
                return dsem[d.eng][d.qidx % nd], 16 * (d.qidx // nd + 1)

            block = es.enter_context(nc.Block())

            def run(ename, e):
                for op in self.ops[ename]:
                    ws = list(op.waits)
                    emb = None
                    if op.kind == "c" and ws:
                        emb = ws.pop()
                    for w in ws:
                        s, v = ev(w)
                        e.wait_ge(s, v)
                    if op.kind == "d":
                        nd = self.ND[ename]
                        if op.qidx >= nd:
                            e.wait_ge(dsem[ename][op.qidx % nd], 16 * (op.qidx // nd))
                        op.fn(e).then_inc(dsem[ename][op.qidx % nd], 16)
                    elif op.kind == "cc":
                        op.fn(e).then_inc(ccsem[op.qidx])
                    else:
                        ins = op.fn(e)
                        if emb is not None:
                            s, v = ev(emb)
                            ins._wait_ge(s, v)
                        key = (ename, op.seq)
                        if key in signo:
                            n = signo[key]
                            ins.then_inc(csem[ename][n // self.EPOCH], 1)
                if ename in self.dmaq:
                    nd = self.ND[ename]
                    q = self.dmaq[ename]
                    for i in range(min(nd, len(q))):
                        last = ((len(q) - 1 - i) // nd) * nd + i
                        e.wait_ge(dsem[ename][i], 16 * (last // nd + 1))
                if ename == "pool":
                    for i in range(len(self.ccs)):
                        e.wait_ge(ccsem[i], 1)

            @block.tensor
            def _(e):
                run("pe", e)

            @block.scalar
            def _(e):
                run("act", e)

            @block.vector
            def _(e):
                run("dve", e)

            @block.gpsimd
            def _(e):
                run("pool", e)

            @block.sync
            def _(e):
                run("sp", e)


class _Stop(Exception):
    pass


STAGE = None


def build_program():
    nc = bass.Bass("TRN2", target_bir_lowering=False)
    K = KB(nc)

    def ck(n):
        if STAGE is not None and n > STAGE:
            raise _Stop()

    import os
    DUMP = os.environ.get("DBG_DUMP", "") != ""

    def dump(name, ap2d, dt, ncols):
        if not DUMP:
            return
        K.ext_out.add("dbg_" + name)
        t_ = nc.dram_tensor("dbg_" + name, [128, ncols], dt, kind="ExternalOutput")
        K.rec("sp", lambda e: e.dma_start(out=t_[:, :], in_=ap2d), [ap2d], [t_[:, :]], kind="d")

    def din(name, shape, dt=F32):
        K.ext_in.add(name)
        return nc.dram_tensor(name, list(shape), dt, kind="ExternalInput")

    def dout(name, shape):
        K.ext_out.add(name)
        return nc.dram_tensor(name, list(shape), F32, kind="ExternalOutput")

    def dscr(name, shape, dt=F32):
        return nc.dram_tensor(name, list(shape), dt)

    d_xin = din("xin", [T, D])
    d_cT = din("cT", [128, 16, 2])
    d_pf = din("pf", [128, L * PFL])
    d_gkv = din("gkv", [L, 128, 512])
    d_ropeF = din("ropeF", [64, 2, T])
    d_ropeT = din("ropeT", [128, 9, 2, 64])
    d_qB = din("qB", [64, TP])
    d_khot = din("khot", [64, TP])
    d_hsel = din("hsel", [128, 4])
    d_icnt = din("icnt", [128, 4, 16])
    d_sconv = din("sconv", [L, 128, 8, 30])
    d_spool = din("spool", [L, 128, 8, 15])
    d_sffn = din("sffn", [L, 128, 2, NFF])
    if STAGE is None or STAGE >= 4:
        d_cacheT = din("cacheT", [L, 576, 4096])
        d_cacheV = din("cacheV", [L, 4096, 512])
    WSPEC = [("wmod", 48, 4096), ("wA", 8, 4096), ("wB", 4, 4096), ("wQ", 2, 4096), ("wKVR", 1, 9216),
             ("wHD", 16, 2048), ("wUV", 4, 2048), ("wM4", 16, 9472), ("wOUT", 8, 4096), ("wUP", NFF, 4096),
             ("wDN", 32, 2816)]
    FIRST = {"wmod": 0, "wKVR": 1, "wA": 2, "wB": 2, "wQ": 2, "wHD": 4, "wUV": 4, "wM4": 5, "wOUT": 6, "wUP": 7, "wDN": 8}
    w_full = {}
    for name, nblk, E in WSPEC:
        w_full[name] = []
        for l in range(L):
            need = STAGE is None or (l == 0 and STAGE >= FIRST[name])
            if need:
                w_full[name].append(din(f"{name}{l}", [nblk * 128, E]))
            else:
                w_full[name].append(nc.dram_tensor(f"d_wf_{name}{l}", [nblk * 128, E], F32))

    def gw(name, l, n):
        return w_full[name][l][n * 128:(n + 1) * 128, :]
    o_y = dout("o_y", [T, D])
    o_ckv = dout("o_ckv", [L, T, 512])
    o_kr = dout("o_kr", [L, T, 64])
    o_conv = dout("o_conv", [L, 2, 30, 1024])
    o_pool = dout("o_pool", [L, 2, 15, 1024])
    o_ffn = dout("o_ffn", [L, 2, 2, DFF])
    xs = [dscr(f"d_xs{f}", [128, T]) for f in range(16)]
    ys = [dscr(f"d_ys{f}", [128, T]) for f in range(16)]
    d_hsp = dscr("d_hsp", [128, 16 * T], BF16)
    d_sasp = dscr("d_sasp", [128, 8 * T], BF16)
    d_msp = dscr("d_msp", [128, 8 * T], BF16)
    d_bkv = [dscr(f"d_bkv{i}", [384, 1024], BF16) for i in range(3)]
    d_gkvb = [dscr(f"d_gkvb{i}", [1536, 1024], BF16) for i in range(3)]
    d_bh = dscr("d_bh", [128, 360])
    d_gh = dscr("d_gh", [512, 360])
    d_bh3 = dscr("d_bh3", [128, 88])
    d_gh3 = dscr("d_gh3", [512, 88])

    es = ExitStack()
    sbt = lambda name, shape, dt: es.enter_context(nc.sbuf_tensor(name, list(shape), dt))
    identf = sbt("sb_identf", [128, 128], F32)
    identb = sbt("sb_identb", [128, 128], BF16)
    ones = sbt("sb_ones", [128, 128], BF16)
    pf = sbt("sb_pf", [128, L * PFL], F32)
    modT = sbt("sb_mod", [128, L, 96, 2], F32)
    mv = sbt("sb_mv", [128, L, 6, 16, 2], F32)
    ropeF = sbt("sb_ropeF", [64, 2, T], F32)
    hsel = sbt("sb_hsel", [128, 4], F32)
    icnt = sbt("sb_icnt", [128, 4, 16], F32)
    cTf = sbt("sb_cTf", [128, 16, 2], F32)
    cTb = sbt("sb_cTb", [128, 16, 2], BF16)
    small = sbt("sb_small", [128, 64], F32)
    wbuf = sbt("sb_wbuf", [128, NB, WCAP], BF16)
    mbuf = sbt("sb_mbuf", [128, 2048], BF16)
    arena = sbt("sb_arena", [128, ARENA // 4], F32)
    ps = es.enter_context(nc.psum_tensor("ps_all", [128, 4096], F32))

    def view(off, dt, shape):
        dsz = dsize(dt)
        n = int(np.prod(shape[1:]))
        nbytes = n * dsz
        assert off % 4 == 0 and nbytes % 4 == 0 and off + nbytes <= ARENA, (off, shape)
        a = arena[:, off // 4:(off + nbytes) // 4]
        if dt == BF16:
            a = a.bitcast(BF16)
        if len(shape) == 3:
            a = a.rearrange("p (a b) -> p a b", a=shape[1])
        elif len(shape) == 4:
            a = a.rearrange("p (a b c) -> p a b c", a=shape[1], b=shape[2])
        if shape[0] != 128:
            a = a[0:shape[0]]
        return a

    def bank(b, n=512, lo=0):
        return ps[:, 512 * b + lo:512 * b + lo + n]

    def bank_bf(b, lo_bytes, shape):
        n = int(np.prod(shape[1:]))
        a = ps[:, 512 * b + lo_bytes // 4:512 * b + lo_bytes // 4 + n // 2].bitcast(BF16)
        if len(shape) == 3:
            a = a.rearrange("p (a b) -> p a b", a=shape[1])
        return a

    def mm(out, lhsT, rhs, start=True, stop=True):
        K.rec("pe", lambda e: e.matmul(out, lhsT=lhsT, rhs=rhs, start=start, stop=stop), [lhsT, rhs], [out])

    def tr(out, in_, ident):
        K.rec("pe", lambda e: e.transpose(out, in_, ident), [in_, ident], [out])

    def act(out, in_, func, scale=None, bias=None):
        ins = [in_]
        kw = {}
        if scale is not None:
            kw["scale"] = scale
            if not isinstance(scale, float):
                ins.append(scale)
        if bias is not None:
            kw["bias"] = bias
            if not isinstance(bias, float):
                ins.append(bias)
        K.rec("act", lambda e: e.activation(out=out, in_=in_, func=func, **kw), ins, [out])

    def tt(eng, out, in0, in1, op):
        K.rec(eng, lambda e: e.tensor_tensor(out=out, in0=in0, in1=in1, op=op), [in0, in1], [out])

    def ts(eng, out, in0, s1, s2, op0, op1=None):
        ins = [in0] + [s for s in (s1, s2) if s is not None and not isinstance(s, float)]
        if op1 is None:
            K.rec(eng, lambda e: e.tensor_scalar(out=out, in0=in0, scalar1=s1, scalar2=None, op0=op0), ins, [out])
        else:
            K.rec(eng, lambda e: e.tensor_scalar(out=out, in0=in0, scalar1=s1, scalar2=s2, op0=op0, op1=op1), ins, [out])

    def stt(eng, out, in0, scalar, in1, op0, op1):
        ins = [in0, in1] + ([] if isinstance(scalar, float) else [scalar])
        K.rec(eng, lambda e: e.scalar_tensor_tensor(out=out, in0=in0, scalar=scalar, in1=in1, op0=op0, op1=op1), ins, [out])

    def cp(eng, out, in_):
        if eng == "act":
            K.rec("act", lambda e: e.copy(out=out, in_=in_), [in_], [out])
        else:
            K.rec(eng, lambda e: e.tensor_copy(out=out, in_=in_), [in_], [out])

    def recip(out, in_):
        K.rec("dve", lambda e: e.reciprocal(out=out, in_=in_), [in_], [out])

    def dma(q, out, in_):
        return K.rec(q, lambda e: e.dma_start(out=out, in_=in_), [in_], [out], kind="d")

    def allgather(src, dst, groups=((0, 1, 2, 3), (4, 5, 6, 7))):
        assert src.ap().nbytes() if False else True
        K.rec("pool", lambda e: e.collective_compute("AllGather", ALU.bypass, replica_groups=[list(g) for g in groups],
                                                     ins=[src.ap().opt()], outs=[dst.ap().opt()]),
              [src.ap()], [dst.ap()], kind="cc")

    cpi = [0]

    def cpa(out, in_):
        cpi[0] += 1
        cp("act" if cpi[0] % 2 else "dve", out, in_)

    def rsqrt_to(dst, src, scale):
        ts("dve", dst, src, scale, EPS, ALU.mult, ALU.add)
        act(dst, dst, AF.Sqrt)
        recip(dst, dst)

    plan = []
    for n in range(16):
        plan.append(("mod0a", gw("wmod", 0, n), 4096))
    for l in range(L):
        for j in range(8):
            plan.append((f"A{l}", gw("wA", l, j), 4096))
        for j in range(4):
            plan.append((f"B{l}", gw("wB", l, j), 4096))
        for j in range(2):
            plan.append((f"Q{l}", gw("wQ", l, j), 4096))
        for h in range(16):
            plan.append((f"HD{l}", gw("wHD", l, h), 2048))
        for j in range(4):
            plan.append((f"UV{l}", gw("wUV", l, j), 2048))
        for f in range(16):
            plan.append((f"M4{l}", gw("wM4", l, f)[:, 0:3072], 3072))
            plan.append((f"M4{l}", gw("wM4", l, f)[:, 3072:5376], 2304))
            plan.append((f"M4{l}", gw("wM4", l, f)[:, 5376:9472], 4096))
        for j in range(8):
            plan.append((f"OUT{l}", gw("wOUT", l, j), 4096))
        for j in range(NFF):
            plan.append((f"UP{l}", gw("wUP", l, j), 4096))
        for j in range(32):
            plan.append((f"DN{l}", gw("wDN", l, j), 2816))
    wstate = {"issued": 0, "next": 0}

    def wget(tag):
        n = wstate["next"]
        assert plan[n][0] == tag, (plan[n][0], tag, n)
        while wstate["issued"] < min(len(plan), n + wstate.get("depth", NB)):
            m = wstate["issued"]
            _, src, E = plan[m]
            dma("pool", wbuf[:, m % NB, 0:E], src)
            wstate["issued"] += 1
        wstate["next"] += 1
        return wbuf[:, n % NB, 0:plan[n][2]]

    def w3(wb, k, m):
        return wb.rearrange("p (k m) -> p k m", k=k)

    pbs = {"list": list(range(8)), "i": 0}

    def pb_set(lst):
        pbs["list"] = list(lst)
        pbs["i"] = 0

    def pb():
        b = pbs["list"][pbs["i"] % len(pbs["list"])]
        pbs["i"] += 1
        return b

    tiles3 = [(0, 352), (352, 704), (704, 1056)]

    def segs(lo, hi, H):
        out = []
        if lo < TP:
            e = min(hi, TP)
            out.append((0, e - lo, H + lo))
        if hi > TP:
            s = max(lo, TP)
            out.append((s - lo, hi - lo, s + 2 * H))
        return out

    def pfv(l, name, n):
        o = l * PFL + PF[name]
        return pf[:, o:o + n]

    A0 = 0
    hT = view(A0, BF16, [128, 16, T])
    XNEW = view(33792, F32, [128, 16, T])
    QLAT = view(150528, BF16, [128, 4, T])
    KTSN = view(158976, BF16, [128, 5, 32])
    VSN = view(159296, BF16, [128, 512])
    OT = view(107520, BF16, [128, 16, T])
    BC0 = view(140864, F32, [128, 1088])
    BC1 = view(145216, F32, [128, 1088])
    SM2 = view(160320, F32, [128, 880])

    K.rec("pool", lambda e: e.memset(identf[:], 0.0), [], [identf[:]])
    K.rec("pool", lambda e: e.affine_select(out=identf[:], in_=identf[:], pattern=[[-1, 128]], compare_op=ALU.not_equal,
                                            fill=1.0, base=0, channel_multiplier=1), [identf[:]], [identf[:]])
    cp("dve", identb[:], identf[:])
    K.rec("dve", lambda e: e.memset(ones[:], 1.0), [], [ones[:]])
    dma("sp", pf[:], d_pf[:, :])
    dma("sp", cTf[:], d_cT[:, :, :])
    dma("sp", ropeF[:], d_ropeF[:, :, :])
    dma("sp", hsel[:], d_hsel[:, :])
    dma("sp", icnt[:], d_icnt[:, :, :])
    act(cTb[:], cTf[:], AF.Silu)

    def mod_blocks(l, ns, tag):
        bk = 7
        for n in ns:
            wb = w3(wget(tag), 16, 256)
            for m in range(2):
                ch = 2 * n + m
                for k in range(16):
                    mm(bank(bk, 2, 2 * ch), wb[:, k, 128 * m:128 * m + 128], cTb[:, k, :], k == 0, k == 15)

    def mod_compute(l, n0, n1, tag):
        mod_blocks(l, range(n0, n1), tag)
        mod_finish(l, n0, n1)

    def mod_finish(l, n0, n1, psv=None):
        bk = 7
        c0, c1 = 2 * n0, 2 * n1
        if psv is None:
            psv = bank(bk, 192).rearrange("p (c s) -> p c s", s=2)[:, c0:c1, :]
        for s in range(2):
            tt("dve", modT[:, l, c0:c1, s], psv[:, :, s], pfv(l, "bmod", 96)[:, c0:c1], ALU.add)
        for s in range(2):
            tmp = small[:, 0:16]
            if c0 == 0:
                ts("dve", tmp, modT[:, l, 16:32, s], 1.0, None, ALU.add)
                tt("dve", mv[:, l, 0, :, s], tmp, pfv(l, "gpm", 16), ALU.mult)
                cp("dve", mv[:, l, 1, :, s], modT[:, l, 0:16, s])
            if c1 == 96:
                tt("dve", mv[:, l, 2, :, s], modT[:, l, 32:48, s], pfv(l, "gpo", 16), ALU.mult)
                ts("dve", tmp, modT[:, l, 64:80, s], 1.0, None, ALU.add)
                tt("dve", mv[:, l, 3, :, s], tmp, pfv(l, "gpf", 16), ALU.mult)
                cp("dve", mv[:, l, 4, :, s], modT[:, l, 48:64, s])
                tt("dve", mv[:, l, 5, :, s], modT[:, l, 80:96, s], pfv(l, "gpof", 16), ALU.mult)


    TOK = [view(101376, F32, [128, D]), view(109568, F32, [128, D])]
    pb_set([0, 1, 2, 3])
    for t9 in range(9):
        rows = 128 if t9 < 8 else 32
        tc0 = t9 * 128
        tok = TOK[t9 % 2]
        dma("sp", tok[0:rows, :], d_xin[tc0:tc0 + rows, :])
        for g in range(4):
            b = pb()
            for j in range(4):
                f = 4 * g + j
                tr(bank(b, rows, 128 * j), tok[0:rows, 128 * f:128 * f + 128], identf[0:rows, 0:rows])
            src = bank(b).rearrange("p (j t) -> p j t", j=4)[:, :, 0:rows]
            cpa(XNEW[:, 4 * g:4 * g + 4, tc0:tc0 + rows], src)

    mod_compute(0, 0, 16, "mod0a")

    def prenorm(l, sub, dst):
        SQ = [view(101376, BF16, [128, T]), view(103488, BF16, [128, T])]
        TMPF = [view(105600, F32, [128, T]), view(109824, F32, [128, T])]
        ssb = [5, 6, 7]
        for f in range(16):
            sq = SQ[f % 2]
            act(sq, XNEW[:, f, :], AF.Square)
            for i, (lo, hi) in enumerate(tiles3):
                mm(bank(ssb[i], NT), ones[:], sq[:, lo:hi], f == 0, f == 15)
            dma("sp", xs[f][:, :], XNEW[:, f, :])
        for i, (lo, hi) in enumerate(tiles3):
            rsqrt_to(BC0[:, lo:hi], bank(ssb[i], NT), 1.0 / D)
        ia, ib = (0, 1) if sub == 0 else (3, 4)
        for f in range(16):
            tmp = TMPF[f % 2]
            tt("dve", tmp, XNEW[:, f, :], BC0[:, 0:T], ALU.mult)
            ts("dve", dst[:, f, 0:TP], tmp[:, 0:TP], mv[:, l, ia, f, 0:1], mv[:, l, ib, f, 0:1], ALU.mult, ALU.add)
            ts("dve", dst[:, f, TP:T], tmp[:, TP:T], mv[:, l, ia, f, 1:2], mv[:, l, ib, f, 1:2], ALU.mult, ALU.add)

    def tail_out(src_fn, nch, w, dst):
        st = view(101376, F32, [128, 1024])
        for j in range(nch):
            tr(bank(2 + j // 4, 128, 128 * (j % 4))[0:w, :], src_fn(j), identf[:])
        for half in range(nch // 4):
            cpa(st[0:w, 512 * half:512 * half + 512], bank(2 + half)[0:w, :])
        dma("sp", dst, st[0:w, 0:nch * 128])

    def residual_pass(l, sub):
        YB = [view(101376, F32, [128, T]), view(105600, F32, [128, T])]
        TMPF = [view(109824, F32, [128, T]), view(114048, F32, [128, T])]
        for i, (lo, hi) in enumerate(tiles3):
            rsqrt_to(BC0[:, lo:hi], bank(5 + i, NT), 1.0 / D)
        ig = 2 if sub == 0 else 5
        for f in range(16):
            yb = YB[f % 2]
            tmp = TMPF[f % 2]
            dma("sp", yb, ys[f][:, :])
            dma("sp", XNEW[:, f, :], xs[f][:, :])
            tt("dve", tmp, yb, BC0[:, 0:T], ALU.mult)
            stt("dve", XNEW[:, f, 0:TP], tmp[:, 0:TP], mv[:, l, ig, f, 0:1], XNEW[:, f, 0:TP], ALU.mult, ALU.add)
            stt("dve", XNEW[:, f, TP:T], tmp[:, TP:T], mv[:, l, ig, f, 1:2], XNEW[:, f, TP:T], ALU.mult, ALU.add)

    def yproj_evac(b, f, lo, hi, i, YF, first, last):
        SQt = [view(141312, BF16, [128, NT]), view(142016, BF16, [128, NT])]
        sq = SQt[(f * 3 + i) % 2]
        cp("act", YF[:, lo:hi], bank(b, NT))
        act(sq, bank(b, NT), AF.Square)
        mm(bank(5 + i, NT), ones[:], sq, first, last)

    def layer(l):
        UAT = view(33792, BF16, [128, 8, 1116])
        ZBT = view(51648, BF16, [128, 8, 1086])
        KTST = view(85920, BF16, [128, 5, T])
        VST = view(96480, BF16, [128, 9, 512])
        WKVR = view(105696, BF16, [128, 16, 576])
        ZQST = view(105696, F32, [128, 4, T])
        UATAIL = view(124128, F32, [128, 8, 60])
        ZBTAIL = view(126048, F32, [128, 8, 30])
        GHB = view(127008, F32, [128, 4, 360])
        ROPET = view(132768, F32, [128, 9, 2, 64])
        GKV = view(137376, F32, [128, 512])
        SCONV = view(139424, F32, [128, 8, 30])
        SPOOL = view(140384, F32, [128, 8, 15])
        JUNK = [view(69024, F32, [128, 512]), view(71072, F32, [128, 512])]
        CKVF = [view(73120, F32, [128, 512]), view(75168, F32, [128, 512])]
        KRU = [view(77216, F32, [128, 64]), view(77472, F32, [128, 64])]
        KRV = [view(77728, F32, [128, 64]), view(77984, F32, [128, 64])]
        KRF = [view(78240, F32, [128, 64]), view(78496, F32, [128, 64])]
        KRB = [view(78752, BF16, [128, 64]), view(78880, BF16, [128, 64])]
        SGT = [view(79008, F32, [128, NT]), view(80416, F32, [128, NT])]
        SQT = [view(81824, BF16, [128, NT]), view(82528, BF16, [128, NT])]

        dma("pool", WKVR.rearrange("p k m -> p (k m)"), gw("wKVR", l, 0))
        dma("sp", ROPET, d_ropeT[:, :, :, :])
        dma("sp", GKV, d_gkv[l])
        dma("pool", KTST[64:128, 4, 0:TP], d_khot[:, :])
        dma("sp", SCONV, d_sconv[l])
        dma("sp", SPOOL, d_spool[l])
        for t9 in range(9):
            rows = 128 if t9 < 8 else 32
            tc0 = t9 * 128
            i2 = t9 % 2
            bx, by = (0, 1) if i2 == 0 else (2, 3)
            psx = bank(bx)[0:rows, :]
            psy = bank(by, 64)[0:rows, :]
            for k in range(16):
                mm(psx, hT[:, k, tc0:tc0 + rows], WKVR[:, k, 0:512], k == 0, k == 15)
            for k in range(16):
                mm(psy, hT[:, k, tc0:tc0 + rows], WKVR[:, k, 512:576], k == 0, k == 15)
            junk = JUNK[i2][0:rows]
            ssk = small[0:rows, 16 + t9:17 + t9]
            act(junk, psx, AF.Square)
            K.rec("dve", lambda e, ssk=ssk, junk=junk: e.reduce_sum(out=ssk, in_=junk, axis=AX.X), [junk], [ssk])
            rsqrt_to(ssk, ssk, 1.0 / 512)
            ckvf = CKVF[i2][0:rows]
            stt("dve", ckvf, psx, ssk, GKV[0:rows], ALU.mult, ALU.mult)
            dma("sp", o_ckv[l, tc0:tc0 + rows, :], ckvf)
            cp("act", VST[0:rows, t9, :], ckvf)
            pst = bank_bf(4 + i2, 0, [128, 4, 128])
            for c in range(4):
                tr(pst[:, c, 0:rows], VST[0:rows, t9, 128 * c:128 * c + 128], identb[0:rows, 0:rows])
            cpa(KTST[:, 0:4, tc0:tc0 + rows], pst[:, :, 0:rows])
            kru, krv, krf, krb = KRU[i2][0:rows], KRV[i2][0:rows], KRF[i2][0:rows], KRB[i2][0:rows]
            tt("dve", kru, psy, ROPET[0:rows, t9, 0, :], ALU.mult)
            tt("dve", krv[:, 0:32], psy[:, 32:64], ROPET[0:rows, t9, 1, 0:32], ALU.mult)
            tt("dve", krv[:, 32:64], psy[:, 0:32], ROPET[0:rows, t9, 1, 32:64], ALU.mult)
            tt("dve", krf, kru, krv, ALU.add)
            dma("sp", o_kr[l, tc0:tc0 + rows, :], krf)
            cp("act", krb, krf)
            pst2 = bank_bf(6 + i2, 0, [128, 128])
            tr(pst2[0:64, 0:rows], krb, identb[0:rows, 0:rows])
            cpa(KTST[0:64, 4, tc0:tc0 + rows], pst2[0:64, 0:rows])
        ck(1.5)
        dma("sp", d_bkv[0][0:384, :].rearrange("(c p) k -> p c k", p=128), KTST[:, 0:3, 0:TP])
        dma("sp", d_bkv[1][0:256, :].rearrange("(c p) k -> p c k", p=128), KTST[:, 3:5, 0:TP])
        dma("sp", d_bkv[1][256:384, :].rearrange("r (x d) -> (r x) d", d=512).rearrange("(t p) d -> p t d", p=128),
            VST[:, 0:2, :])
        dma("sp", d_bkv[2][0:384, :].rearrange("r (x d) -> (r x) d", d=512).rearrange("(t p) d -> p t d", p=128),
            VST[:, 2:8, :])
        cp("dve", KTSN, KTST[:, :, TP:T])
        cp("dve", VSN[0:32, :], VST[0:32, 8, :])
        ck(1.7)
        for i3 in range(3):
            allgather(d_bkv[i3], d_gkvb[i3])

        ck(2)
        pb_set([0, 1, 2, 3, 4, 5, 6, 7])
        for j in range(8):
            wb = w3(wget(f"A{l}"), 16, 256)
            for i, (lo, hi) in enumerate(tiles3):
                bu, bg = pb(), pb()
                for k in range(16):
                    mm(bank(bu, NT), wb[:, k, 0:128], hT[:, k, lo:hi], k == 0, k == 15)
                for k in range(16):
                    mm(bank(bg, NT), wb[:, k, 128:256], hT[:, k, lo:hi], k == 0, k == 15)
                sg = SGT[i % 2]
                act(sg, bank(bg, NT), AF.Sigmoid)
                for (a, b_, dlo) in segs(lo, hi, 30):
                    tt("dve", UAT[:, j, dlo:dlo + (b_ - a)], bank(bu, NT)[:, a:b_], sg[:, a:b_], ALU.mult)
                if i == 2:
                    tt("dve", UATAIL[:, j, 0:30], bank(bu, NT)[:, 290:320], sg[:, 290:320], ALU.mult)
                    tt("dve", UATAIL[:, j, 30:60], bank(bu, NT)[:, 322:352], sg[:, 322:352], ALU.mult)
        for n in range(4):
            wb = w3(wget(f"B{l}"), 16, 256)
            for m in range(2):
                ch = 2 * n + m
                for i, (lo, hi) in enumerate(tiles3):
                    b = pb()
                    for k in range(16):
                        mm(bank(b, NT), wb[:, k, 128 * m:128 * m + 128], hT[:, k, lo:hi], k == 0, k == 15)
                    for (a, b_, dlo) in segs(lo, hi, 15):
                        cpa(ZBT[:, ch, dlo:dlo + (b_ - a)], bank(b, NT)[:, a:b_])
                    if i == 2:
                        cp("dve", ZBTAIL[:, ch, 0:15], bank(b, NT)[:, 305:320])
                        cp("dve", ZBTAIL[:, ch, 15:30], bank(b, NT)[:, 337:352])
        dma("sp", d_bh[:, 0:240].rearrange("p (j t) -> p j t", j=8), UATAIL[:, :, 0:30])
        dma("sp", d_bh[:, 240:360].rearrange("p (j t) -> p j t", j=8), ZBTAIL[:, :, 0:15])
        allgather(d_bh, d_gh)
        tail_out(lambda j: UATAIL[:, j, 0:30], 8, 30, o_conv[l, 0])
        tail_out(lambda j: UATAIL[:, j, 30:60], 8, 30, o_conv[l, 1])
        tail_out(lambda j: ZBTAIL[:, j, 0:15], 8, 15, o_pool[l, 0])
        tail_out(lambda j: ZBTAIL[:, j, 15:30], 8, 15, o_pool[l, 1])

        pb_set([0, 1, 2, 3, 4])
        for n in range(2):
            wb = w3(wget(f"Q{l}"), 16, 256)
            for m in range(2):
                ch = 2 * n + m
                for i, (lo, hi) in enumerate(tiles3):
                    b = pb()
                    for k in range(16):
                        mm(bank(b, NT), wb[:, k, 128 * m:128 * m + 128], hT[:, k, lo:hi], k == 0, k == 15)
                    cp("act", ZQST[:, ch, lo:hi], bank(b, NT))
                    sq = SQT[i % 2]
                    act(sq, bank(b, NT), AF.Square)
                    mm(bank(5 + i, NT), ones[:], sq, ch == 0, ch == 3)
        for i, (lo, hi) in enumerate(tiles3):
            rsqrt_to(BC0[:, lo:hi], bank(5 + i, NT), 1.0 / 512)
        for ch in range(4):
            stt("dve", QLAT[:, ch, :], ZQST[:, ch, :], pfv(l, "gq", 4)[:, ch:ch + 1], BC0[:, 0:T], ALU.mult, ALU.mult)
        dma("sp", d_hsp[:, :], hT.rearrange("p a b -> p (a b)"))
        if l == 0:
            dump("hT", hT.rearrange("p a b -> p (a b)"), BF16, 16 * T)
            dump("qlat", QLAT.rearrange("p a b -> p (a b)"), BF16, 4 * T)

        ck(3)
        HT = view(69024, F32, [128, 360])
        dma("sp", GHB, d_gh.ap().rearrange("(r p) f -> p r f", p=128))
        ts("dve", HT, GHB[:, 0, :], hsel[:, 0:1], None, ALU.mult)
        for r in range(1, 4):
            stt("dve", HT, GHB[:, r, :], hsel[:, r:r + 1], HT, ALU.mult, ALU.add)
        cp("dve", UAT[:, :, 0:30], HT[:, 0:240].rearrange("p (j t) -> p j t", j=8))
        cp("dve", ZBT[:, :, 0:15], HT[:, 240:360].rearrange("p (j t) -> p j t", j=8))
        cp("dve", UAT[:, :, 1054:1084], SCONV)
        cp("dve", ZBT[:, :, 1039:1054], SPOOL)
        MT = view(120672, BF16, [128, 8, T])
        SAT = view(103776, BF16, [128, 8, T])
        AT = view(69024, F32, [128, 8, 1086])
        P = [view(15872, F32, [128, 2, 1086]), view(24560, F32, [128, 2, 1086])]
        T16 = view(33248, F32, [128, 16])
        for g in range(4):
            src = ZBT[:, 2 * g:2 * g + 2, :]
            w = 2 ** (g + 1)
            cur = src
            for i in range(g + 1):
                st = 2 ** i
                dst = P[i % 2]
                tt("dve", dst[:, :, st:1086], cur[:, :, st:1086], cur[:, :, 0:1086 - st], ALU.add)
                cur = dst
            stt("dve", MT[:, 2 * g:2 * g + 2, 0:TP], cur[:, :, 15:15 + TP], 1.0 / w, src[:, :, 15:15 + TP], ALU.mult, ALU.subtract)
            stt("dve", MT[:, 2 * g:2 * g + 2, TP:T], cur[:, :, 1054:1086], 1.0 / w, src[:, :, 1054:1086], ALU.mult, ALU.subtract)
            for c2 in range(2):
                tt("dve", T16, cur[:, c2, 15:31], icnt[:, g, :], ALU.mult)
                tt("dve", MT[:, 2 * g + c2, 0:16], T16, src[:, c2, 15:31], ALU.subtract)
        DG = [view(0, BF16, [128, 31, 128]), view(7936, BF16, [128, 31, 128])]
        SQc = [view(160320, BF16, [128, 362]), view(161044, BF16, [128, 362])]
        ABc = [view(161768, BF16, [128, 362]), view(162492, BF16, [128, 362])]
        ctiles = [(0, 362), (362, 724), (724, 1086)]
        wd = pfv(l, "wdwa", 248).rearrange("p (j k) -> p j k", j=8)
        for j in range(8):
            dg = DG[j % 2]
            for k in range(31):
                if k % 2 == 0:
                    ts("dve", dg[:, k, :], identb[:], wd[:, j, k:k + 1], None, ALU.mult)
                else:
                    act(dg[:, k, :], identb[:], AF.Copy, scale=wd[:, j, k:k + 1])
            for i, (lo, hi) in enumerate(ctiles):
                b = i if j % 2 == 0 else 3 + i
                b = [0, 1][(3 * j + i) % 2]
                for k in range(31):
                    mm(bank(b, 362), dg[:, k, :], UAT[:, j, lo + k:lo + k + 362], k == 0, k == 30)
                ts("dve", AT[:, j, lo:hi], bank(b, 362), pfv(l, "bdwa", 8)[:, j:j + 1], None, ALU.add)
                sq, ab = SQc[i % 2], ABc[i % 2]
                act(sq, AT[:, j, lo:hi], AF.Square)
                cp("act", ab, AT[:, j, lo:hi])
                mm(bank(2 + i, 362), ones[:], sq, j == 0, j == 7)
                mm(bank(5 + i, 362), ones[:], ab, j == 0, j == 7)
        TMPL = view(0, F32, [128, 1088])
        for i, (lo, hi) in enumerate(ctiles):
            ts("dve", BC1[:, lo:hi], bank(5 + i, 362), 1.0 / 1024, None, ALU.mult)
            tt("dve", TMPL[:, lo:hi], BC1[:, lo:hi], BC1[:, lo:hi], ALU.mult)
            stt("dve", BC0[:, lo:hi], bank(2 + i, 362), 1.0 / 1024, TMPL[:, lo:hi], ALU.mult, ALU.subtract)
            ts("dve", BC0[:, lo:hi], BC0[:, lo:hi], EPS, None, ALU.add)
            act(BC0[:, lo:hi], BC0[:, lo:hi], AF.Sqrt)
            recip(BC0[:, lo:hi], BC0[:, lo:hi])
        for j in range(8):
            tt("dve", AT[:, j, :], AT[:, j, :], BC1[:, 0:1086], ALU.subtract)
            tt("dve", AT[:, j, :], AT[:, j, :], BC0[:, 0:1086], ALU.mult)
            ts("dve", AT[:, j, :], AT[:, j, :], pfv(l, "lng", 8)[:, j:j + 1], pfv(l, "lnb", 8)[:, j:j + 1], ALU.mult, ALU.add)
            act(SAT[:, j, 0:TP], AT[:, j, 0:TP], AF.Silu)
            act(SAT[:, j, TP:T], AT[:, j, 1054:1086], AF.Silu)
        if l == 0:
            dump("sat", SAT.rearrange("p a b -> p (a b)"), BF16, 8 * T)
            dump("mt", MT.rearrange("p a b -> p (a b)"), BF16, 8 * T)
        dma("sp", d_sasp[:, :], SAT.rearrange("p a b -> p (a b)"))
        dma("sp", d_msp[:, :], MT.rearrange("p a b -> p (a b)"))

        ck(4)
        QF = [view(0, BF16, [128, 5, TP]), view(10240, BF16, [128, 5, TP])]
        QH1 = view(20480, BF16, [128, T])
        QH = [QH1, QH1]
        ACCS = view(22592, F32, [128, 512])
        PT = [view(24704 + 1024 * i, BF16, [128, 512]) for i in range(4)]
        ONORM = view(28800, BF16, [128, 4, 512])
        RS = view(32896, F32, [128, 8])
        KT = view(33792, BF16, [128, 5, 4096])
        VV = view(74752, BF16, [128, 32, 512])
        ONT = view(141312, BF16, [128, 4, 512])
        QS = view(145408, BF16, [128, 5, 512])
        RT1 = view(160320, F32, [128, NT])
        RT2 = view(161728, F32, [128, NT])
        dma("pool", QF[0][64:128, 4, :], d_qB[:, :])
        dma("pool", QF[1][64:128, 4, :], d_qB[:, :])
        for g in range(4):
            r0 = 384 * g
            dma("sp", KT[:, 0:3, 1024 * g:1024 * g + 1024], d_gkvb[0][r0:r0 + 384, :].rearrange("(c p) k -> p c k", p=128))
            dma("sp", KT[:, 3:5, 1024 * g:1024 * g + 1024], d_gkvb[1][r0:r0 + 256, :].rearrange("(c p) k -> p c k", p=128))
            dma("sp", VV[:, 8 * g:8 * g + 2, :],
                d_gkvb[1][r0 + 256:r0 + 384, :].rearrange("r (x d) -> (r x) d", d=512).rearrange("(t p) d -> p t d", p=128))
            dma("sp", VV[:, 8 * g + 2:8 * g + 8, :],
                d_gkvb[2][r0:r0 + 384, :].rearrange("r (x d) -> (r x) d", d=512).rearrange("(t p) d -> p t d", p=128))
        UB = 7
        SUMS = bank(6, 8, 0)
        onesf = small[:, 40:41]
        K.rec("dve", lambda e: e.memset(onesf, 1.0), [], [onesf])

        def units(h):
            wb = wget(f"HD{l}")
            hp = h % 2
            wqN = wb[:, 0:512].rearrange("p (k m) -> p k m", k=4)
            wqR = wb[:, 512:768].rearrange("p (k m) -> p k m", k=4)
            wqS = wb[:, 768:1024].rearrange("p (k m) -> p k m", k=4)
            wuk = wb[:, 1024:1536]
            wuv = wb[:, 1536:2048].rearrange("p (k m) -> p k m", k=4)
            us = []
            for i, (lo, hi) in enumerate(tiles3):
                def u1(lo=lo, hi=hi):
                    b = bank(UB, NT)
                    for k in range(4):
                        mm(b, wqN[:, k, :], QLAT[:, k, lo:hi], k == 0, k == 3)
                    cp("act", QH[hp][:, lo:hi], b)
                us.append(u1)
                for rc in range(4):
                    def u2(lo=lo, hi=hi, rc=rc, i=i):
                        b = bank(UB, NT)
                        mm(b, wuk[:, 128 * rc:128 * rc + 128], QH[hp][:, lo:hi], True, True)
                        pe_ = min(hi, TP)
                        cp("act", QF[hp][:, rc, lo:pe_], b[:, 0:pe_ - lo])
                        if i == 2:
                            cp(os.environ.get("DBG_QSE", "dve"), QS[:, rc, 32 * h:32 * h + 32], b[:, 320:352])
                    us.append(u2)

                def u3a(lo=lo, hi=hi):
                    b = bank(UB, NT)[0:64, :]
                    for k in range(4):
                        mm(b, wqR[:, k, :], QLAT[:, k, lo:hi], k == 0, k == 3)
                    tt("dve", RT1[0:64, :], b, ropeF[:, 0, lo:hi], ALU.mult)
                us.append(u3a)

                def u3b(lo=lo, hi=hi, i=i):
                    b = bank(UB, NT)[0:64, :]
                    for k in range(4):
                        mm(b, wqS[:, k, :], QLAT[:, k, lo:hi], k == 0, k == 3)
                    tt("dve", RT2[0:64, :], b, ropeF[:, 1, lo:hi], ALU.mult)
                    pe_ = min(hi, TP)
                    tt("dve", QF[hp][0:64, 4, lo:pe_], RT1[0:64, 0:pe_ - lo], RT2[0:64, 0:pe_ - lo], ALU.add)
                    if i == 2:
                        tt("dve", QS[0:64, 4, 32 * h:32 * h + 32], RT1[0:64, 320:352], RT2[0:64, 320:352], ALU.add)
                us.append(u3b)
            return us, wuv

        def attention(qchunk, ktiles, par, hook):
            n = len(ktiles)

            def pv(kt):
                _, vap, nk = ktiles[kt]
                p_ = PT[kt % 4]
                for s in range(4):
                    mm(bank(2 + s), p_[0:nk, 128 * s:128 * s + 128], vap, kt == 0, kt == n - 1)
                if kt == 0:
                    cp("dve", ACCS[0:nk, :], p_[0:nk, :])
                else:
                    tt("dve", ACCS[0:nk, :], ACCS[0:nk, :], p_[0:nk, :], ALU.add)
            for kt in range(n):
                kfn, _, nk = ktiles[kt]
                sb_ = bank(kt % 2)[0:nk, :]
                for c in range(5):
                    mm(sb_, kfn(c), qchunk(c), c == 0, c == 4)
                if kt >= 2:
                    pv(kt - 2)
                act(PT[kt % 4][0:nk, :], sb_, AF.Exp, scale=ATTN_SCALE)
                hook(kt)
            pv(n - 2)
            pv(n - 1)
            for s in range(4):
                mm(SUMS[:, 4 * par + s:4 * par + s + 1], ACCS[:, 128 * s:128 * s + 128], onesf, True, True)

        def tail_evac(par):
            recip(RS[:, 4 * par:4 * par + 4], SUMS[:, 4 * par:4 * par + 4])
            for s in range(4):
                if s % 2 == 0:
                    ts("dve", ONORM[:, s, :], bank(2 + s), RS[:, 4 * par + s:4 * par + s + 1], None, ALU.mult)
                else:
                    act(ONORM[:, s, :], bank(2 + s), AF.Copy, scale=RS[:, 4 * par + s:4 * par + s + 1])

        def tail_tr():
            tb = bank_bf(7, 0, [128, 4, 128])
            for s in range(4):
                for rc in range(4):
                    tr(tb[:, rc, :], ONORM[:, s, 128 * rc:128 * rc + 128], identb[:])
                cpa(ONT[:, :, 128 * s:128 * s + 128], tb)

        def tail_pe(h, qb, wuv):
            tail_tr()
            for half in range(2):
                wbk = bank(7, 256, 256)
                for rc in range(4):
                    mm(wbk, wuv[:, rc, :], ONT[:, rc, 256 * half:256 * half + 256], rc == 0, rc == 3)
                cpa(OT[:, h, 512 * qb + 256 * half:512 * qb + 256 * half + 256], wbk)

        import os
        NH = int(os.environ.get("DBG_NH", "16"))
        NOS = os.environ.get("DBG_NOS", "") != ""
        ck(4.1)
        wstate["depth"] = 2
        modq = []
        if l == 0 and STAGE is None:
            modq = [(0, n, m, 2 * n - 32 + m) for n in range(16, 48) for m in range(2)] + \
                   [(1, n, m, 64 + 2 * n + m) for n in range(48) for m in range(2)]

        def mod_stream_one():
            lm, n, m, cc = modq.pop(0)
            src = gw("wmod", lm, n).rearrange("p (k m) -> p k m", m=256)[:, :, 128 * m:128 * m + 128]
            wbm = mbuf[:, :].rearrange("p (k m) -> p k m", m=128)
            dma("pool", wbm, src)
            for k in range(16):
                mm(bank(6, 2, 64 + 2 * cc), wbm[:, k, :], cTb[:, k, :], k == 0, k == 15)
        us0, wuv_cur = units(0)
        NU = int(os.environ.get("DBG_NU", "99"))
        for u in us0[:NU]:
            u()
        ck(4.2)
        pend = []
        it = 0
        nxt = {}
        for h in range(NH):
            usn, wuv_next = [], None
            hp = h % 2
            for qb in range(2):
                par = it % 2
                it += 1
                q0 = 512 * qb
                ktl = [((lambda c, kt=kt: KT[:, c, 128 * kt:128 * kt + 128]), VV[:, kt, :], 128) for kt in range(32)]
                upos = [0]

                def hook(kt, qb=qb, h=h):
                    if kt == 4 and pend:
                        pend.pop(0)()
                    if modq and kt % 6 == 1:
                        mod_stream_one()
                    if kt == 5 and qb == 0 and h < NH - 1:
                        u_, w_ = units(h + 1)
                        usn.extend(u_)
                        nxt["wuv"] = w_
                    tot = qb * 32 + kt
                    want = (tot * len(usn)) // 60 if usn else 0
                    while usn and upos_g[0] < min(want, len(usn)):
                        usn[upos_g[0]]()
                        upos_g[0] += 1
                if qb == 0:
                    upos_g = [0]
                attention(lambda c: QF[hp][:, c, q0:q0 + 512], ktl, par, hook)
                tail_evac(par)
                pend.append(lambda h=h, qb=qb, wuv=wuv_cur: tail_pe(h, qb, wuv))
            while usn and upos_g[0] < len(usn):
                usn[upos_g[0]]()
                upos_g[0] += 1
            wuv_cur = nxt.get("wuv")
        while pend:
            pend.pop(0)()
        if l == 0 and STAGE is None:
            while modq:
                mod_stream_one()
            psall = bank(6, 320, 64).rearrange("p (c s) -> p c s", s=2)
            mod_finish(0, 16, 48, psall[:, 0:64, :])
            mod_finish(1, 0, 48, psall[:, 64:160, :])
        if NOS:
            raise _Stop()
        for g in range(4):
            dma("pool", KT[:, 0:4, 1024 * g:1024 * g + 1024],
                d_cacheT[l, 0:512, 1024 * g:1024 * g + 1024].rearrange("(c p) k -> p c k", p=128))
            dma("pool", KT[0:64, 4, 1024 * g:1024 * g + 1024], d_cacheT[l, 512:576, 1024 * g:1024 * g + 1024])
            dma("pool", VV[:, 8 * g:8 * g + 8, :],
                d_cacheV[l, 1024 * g:1024 * g + 1024, :].rearrange("(t p) d -> p t d", p=128))

        def kf_cache(kt):
            return lambda c: (KT[:, c, 128 * kt:128 * kt + 128] if c < 4 else KT[0:64, 4, 128 * kt:128 * kt + 128])
        ktl = [(kf_cache(kt), VV[:, kt, :], 128) for kt in range(32)]
        ktl.append(((lambda c: (KTSN[:, c, :] if c < 4 else KTSN[0:64, 4, :])), VSN[0:32, :], 32))
        par = it % 2
        attention(lambda c: (QS[:, c, :] if c < 4 else QS[0:64, 4, :]), ktl, par, lambda kt: None)
        tail_evac(par)
        tail_tr()
        for j in range(4):
            wb = wget(f"UV{l}").rearrange("p (h k m) -> p h k m", h=4, k=4)
            for hl in range(4):
                h = 4 * j + hl
                wbk = bank(7, 32, 256)
                for rc in range(4):
                    mm(wbk, wb[:, hl, rc, :], ONT[:, rc, 128 * j + 32 * hl:128 * j + 32 * hl + 32], rc == 0, rc == 3)
                cpa(OT[:, h, TP:T], wbk)

        if l == 0:
            dump("ot", OT.rearrange("p a b -> p (a b)"), BF16, 16 * T)
        ck(5)
        wstate["depth"] = NB
        SAT2 = view(33792, BF16, [128, 8, T])
        MT2 = view(50688, BF16, [128, 8, T])
        MERGED = view(67584, BF16, [128, 16, T])
        dma("sp", hT.rearrange("p a b -> p (a b)"), d_hsp[:, :])
        dma("sp", SAT2.rearrange("p a b -> p (a b)"), d_sasp[:, :])
        dma("sp", MT2.rearrange("p a b -> p (a b)"), d_msp[:, :])
        SG = [view(101376, F32, [128, NT]), view(102784, F32, [128, NT])]
        ACC = [view(141312 + 1408 * i, F32, [128, NT]) for i in range(3)]
        T2 = [view(145536, F32, [128, NT]), view(146944, F32, [128, NT])]
        pb_set([0, 1, 2, 3, 4, 5, 6, 7])
        psc = pfv(l, "psc", 16)
        for f in range(16):
            g4 = f // 4
            for pair in range(3):
                cw = wget(f"M4{l}").rearrange("p (k m) -> p k m", m=128)
                c1 = c2 = c3 = cw
                for i, (lo, hi) in enumerate(tiles3):
                    bo, bg = pb(), pb()
                    if pair == 0:
                        for k in range(8):
                            mm(bank(bo, NT), c1[:, k, :], SAT2[:, k, lo:hi], k == 0, k == 7)
                        gwt, gof = c1, 8
                    elif pair == 1:
                        for k in range(2):
                            mm(bank(bo, NT), c2[:, k, :], MT2[:, 2 * g4 + k, lo:hi], k == 0, k == 1)
                        gwt, gof = c2, 2
                    else:
                        for k in range(16):
                            mm(bank(bo, NT), c3[:, k, :], OT[:, k, lo:hi], k == 0, k == 15)
                        gwt, gof = c3, 16
                    for k in range(16):
                        mm(bank(bg, NT), gwt[:, gof + k, :], hT[:, k, lo:hi], k == 0, k == 15)
                    sg = SG[(pair * 3 + i) % 2]
                    act(sg, bank(bg, NT), AF.Sigmoid)
                    if pair == 0:
                        tt("dve", ACC[i], bank(bo, NT), sg, ALU.mult)
                    elif pair == 1:
                        t2 = T2[i % 2]
                        stt("dve", t2, bank(bo, NT), psc[:, f:f + 1], sg, ALU.mult, ALU.mult)
                        tt("dve", ACC[i], ACC[i], t2, ALU.add)
                    else:
                        t2 = T2[i % 2]
                        tt("dve", t2, bank(bo, NT), sg, ALU.mult)
                        tt("dve", MERGED[:, f, lo:hi], ACC[i], t2, ALU.add)

        if l == 0:
            dump("merged", MERGED.rearrange("p a b -> p (a b)"), BF16, 16 * T)
        ck(6)
        YF = [view(101376, F32, [128, T]), view(105600, F32, [128, T])]
        pb_set([0, 1, 2, 3, 4])
        for n in range(8):
            wb = w3(wget(f"OUT{l}"), 16, 256)
            for m in range(2):
                f = 2 * n + m
                yf = YF[f % 2]
                for i, (lo, hi) in enumerate(tiles3):
                    b = pb()
                    for k in range(16):
                        mm(bank(b, NT), wb[:, k, 128 * m:128 * m + 128], MERGED[:, k, lo:hi], k == 0, k == 15)
                    yproj_evac(b, f, lo, hi, i, yf, f == 0, f == 15)
                dma("sp", ys[f][:, :], yf)
        residual_pass(l, 0)
        if l == 0:
            dump("xnew", XNEW.rearrange("p a b -> p (a b)"), F32, 16 * T)
        prenorm(l, 1, hT)

        ck(7)
        ACTT = view(33792, BF16, [128, NFF, T])
        UPG = [view(126720, F32, [128, 1060]), view(130960, F32, [128, 1060])]
        UPV = [view(135200, F32, [128, T]), view(139424, F32, [128, T])]
        CV = view(143648, F32, [128, 1060])
        SFFN = view(160320, F32, [128, 2, NFF])
        PG01 = view(160672, F32, [128, 2, NFF])
        PV01 = view(161024, F32, [128, 2, NFF])
        BH3 = view(161376, F32, [128, 2, NFF])
        STL = view(161728, F32, [128, 2, NFF])
        H3 = view(162080, F32, [128, 2, NFF])
        GH3B = view(147888, F32, [128, 4, 88])
        C01 = view(149296, F32, [128, 2, NFF])
        TQ = view(149648, F32, [128, NFF])
        dma("sp", SFFN, d_sffn[l])
        for u in UPG:
            K.rec("dve", lambda e, u=u: e.memset(u[:, 0:2], 0.0), [], [u[:, 0:2]])
        wf = pfv(l, "wdwf", 132).rearrange("p (j k) -> p j k", j=NFF)
        bf_ = pfv(l, "bdwf", NFF)
        pb_set([0, 1, 2, 3, 4, 5, 6, 7])
        for j in range(NFF):
            wb = w3(wget(f"UP{l}"), 16, 256)
            upg, upv = UPG[j % 2], UPV[j % 2]
            cp("dve", upg[:, 1026:1028], SFFN[:, :, j])
            for i, (lo, hi) in enumerate(tiles3):
                bg, bv = pb(), pb()
                for k in range(16):
                    mm(bank(bg, NT), wb[:, k, 0:128], hT[:, k, lo:hi], k == 0, k == 15)
                for k in range(16):
                    mm(bank(bv, NT), wb[:, k, 128:256], hT[:, k, lo:hi], k == 0, k == 15)
                for (a, b_, dlo) in segs(lo, hi, 2):
                    cp("act", upg[:, dlo:dlo + (b_ - a)], bank(bg, NT)[:, a:b_])
                cp("act", upv[:, lo:hi], bank(bv, NT))
            cp("dve", PG01[:, :, j], upg[:, 2:4])
            cp("dve", PV01[:, :, j], upv[:, 0:2])
            cp("dve", BH3[:, :, j], upg[:, 1024:1026])
            cp("dve", STL[:, :, j], upg[:, 1058:1060])
            ts("dve", CV[:, 0:1058], upg[:, 0:1058], wf[:, j, 0:1], bf_[:, j:j + 1], ALU.mult, ALU.add)
            stt("dve", CV[:, 0:1058], upg[:, 1:1059], wf[:, j, 1:2], CV[:, 0:1058], ALU.mult, ALU.add)
            stt("dve", CV[:, 0:1058], upg[:, 2:1060], wf[:, j, 2:3], CV[:, 0:1058], ALU.mult, ALU.add)
            act(CV[:, 0:1058], CV[:, 0:1058], AF.Silu)
            tt("dve", ACTT[:, j, 0:TP], CV[:, 0:TP], upv[:, 0:TP], ALU.mult)
            tt("dve", ACTT[:, j, TP:T], CV[:, 1026:1058], upv[:, TP:T], ALU.mult)
        dma("sp", d_bh3[:, :].rearrange("p (t j) -> p t j", t=2), BH3)
        allgather(d_bh3, d_gh3)
        stf = view(126720, F32, [128, 128])
        for which, src in ((0, BH3), (1, STL)):
            tr(bank(0, 128)[0:88, :], src.rearrange("p t j -> p (t j)"), identf[:])
            cp("dve", stf[0:88, :], bank(0, 128)[0:88, :])
            for t2_ in range(2):
                dma("sp", o_ffn[l, which, t2_, :].rearrange("(j p) -> j p", p=128), stf[44 * t2_:44 * t2_ + 44, :])

        def patch():
            dma("sp", GH3B, d_gh3.ap().rearrange("(r p) f -> p r f", p=128))
            h3f = H3.rearrange("p t j -> p (t j)")
            ts("dve", h3f, GH3B[:, 0, :], hsel[:, 0:1], None, ALU.mult)
            for r in range(1, 4):
                stt("dve", h3f, GH3B[:, r, :], hsel[:, r:r + 1], h3f, ALU.mult, ALU.add)
            w0, w1, w2 = wf[:, :, 0], wf[:, :, 1], wf[:, :, 2]
            h0, h1 = H3[:, 0, :], H3[:, 1, :]
            g0, g1 = PG01[:, 0, :], PG01[:, 1, :]
            c0, c1_ = C01[:, 0, :], C01[:, 1, :]
            tt("dve", c0, w0, h0, ALU.mult)
            tt("dve", TQ, w1, h1, ALU.mult)
            tt("dve", c0, c0, TQ, ALU.add)
            tt("dve", TQ, w2, g0, ALU.mult)
            tt("dve", c0, c0, TQ, ALU.add)
            tt("dve", c0, c0, bf_, ALU.add)
            tt("dve", c1_, w0, h1, ALU.mult)
            tt("dve", TQ, w1, g0, ALU.mult)
            tt("dve", c1_, c1_, TQ, ALU.add)
            tt("dve", TQ, w2, g1, ALU.mult)
            tt("dve", c1_, c1_, TQ, ALU.add)
            tt("dve", c1_, c1_, bf_, ALU.add)
            act(C01, C01, AF.Silu)
            tt("dve", C01, C01, PV01, ALU.mult)
            cp("dve", ACTT[:, :, 0:2].rearrange("p j t -> p t j"), C01)

        ck(8)
        YF = [view(126720, F32, [128, T]), view(130944, F32, [128, T])]
        pb_set([0, 1, 2, 3, 4])
        wstate["depth"] = 2
        for f in range(16):
            hb0 = wget(f"DN{l}").rearrange("p (k m) -> p k m", m=128)
            hb1 = wget(f"DN{l}").rearrange("p (k m) -> p k m", m=128)
            yf = YF[f % 2]
            for i in (1, 2, 0):
                lo, hi = tiles3[i]
                if f == 0 and i == 0:
                    patch()
                b = pb()
                for k in range(NFF):
                    hb = hb0 if k < 22 else hb1
                    mm(bank(b, NT), hb[:, k % 22, :], ACTT[:, k, lo:hi], k == 0, k == NFF - 1)
                yproj_evac(b, f, lo, hi, i, yf, f == 0, f == 15)
            dma("sp", ys[f][:, :], yf)
        wstate["depth"] = NB
        residual_pass(l, 1)
        ck(9)

    def final_out():
        pb_set([0, 1, 2, 3])
        for t9 in range(9):
            rows = 128 if t9 < 8 else 32
            tc0 = t9 * 128
            tok = TOK[t9 % 2]
            for g in range(4):
                b = pb()
                for j in range(4):
                    f = 4 * g + j
                    tr(bank(b, 128, 128 * j)[0:rows, :], XNEW[:, f, tc0:tc0 + rows], identf[:])
                cpa(tok[0:rows, 512 * g:512 * g + 512], bank(b)[0:rows, :])
            dma("sp", o_y[tc0:tc0 + rows, :], tok[0:rows, :])

    try:
        prenorm(0, 0, hT)
        ck(1)
        for l in range(L):
            layer(l)
            if l + 1 < L:
                prenorm(l + 1, 0, hT)
        final_out()
    except _Stop:
        if STAGE == 0:
            final_out()

    K.emit()
    es.close()
    nc._ext_in = set(K.ext_in)
    return nc


def _blk(w, cols):
    K_ = w.shape[0]
    kc = K_ // 128
    out = []
    for c in cols:
        sub = w[:, c]
        out.append(sub.reshape(kc, 128, len(c)).transpose(1, 0, 2).reshape(128, kc * len(c)))
    return np.ascontiguousarray(np.stack(out))


def prep_shared(inp):
    f = np.float32
    sh = {}
    w_in = inp["w_in"]
    ar = np.arange
    sh["wmod"] = np.stack([_blk(inp["w_mod"][l], [ar(256 * n, 256 * n + 256) for n in range(48)]) for l in range(L)])
    sh["wA"] = np.stack([_blk(w_in[l], [np.concatenate([ar(128 * j, 128 * j + 128), ar(1024 + 128 * j, 1024 + 128 * j + 128)])
                                       for j in range(8)]) for l in range(L)])
    sh["wB"] = np.stack([_blk(w_in[l], [ar(OFF_B + 256 * n, OFF_B + 256 * n + 256) for n in range(4)]) for l in range(L)])
    sh["wQ"] = np.stack([_blk(w_in[l], [ar(OFF_Q + 256 * n, OFF_Q + 256 * n + 256) for n in range(2)]) for l in range(L)])
    sh["wKVR"] = np.stack([_blk(w_in[l], [ar(OFF_KV, OFF_G)])[0] for l in range(L)])
    whd = np.zeros((L, 16, 128, 2048), f)
    wuvp = np.zeros((L, 4, 128, 4, 512), f)
    for l in range(L):
        wuq = inp["w_uq"][l].reshape(4, 128, 16, 192)
        wuk = inp["w_uk"][l]
        wuv = inp["w_uv"][l].reshape(4, 128, 16, 128)
        for h in range(16):
            qn = wuq[:, :, h, 0:128].transpose(1, 0, 2).reshape(128, 512)
            qr = wuq[:, :, h, 128:192]
            qs = np.concatenate([qr[..., 32:64], qr[..., 0:32]], axis=-1)
            whd[l, h, :, 0:512] = qn
            whd[l, h, :, 512:768] = qr.transpose(1, 0, 2).reshape(128, 256)
            whd[l, h, :, 768:1024] = qs.transpose(1, 0, 2).reshape(128, 256)
            whd[l, h, :, 1024:1536] = wuk[:, h, :].T
            uv = wuv[:, :, h, :].transpose(1, 0, 2).reshape(128, 512)
            whd[l, h, :, 1536:2048] = uv
            wuvp[l, h // 4, :, h % 4, :] = uv
    sh["wHD"] = whd
    sh["wUV"] = wuvp.reshape(L, 4, 128, 2048)
    wm4 = np.zeros((L, 16, 128, 9472), f)
    for l in range(L):
        pa = inp["w_pa"][l].reshape(8, 128, 16, 128)
        oc = inp["w_oc"][l].reshape(16, 128, 16, 128)
        pool = inp["w_pool"][l].reshape(4, 2, 128, 4, 128)
        wg = w_in[l][:, OFF_G:].reshape(16, 128, 3, 16, 128)
        for fch in range(16):
            o = 0
            blkA = pa[:, :, fch, :].transpose(1, 0, 2).reshape(128, 1024)
            wm4[l, fch, :, 0:1024] = blkA
            wm4[l, fch, :, 1024:3072] = wg[:, :, 0, fch, :].transpose(1, 0, 2).reshape(128, 2048)
            wm4[l, fch, :, 3072:3328] = pool[fch // 4, :, :, fch % 4, :].transpose(1, 0, 2).reshape(128, 256)
            wm4[l, fch, :, 3328:5376] = wg[:, :, 1, fch, :].transpose(1, 0, 2).reshape(128, 2048)
            wm4[l, fch, :, 5376:7424] = oc[:, :, fch, :].transpose(1, 0, 2).reshape(128, 2048)
            wm4[l, fch, :, 7424:9472] = wg[:, :, 2, fch, :].transpose(1, 0, 2).reshape(128, 2048)
    sh["wM4"] = wm4
    sh["wOUT"] = np.stack([_blk(inp["w_out"][l], [ar(256 * n, 256 * n + 256) for n in range(8)]) for l in range(L)])
    sh["wUP"] = np.stack([_blk(inp["w_up"][l], [np.concatenate([ar(128 * j, 128 * j + 128), ar(DFF + 128 * j, DFF + 128 * j + 128)])
                                              for j in range(NFF)]) for l in range(L)])
    wdn = np.zeros((L, 32, 128, 2816), f)
    for l in range(L):
        wd = inp["w_down"][l].reshape(2, 22, 128, 16, 128)
        for fch in range(16):
            for half in range(2):
                wdn[l, 2 * fch + half] = wd[half, :, :, fch, :].transpose(1, 0, 2).reshape(128, 2816)
    sh["wDN"] = wdn
    pfa = np.zeros((128, L * PFL), f)
    for l in range(L):
        def put(name, arr):
            pfa[:, l * PFL + PF[name]:l * PFL + PF[name] + arr.shape[1]] = arr
        fm = lambda v: v.reshape(-1, 128).T
        put("gpm", fm(inp["g_pre_mix"][l])); put("gpo", fm(inp["g_post_mix"][l]))
        put("gpf", fm(inp["g_pre_ffn"][l])); put("gpof", fm(inp["g_post_ffn"][l]))
        put("psc", fm(inp["pool_scale"][l])); put("bdwa", fm(inp["b_dwa"][l]))
        put("lng", fm(inp["ln_a_g"][l])); put("lnb", fm(inp["ln_a_b"][l]))
        put("gq", fm(inp["g_q_lat"][l])); put("bdwf", fm(inp["b_dwf"][l]))
        put("bmod", fm(inp["b_mod"][l]))
        put("wdwa", inp["w_dwa"][l].reshape(31, 8, 128).transpose(2, 1, 0).reshape(128, 248))
        put("wdwf", inp["w_dwf"][l].reshape(3, NFF, 128).transpose(2, 1, 0).reshape(128, 132))
    sh["pf"] = pfa
    sh["gkv"] = np.ascontiguousarray(np.broadcast_to(inp["g_kv_lat"][:, None, :], (L, 128, 512))).astype(f)
    return sh


def prep_core(inp, c):
    f = np.float32
    b, r = c // 4, c % 4
    m = {}
    m["xin"] = np.ascontiguousarray(np.concatenate([inp["x_prompt"][b, 1024 * r:1024 * r + 1024], inp["x_sample"][c]], axis=0))
    cv = np.stack([inp["c_prompt"][b], inp["c_sample"][c]], axis=0)
    m["cT"] = np.ascontiguousarray(cv.reshape(2, 16, 128).transpose(2, 1, 0))
    pos = np.concatenate([1024 * r + np.arange(1024), 4096 + np.arange(32)]).astype(f)
    inv = (np.float32(10000.0) ** (-np.arange(32, dtype=f) / np.float32(32))).astype(f)
    ang = (pos[:, None] * inv[None, :]).astype(f)
    cos, sin = np.cos(ang).astype(f), np.sin(ang).astype(f)
    ropeF = np.zeros((64, 2, T), f)
    ropeF[0:32, 0] = cos.T; ropeF[32:64, 0] = cos.T
    ropeF[0:32, 1] = -sin.T; ropeF[32:64, 1] = sin.T
    m["ropeF"] = ropeF
    ropeT = np.zeros((128, 9, 2, 64), f)
    cc2 = np.concatenate([cos, cos], axis=1)
    ss2 = np.concatenate([-sin, sin], axis=1)
    pad = np.zeros((9 * 128, 64), f)
    pad[:T] = cc2
    ropeT[:, :, 0, :] = pad.reshape(9, 128, 64).transpose(1, 0, 2)
    pad = np.zeros((9 * 128, 64), f)
    pad[:T] = ss2
    ropeT[:, :, 1, :] = pad.reshape(9, 128, 64).transpose(1, 0, 2)
    m["ropeT"] = ropeT
    qch = (1024 * r + np.arange(1024)) // 64
    cidx = np.arange(64)[:, None]
    m["qB"] = np.where(cidx > qch[None, :], -30000.0, 0.0).astype(f)
    m["khot"] = (cidx == qch[None, :]).astype(f)
    hs = np.zeros((128, 4), f)
    if r > 0:
        hs[:, r - 1] = 1.0
    m["hsel"] = hs
    ic = np.zeros((128, 4, 16), f)
    for g, w in enumerate((2, 4, 8, 16)):
        ic[:, g, :] = 1.0 / np.minimum(w, 1024 * r + np.arange(16) + 1).astype(f)
    m["icnt"] = ic
    m["sconv"] = np.ascontiguousarray(inp["state_conv"][:, c].reshape(L, 30, 8, 128).transpose(0, 3, 2, 1))
    m["spool"] = np.ascontiguousarray(inp["state_pool"][:, c].reshape(L, 15, 8, 128).transpose(0, 3, 2, 1))
    m["sffn"] = np.ascontiguousarray(inp["state_ffn"][:, c].reshape(L, 2, NFF, 128).transpose(0, 3, 1, 2))
    m["cacheT"] = np.ascontiguousarray(np.concatenate([inp["cache_ckv"][:, c].transpose(0, 2, 1),
                                                       inp["cache_krope"][:, c].transpose(0, 2, 1)], axis=1))
    m["cacheV"] = np.ascontiguousarray(inp["cache_ckv"][:, c])
    return m


WNAMES = ("wmod", "wA", "wB", "wQ", "wKVR", "wHD", "wUV", "wM4", "wOUT", "wUP", "wDN")


def shard_shared(sh, c):
    m = {"pf": sh["pf"], "gkv": sh["gkv"]}
    for name in WNAMES:
        a = sh[name]
        E = a.shape[-1]
        a2 = a.reshape(L, -1, E)
        for l in range(L):
            m[f"{name}{l}"] = a2[l]
    return m


_NC = None


def kernel(**inputs):
    global _NC
    inp = {k: np.asarray(v, dtype=np.float32) for k, v in inputs.items()}
    if _NC is None:
        _NC = build_program()
    nc = _NC
    sh = prep_shared(inp)
    in_maps = []
    for c in range(8):
        m = prep_core(inp, c)
        m.update(shard_shared(sh, c))
        m = {k: v for k, v in m.items() if k in nc._ext_in}
        in_maps.append(m)
    res = run_bass_kernel_spmd(nc, in_maps, core_ids=list(range(8)))
    R = res.results
    f = np.float32
    y_p = np.zeros((2, 4096, D), f); y_s = np.zeros((8, 32, D), f)
    p_ckv = np.zeros((L, 2, 4096, 512), f); p_kr = np.zeros((L, 2, 4096, 64), f)
    p_conv = np.zeros((L, 2, 30, 1024), f); p_pool = np.zeros((L, 2, 15, 1024), f); p_ffn = np.zeros((L, 2, 2, DFF), f)
    s_ckv = np.zeros((L, 8, 32, 512), f); s_kr = np.zeros((L, 8, 32, 64), f)
    s_conv = np.zeros((L, 8, 30, 1024), f); s_pool = np.zeros((L, 8, 15, 1024), f); s_ffn = np.zeros((L, 8, 2, DFF), f)
    for c in range(8):
        b, r = c // 4, c % 4
        o = R[c]
        y_p[b, 1024 * r:1024 * r + 1024] = o["o_y"][:1024]
        y_s[c] = o["o_y"][1024:]
        p_ckv[:, b, 1024 * r:1024 * r + 1024] = o["o_ckv"][:, :1024]
        s_ckv[:, c] = o["o_ckv"][:, 1024:]
        p_kr[:, b, 1024 * r:1024 * r + 1024] = o["o_kr"][:, :1024]
        s_kr[:, c] = o["o_kr"][:, 1024:]
        if r == 3:
            p_conv[:, b] = o["o_conv"][:, 0]
            p_pool[:, b] = o["o_pool"][:, 0]
            p_ffn[:, b] = o["o_ffn"][:, 0]
        s_conv[:, c] = o["o_conv"][:, 1]
        s_pool[:, c] = o["o_pool"][:, 1]
        s_ffn[:, c] = o["o_ffn"][:, 1]
    return (y_p, y_s, p_ckv, p_kr, p_conv, p_pool, p_ffn, s_ckv, s_kr, s_conv, s_pool, s_ffn)
```
